# Optimizing a Trainium2 kernel written in Bass

```python
import math
import jax
import jax.numpy as jnp
from jax import lax
import numpy as np

D_MODEL = 2048
BATCH = 2
SEQ = 8192
DEPTH = 2
DEC_BATCH = 16
DEC_SEQ = 16
PAST_LEN = 4096

CHUNK = 64
N_META = 16
MIX_WIDTH = D_MODEL
LRU_WIDTH = MIX_WIDTH // 2
LRU_BLOCKS = 16
LRU_BLOCK = LRU_WIDTH // LRU_BLOCKS
LRU_C = 8.0
CONV_W = 4
GDN_HEADS = 8
GDN_DK = 128
GDN_DV = (MIX_WIDTH - LRU_WIDTH) // GDN_HEADS
GDN_QK = GDN_HEADS * GDN_DK
GDN_VW = GDN_HEADS * GDN_DV
GDN_QKV = 2 * GDN_QK + GDN_VW
N_IN = 2 * LRU_WIDTH + GDN_QKV + GDN_VW + 2 * GDN_HEADS
D_FF = 5632
EPS = 1e-6

kernel_name = 'hymba_rglru_gdn_macaron_stream_step'


def rms_norm(x, w):
    xf = x.astype(jnp.float32)
    y = xf * lax.rsqrt(jnp.mean(xf * xf, axis=-1, keepdims=True) + EPS)
    return (y * w.astype(jnp.float32)).astype(x.dtype)


def l2_normalize(x):
    return x * lax.rsqrt(jnp.sum(x * x, axis=-1, keepdims=True) + EPS)


def swiglu(h, w_gate, w_up, w_down):
    return (jax.nn.silu(h @ w_gate) * (h @ w_up)) @ w_down


def causal_conv(x, hist, w, b):
    t = x.shape[1]
    xp = jnp.concatenate([hist.astype(x.dtype), x], axis=1)
    y = xp[:, 0:t] * w[0]
    for i in range(1, CONV_W):
        y = y + xp[:, i:i + t] * w[i]
    if b is not None:
        y = y + b
    return y, xp[:, xp.shape[1] - (CONV_W - 1):]


def _lin_combine(earlier, later):
    a1, b1 = earlier
    a2, b2 = later
    return a1 * a2, a2 * b1 + b2


def rg_lru(x, rg_w, rg_b, ig_w, ig_b, lam, h0):
    bsz, t, _ = x.shape
    xf = x.astype(jnp.float32)
    xb = xf.reshape(bsz, t, LRU_BLOCKS, LRU_BLOCK)
    r = jax.nn.sigmoid(jnp.einsum('btni,nij->btnj', xb, rg_w.astype(jnp.float32)).reshape(bsz, t, LRU_WIDTH) + rg_b.astype(jnp.float32))
    ig = jax.nn.sigmoid(jnp.einsum('btni,nij->btnj', xb, ig_w.astype(jnp.float32)).reshape(bsz, t, LRU_WIDTH) + ig_b.astype(jnp.float32))
    log_a = -LRU_C * r * jax.nn.softplus(-lam.astype(jnp.float32))
    a = jnp.exp(log_a)
    b = jnp.sqrt(-jnp.expm1(2.0 * log_a)) * ig * xf
    a_cum, b_cum = lax.associative_scan(_lin_combine, (a, b), axis=1)
    h = a_cum * h0.astype(jnp.float32)[:, None, :] + b_cum
    return h, h[:, -1]


def _to_blocks(a, pad, n):
    a = jnp.pad(a, [(0, 0), (pad, 0)] + [(0, 0)] * (a.ndim - 2))
    a = a.reshape((a.shape[0], n, CHUNK) + a.shape[2:])
    return jnp.moveaxis(a, (1, 3), (0, 2))


def gated_delta_rule(q, k, v, g, beta, s0):
    bsz, t, h, _ = q.shape
    dv = v.shape[-1]
    pad = (-t) % CHUNK
    n = (t + pad) // CHUNK
    qb, kb, vb = _to_blocks(q, pad, n), _to_blocks(k, pad, n), _to_blocks(v, pad, n)
    gb, bb = _to_blocks(g, pad, n), _to_blocks(beta, pad, n)
    gc = jnp.cumsum(gb, axis=-1)
    idx = jnp.arange(CHUNK)
    incl = idx[:, None] >= idx[None, :]
    strict = idx[:, None] > idx[None, :]
    diff = gc[..., :, None] - gc[..., None, :]
    decay = jnp.where(incl, jnp.exp(jnp.where(incl, diff, 0.0)), 0.0)
    kk = jnp.einsum('nbhid,nbhjd->nbhij', kb, kb)
    lmat = jnp.where(strict, bb[..., :, None] * decay * kk, 0.0) + jnp.eye(CHUNK, dtype=jnp.float32)
    rhs = jnp.concatenate([bb[..., None] * vb, (bb * jnp.exp(gc))[..., None] * kb], axis=-1)
    sol = lax.linalg.triangular_solve(lmat, rhs, left_side=True, lower=True, unit_diagonal=True)
    u, wk = sol[..., :dv], sol[..., dv:]
    qk = jnp.einsum('nbhid,nbhjd->nbhij', qb, kb) * decay
    q_dec = qb * jnp.exp(gc)[..., None]
    k_end = kb * jnp.exp(gc[..., -1:] - gc)[..., None]
    g_end = jnp.exp(gc[..., -1])

    def step(s, xs):
        u_c, wk_c, qk_c, qd_c, ke_c, ge_c = xs
        w = u_c - jnp.einsum('bhik,bhkv->bhiv', wk_c, s)
        o = jnp.einsum('bhik,bhkv->bhiv', qd_c, s) + jnp.einsum('bhij,bhjv->bhiv', qk_c, w)
        s = ge_c[..., None, None] * s + jnp.einsum('bhjk,bhjv->bhkv', ke_c, w)
        return s, o

    s_last, o = lax.scan(step, s0.astype(jnp.float32), (u, wk, qk, q_dec, k_end, g_end))
    o = jnp.moveaxis(o, (0, 2), (1, 3)).reshape(bsz, n * CHUNK, h, dv)[:, pad:]
    return o, s_last


def mixer(h, conv_a_hist, lru_h0, conv_b_hist, s0, p, l):
    bsz, t, _ = h.shape
    proj = h @ p['w_in'][l]
    o1 = LRU_WIDTH
    o2 = 2 * LRU_WIDTH
    o3 = o2 + GDN_QKV
    o4 = o3 + GDN_VW
    o5 = o4 + GDN_HEADS
    xa, ga, qkv = proj[..., :o1], proj[..., o1:o2], proj[..., o2:o3]
    z, b_in, al_in = proj[..., o3:o4], proj[..., o4:o5], proj[..., o5:]
    xa_c, new_conv_a = causal_conv(xa, conv_a_hist, p['conv_a_w'][l], p['conv_a_b'][l])
    ha, lru_last = rg_lru(xa_c, p['rg_w'][l], p['rg_b'][l], p['ig_w'][l], p['ig_b'][l], p['lru_lambda'][l], lru_h0)
    ya = rms_norm(ha, p['norm_a'][l]) * jax.nn.gelu(ga.astype(jnp.float32))
    qkv_c, new_conv_b = causal_conv(qkv, conv_b_hist, p['conv_b_w'][l], None)
    qkv_c = jax.nn.silu(qkv_c.astype(jnp.float32))
    q = qkv_c[..., :GDN_QK].reshape(bsz, t, GDN_HEADS, GDN_DK)
    k = qkv_c[..., GDN_QK:2 * GDN_QK].reshape(bsz, t, GDN_HEADS, GDN_DK)
    v = qkv_c[..., 2 * GDN_QK:].reshape(bsz, t, GDN_HEADS, GDN_DV)
    q = l2_normalize(q) * (GDN_DK ** -0.5)
    k = l2_normalize(k)
    beta = jax.nn.sigmoid(b_in.astype(jnp.float32))
    g = -jnp.exp(p['a_log'][l].astype(jnp.float32)) * jax.nn.softplus(al_in.astype(jnp.float32) + p['dt_bias'][l].astype(jnp.float32))
    o, s_last = gated_delta_rule(q, k, v, g, beta, s0)
    yb = rms_norm(o, p['norm_b'][l]) * jax.nn.silu(z.astype(jnp.float32).reshape(bsz, t, GDN_HEADS, GDN_DV))
    y = jnp.concatenate([ya, yb.reshape(bsz, t, GDN_VW)], axis=-1).astype(h.dtype) @ p['w_out'][l]
    return y, new_conv_a, lru_last, new_conv_b, s_last


def trunk(x, conv_a_st, lru_st, conv_b_st, delta_st, p):
    ca, lr, cb, dl = [], [], [], []
    for l in range(DEPTH):
        x = x + 0.5 * swiglu(rms_norm(x, p['ffn1_norm'][l]), p['ffn1_w_gate'][l], p['ffn1_w_up'][l], p['ffn1_w_down'][l])
        y, nca, nlr, ncb, ndl = mixer(rms_norm(x, p['mix_norm'][l]), conv_a_st[l], lru_st[l], conv_b_st[l], delta_st[l], p, l)
        x = x + y
        x = x + 0.5 * swiglu(rms_norm(x, p['ffn2_norm'][l]), p['ffn2_w_gate'][l], p['ffn2_w_up'][l], p['ffn2_w_down'][l])
        ca.append(nca)
        lr.append(nlr)
        cb.append(ncb)
        dl.append(ndl)
    y = rms_norm(x, p['final_norm'])
    return y, jnp.stack(ca, 0), jnp.stack(lr, 0), jnp.stack(cb, 0), jnp.stack(dl, 0)


def setup_inputs(seed: int = 0) -> dict:
    key = jax.random.key(seed)
    ks = iter(jax.random.split(key, 48))
    f32 = jnp.float32

    def nrm(shape, scale):
        return jax.random.normal(next(ks), shape, f32) * scale

    def gain(shape):
        return 1.0 + nrm(shape, 0.02)

    x_prompt = nrm((BATCH, SEQ, D_MODEL), 1.0)
    x_sample = nrm((DEC_BATCH, DEC_SEQ, D_MODEL), 1.0)
    state_conv_a = nrm((DEPTH, DEC_BATCH, CONV_W - 1, LRU_WIDTH), 1.0)
    state_lru = nrm((DEPTH, DEC_BATCH, LRU_WIDTH), 0.5)
    state_conv_b = nrm((DEPTH, DEC_BATCH, CONV_W - 1, GDN_QKV), 1.0)
    state_delta = nrm((DEPTH, DEC_BATCH, GDN_HEADS, GDN_DK, GDN_DV), 0.05)
    meta_tokens = nrm((N_META, D_MODEL), 1.0)
    ffn1_norm = gain((DEPTH, D_MODEL))
    ffn1_w_gate = nrm((DEPTH, D_MODEL, D_FF), D_MODEL ** -0.5)
    ffn1_w_up = nrm((DEPTH, D_MODEL, D_FF), D_MODEL ** -0.5)
    ffn1_w_down = nrm((DEPTH, D_FF, D_MODEL), D_FF ** -0.5)
    mix_norm = gain((DEPTH, D_MODEL))
    w_in = nrm((DEPTH, D_MODEL, N_IN), D_MODEL ** -0.5)
    conv_a_w = nrm((DEPTH, CONV_W, LRU_WIDTH), CONV_W ** -0.5)
    conv_a_b = nrm((DEPTH, LRU_WIDTH), 0.01)
    rg_w = nrm((DEPTH, LRU_BLOCKS, LRU_BLOCK, LRU_BLOCK), LRU_BLOCK ** -0.5)
    rg_b = nrm((DEPTH, LRU_WIDTH), 0.01)
    ig_w = nrm((DEPTH, LRU_BLOCKS, LRU_BLOCK, LRU_BLOCK), LRU_BLOCK ** -0.5)
    ig_b = nrm((DEPTH, LRU_WIDTH), 0.01)
    a_pow = jax.random.uniform(next(ks), (DEPTH, LRU_WIDTH), f32, 0.9, 0.999) ** (1.0 / LRU_C)
    lru_lambda = jnp.log(a_pow) - jnp.log1p(-a_pow)
    norm_a = gain((DEPTH, LRU_WIDTH))
    conv_b_w = nrm((DEPTH, CONV_W, GDN_QKV), CONV_W ** -0.5)
    a_log = jnp.log(jax.random.uniform(next(ks), (DEPTH, GDN_HEADS), f32, 1.0, 16.0))
    dt = jnp.exp(jax.random.uniform(next(ks), (DEPTH, GDN_HEADS), f32, math.log(0.001), math.log(0.1)))
    dt_bias = dt + jnp.log(-jnp.expm1(-dt))
    norm_b = gain((DEPTH, GDN_DV))
    w_out = nrm((DEPTH, MIX_WIDTH, D_MODEL), MIX_WIDTH ** -0.5)
    ffn2_norm = gain((DEPTH, D_MODEL))
    ffn2_w_gate = nrm((DEPTH, D_MODEL, D_FF), D_MODEL ** -0.5)
    ffn2_w_up = nrm((DEPTH, D_MODEL, D_FF), D_MODEL ** -0.5)
    ffn2_w_down = nrm((DEPTH, D_FF, D_MODEL), D_FF ** -0.5)
    final_norm = gain((D_MODEL,))
    return {'x_prompt': x_prompt, 'x_sample': x_sample,
            'state_conv_a': state_conv_a, 'state_lru': state_lru,
            'state_conv_b': state_conv_b, 'state_delta': state_delta,
            'meta_tokens': meta_tokens,
            'ffn1_norm': ffn1_norm, 'ffn1_w_gate': ffn1_w_gate, 'ffn1_w_up': ffn1_w_up, 'ffn1_w_down': ffn1_w_down,
            'mix_norm': mix_norm, 'w_in': w_in,
            'conv_a_w': conv_a_w, 'conv_a_b': conv_a_b, 'rg_w': rg_w, 'rg_b': rg_b, 'ig_w': ig_w, 'ig_b': ig_b,
            'lru_lambda': lru_lambda, 'norm_a': norm_a,
            'conv_b_w': conv_b_w, 'a_log': a_log, 'dt_bias': dt_bias, 'norm_b': norm_b,
            'w_out': w_out,
            'ffn2_norm': ffn2_norm, 'ffn2_w_gate': ffn2_w_gate, 'ffn2_w_up': ffn2_w_up, 'ffn2_w_down': ffn2_w_down,
            'final_norm': final_norm}


def reference(x_prompt, x_sample, state_conv_a, state_lru, state_conv_b, state_delta, meta_tokens,
              ffn1_norm, ffn1_w_gate, ffn1_w_up, ffn1_w_down, mix_norm, w_in,
              conv_a_w, conv_a_b, rg_w, rg_b, ig_w, ig_b, lru_lambda, norm_a,
              conv_b_w, a_log, dt_bias, norm_b, w_out,
              ffn2_norm, ffn2_w_gate, ffn2_w_up, ffn2_w_down, final_norm):
    p = {'ffn1_norm': ffn1_norm, 'ffn1_w_gate': ffn1_w_gate, 'ffn1_w_up': ffn1_w_up, 'ffn1_w_down': ffn1_w_down,
         'mix_norm': mix_norm, 'w_in': w_in,
         'conv_a_w': conv_a_w, 'conv_a_b': conv_a_b, 'rg_w': rg_w, 'rg_b': rg_b, 'ig_w': ig_w, 'ig_b': ig_b,
         'lru_lambda': lru_lambda, 'norm_a': norm_a,
         'conv_b_w': conv_b_w, 'a_log': a_log, 'dt_bias': dt_bias, 'norm_b': norm_b, 'w_out': w_out,
         'ffn2_norm': ffn2_norm, 'ffn2_w_gate': ffn2_w_gate, 'ffn2_w_up': ffn2_w_up, 'ffn2_w_down': ffn2_w_down,
         'final_norm': final_norm}
    bp = x_prompt.shape[0]
    meta = jnp.broadcast_to(meta_tokens.astype(x_prompt.dtype)[None], (bp, N_META, D_MODEL))
    xp = jnp.concatenate([meta, x_prompt], axis=1)
    z_ca = jnp.zeros((DEPTH, bp, CONV_W - 1, LRU_WIDTH), x_prompt.dtype)
    z_lru = jnp.zeros((DEPTH, bp, LRU_WIDTH), jnp.float32)
    z_cb = jnp.zeros((DEPTH, bp, CONV_W - 1, GDN_QKV), x_prompt.dtype)
    z_dl = jnp.zeros((DEPTH, bp, GDN_HEADS, GDN_DK, GDN_DV), jnp.float32)
    yp, p_ca, p_lru, p_cb, p_dl = trunk(xp, z_ca, z_lru, z_cb, z_dl, p)
    ys, s_ca, s_lru, s_cb, s_dl = trunk(x_sample, state_conv_a, state_lru, state_conv_b, state_delta, p)
    return (yp[:, N_META:], ys, p_ca, p_lru, p_cb, p_dl, s_ca, s_lru, s_cb, s_dl)
```

```python
import contextlib
import numpy as np
import concourse.bass as bass
import concourse.mybir as mybir
from concourse.bass_utils import run_bass_kernel_spmd
from concourse.ap import AP as APc

F32 = mybir.dt.float32
BF16 = mybir.dt.bfloat16
ALU = mybir.AluOpType
AF = mybir.ActivationFunctionType

ENGS = ("pe", "dve", "act", "pool", "sp")
EPS = 1e-6


def _flat(xs):
    out = []
    for x in xs:
        if isinstance(x, list):
            out.extend(_flat(x))
        else:
            out.append(x)
    return out


def tm(k):
    return [("ts", k, i) for i in range(4)]


def psr(b):
    return [("bank", b)]


class Prog:
    def __init__(self, nc):
        self.nc = nc
        self.ops = {e: [] for e in ENGS}
        self.cnt = {e: 0 for e in ("pe", "dve", "act", "pool")}
        self.clock = {e: {} for e in ENGS}
        self.vc = {}
        self.last_w = {}
        self.readers = {}
        self.chans = []

    def _need(self, eng, deps):
        waits = {}
        ck = self.clock[eng]
        for (tl, c) in deps:
            if ck.get(tl, 0) >= c:
                continue
            if waits.get(tl, 0) < c:
                waits[tl] = c
        for tl, c in waits.items():
            snap = self.vc.get((tl, c))
            if snap:
                for k, v in snap.items():
                    if ck.get(k, 0) < v:
                        ck[k] = v
            if ck.get(tl, 0) < c:
                ck[tl] = c
        return sorted(waits.items())

    def _deps(self, reads, writes):
        deps = []
        for r in reads:
            lw = self.last_w.get(r)
            if lw:
                deps.append(lw)
        for w in writes:
            lw = self.last_w.get(w)
            if lw:
                deps.append(lw)
            for tl, c in self.readers.get(w, {}).items():
                deps.append((tl, c))
        return deps

    def _commit(self, tl, c, reads, writes):
        for r in reads:
            self.readers.setdefault(r, {})[tl] = c
        for w in writes:
            self.last_w[w] = (tl, c)
            self.readers[w] = {}

    def op(self, eng, fn, reads=(), writes=()):
        reads, writes = _flat(reads), _flat(writes)
        writes = writes + [r for r in reads if isinstance(r, tuple) and r[0] == "bank"]
        deps = self._deps(reads, writes)
        if eng == "pe":
            deps = [d for d in deps if d[0] != "pe"]
        waits = self._need(eng, deps)
        self.cnt[eng] += 1
        c = self.cnt[eng]
        if eng == "pe":
            self.clock[eng][eng] = c
        snap = dict(self.clock[eng])
        snap[eng] = c
        self.vc[(eng, c)] = snap
        self.ops[eng].append((waits, fn, (eng, 1)))
        self._commit(eng, c, reads, writes)

    def dma(self, queue, chan, fn, reads=(), writes=()):
        if chan not in self.cnt:
            self.cnt[chan] = 0
            self.chans.append(chan)
        reads, writes = _flat(reads), _flat(writes)
        deps = self._deps(reads, writes)
        waits = self._need(queue, deps)
        self.cnt[chan] += 16
        c = self.cnt[chan]
        snap = dict(self.clock[queue])
        snap[chan] = c
        self.vc[(chan, c)] = snap
        self.ops[queue].append((waits, fn, (chan, 16)))
        self._commit(chan, c, reads, writes)

    def emit(self, st):
        nc = self.nc
        names = ["pe", "dve", "act", "pool"] + self.chans
        final = [(tl, self.cnt[tl]) for tl in names if self.cnt.get(tl, 0) > 0]
        sems = {}
        for i, n in enumerate(names):
            sems[n] = st.enter_context(nc.semaphore("s%d" % i))
        block = st.enter_context(nc.Block())
        handles = {"pe": block.tensor, "dve": block.vector, "act": block.scalar,
                   "pool": block.gpsimd, "sp": block.sync}

        def make(engname):
            oplist = self.ops[engname]

            def body(e):
                for waits, fn, inc in oplist:
                    for tl, c in waits:
                        e.wait_ge(sems[tl], c)
                    ins = fn(e)
                    ins.then_inc(sems[inc[0]], inc[1])
                if engname == "sp":
                    for tl, c in final:
                        e.wait_ge(sems[tl], c)
            return body

        for engname in ENGS:
            handles[engname](make(engname))


class Cfg:
    def __init__(self, D=2048, DFF=5632, SEQ=8192, BATCH=2, DEC_BATCH=16, DEPTH=2, NCORES=8):
        self.D, self.DFF, self.SEQ, self.BATCH, self.DEC_BATCH, self.DEPTH = D, DFF, SEQ, BATCH, DEC_BATCH, DEPTH
        self.NCORES = NCORES
        self.NMETA = 16
        self.DEC_SEQ = 16
        self.LW = D // 2
        self.H = (D - self.LW) // 128
        self.KD, self.KF, self.KL = D // 128, DFF // 128, self.LW // 128
        self.NQ = 3 * self.H
        self.N_IN = 2 * self.LW + self.NQ * 128 + self.H * 128 + 2 * self.H
        self.T = 512
        self.NTOK = self.NMETA + SEQ
        assert SEQ % self.T == 0 and DEC_BATCH == 2 * NCORES and BATCH <= NCORES
        self.NBIG = SEQ // self.T
        self.UW = 16 * 128
        self.units = []
        for l in range(DEPTH):
            self.units += self._ffn_units(l, 1)
            for c in range(self.KL):
                self.units.append(("xa", l, c, list(range(self.KD))))
            for j in range(self.NQ):
                self.units.append(("qkv", l, j, list(range(self.KD))))
            self.units.append(("tail", l, 0, list(range(self.KD))))
            for c in range(self.KL):
                self.units.append(("ga", l, c, list(range(self.KD))))
            for h in range(self.H):
                self.units.append(("z", l, h, list(range(self.KD))))
            for m in range(self.KD):
                self.units.append(("wout", l, m, list(range(self.KD))))
            self.units += self._ffn_units(l, 2)
        self.NU = len(self.units)
        self.pcol = {}
        n = 0

        def add(name, w):
            nonlocal n
            self.pcol[name] = (n, w)
            n += w
        for l in range(DEPTH):
            add(("ffn1_norm", l), self.KD)
            add(("mix_norm", l), self.KD)
            add(("ffn2_norm", l), self.KD)
            add(("conv_a_w", l), self.KL * 4)
            add(("conv_a_b", l), self.KL)
            add(("rg_b", l), self.KL)
            add(("ig_b", l), self.KL)
            add(("lam", l), self.KL)
            add(("norm_a", l), self.KL)
            add(("conv_b_w", l), self.NQ * 4)
            add(("norm_b", l), 1)
            add(("a_log", l), 1)
            add(("dt_bias", l), 1)
        add(("final_norm",), self.KD)
        self.NP = n

    def _ffn_units(self, l, which):
        us = []
        for f in range(self.KF):
            us.append(("gate", l, which, f, list(range(self.KD))))
            us.append(("up", l, which, f, list(range(self.KD))))
        for m in range(self.KD):
            ks = list(range(self.KF))
            for i in range(0, self.KF, 16):
                us.append(("down", l, which, m, ks[i:i + 16]))
        return us


def build_program(cfg):
    nc = bass.Bass("TRN2", target_bir_lowering=False)
    D, KD, KF, KL, H, NQ, T, DEPTH = cfg.D, cfg.KD, cfg.KF, cfg.KL, cfg.H, cfg.NQ, cfg.T, cfg.DEPTH
    NTOK, NP = cfg.NTOK, cfg.NP

    def din(name, shape):
        return nc.dram_tensor(name, list(shape), F32, kind="ExternalInput").ap()

    def dout(name, shape):
        return nc.dram_tensor(name, list(shape), F32, kind="ExternalOutput").ap()

    xp = din("xp", [KD, 128, NTOK])
    xs = din("xs", [KD, 128, 32])
    sca = din("sca", [DEPTH, 128, 2, KL, 3])
    slru = din("slru", [DEPTH, 128, 2, KL])
    scb = din("scb", [DEPTH, 128, 2, NQ, 3])
    sdl = din("sdl", [DEPTH, 2, 128, H, 128])
    wstream = din("wstream", [cfg.NU, 128, cfg.UW])
    prm_d = din("prm", [128, NP])
    gw_d = din("gw", [DEPTH, 2, 128, KL, 128])
    cst_d = din("cst", [128, 6, 128])
    xsp = nc.dram_tensor("xsp", [KD, 128, T], F32, kind="Internal").ap()
    yp = dout("yp", [KD, 128, NTOK])
    ys = dout("ys", [KD, 128, 32])
    o_ca = dout("o_ca", [DEPTH, 3, 128, KL, 3])
    o_lru = dout("o_lru", [DEPTH, 3, 128, KL])
    o_cb = dout("o_cb", [DEPTH, 3, 128, NQ, 3])
    o_dl = dout("o_dl", [DEPTH, 3, 128, H, 128])

    P = Prog(nc)
    st = contextlib.ExitStack()
    with st:
        def sb(name, shape, dt=F32):
            return st.enter_context(nc.sbuf_tensor(name, list(shape), dt))

        X = sb("X", [128, KD, T])
        HB = sb("HB", [128, KD, T], BF16)
        NAB = max(KF, NQ + KD)
        AB = sb("AB", [128, NAB, T], BF16)
        NSLOT = 4
        WB = [sb("WB%d" % i, [128, 16, 128], BF16) for i in range(NSLOT)]
        HL = sb("HL", [128, KL, T])
        OO = sb("OO", [128, H, T])
        RAW = sb("RAW", [128, 520])
        NTMP = 16
        TMP = [sb("TMP%d" % i, [128, T]) for i in range(NTMP)]
        BT = sb("BT", [128, 3, T], BF16)
        SQF = sb("SQF", [128, T])
        SP_ = [sb("SP%d" % l, [128, H, 128]) for l in range(DEPTH)]
        HAP = [sb("HAP%d" % l, [128, KL, 3]) for l in range(DEPTH)]
        HBP = [sb("HBP%d" % l, [128, NQ, 3]) for l in range(DEPTH)]
        LHP = [sb("LHP%d" % l, [128, KL]) for l in range(DEPTH)]
        HAS = sb("HAS", [128, 2, KL, 3])
        HBS = sb("HBS", [128, 2, NQ, 3])
        LHS = sb("LHS", [128, 2, KL])
        PRM = sb("PRM", [128, NP])
        DER = sb("DER", [128, DEPTH, KL + 1])
        GW = sb("GW", [128, DEPTH * 2 * KL, 128], BF16)
        CST = sb("CST", [128, 6, 128])
        IDB = sb("IDB", [128, 128], BF16)
        ONB = sb("ONB", [128, 128], BF16)

        PS = [st.enter_context(nc.psum_tensor("PS%d" % i, [128, 512], F32)) for i in range(6)]
        PSBs = [st.enter_context(nc.psum_tensor("PSB%d" % i, [128, 1024], BF16)) for i in range(2)]

        IDENT, MSL, MUI, UTRI, ONES = (CST[:, i, :] for i in range(5))

        def pc(name, j=0, rows=128):
            off, w = cfg.pcol[name]
            return PRM[0:rows, off + j:off + j + 1]

        EPSC = sb("EPSC", [128, 1])
        ONEC = sb("ONEC", [128, 1])
        P.op("dve", lambda e: e.memset(EPSC[:], EPS), writes=["EPSC"])
        P.op("dve", lambda e: e.memset(ONEC[:], 1.0), writes=["EPSC"])
        P.dma("sp", "c_prm", lambda e: e.dma_start(out=PRM[:], in_=prm_d[:, :]), writes=["PRM"])
        P.dma("sp", "c_cst", lambda e: e.dma_start(out=CST[:], in_=cst_d[:, :, :]), writes=["CST"])
        P.dma("pool", "c_gw", lambda e: e.dma_start(
            out=GW[:].rearrange("p (a k) j -> p a k j", k=KL),
            in_=gw_d.rearrange("l g p k j -> p (l g) k j")), writes=["GW"])
        P.op("dve", lambda e: e.tensor_copy(out=IDB[:], in_=IDENT), reads=["CST"], writes=["IDB"])
        P.op("dve", lambda e: e.tensor_copy(out=ONB[:], in_=ONES), reads=["CST"], writes=["ONB"])
        for l in range(DEPTH):
            lo, _ = cfg.pcol[("lam", l)]
            P.op("act", lambda e, l=l, lo=lo: e.activation(out=DER[:, l, 0:KL], in_=PRM[:, lo:lo + KL], func=AF.Exp, scale=-1.0),
                 reads=["PRM"], writes=[("DER", l)])
            P.op("act", lambda e, l=l: e.activation(out=DER[:, l, 0:KL], in_=DER[:, l, 0:KL], func=AF.Ln, bias=ONEC[:, 0:1]),
                 reads=[("DER", l), "EPSC"], writes=[("DER", l)])
            P.op("dve", lambda e, l=l: e.tensor_scalar(out=DER[:, l, 0:KL], in0=DER[:, l, 0:KL], scalar1=-8.0, scalar2=None, op0=ALU.mult),
                 reads=[("DER", l)], writes=[("DER", l)])
            ao, _ = cfg.pcol[("a_log", l)]
            P.op("act", lambda e, l=l, ao=ao: e.activation(out=DER[0:H, l, KL:KL + 1], in_=PRM[0:H, ao:ao + 1], func=AF.Exp),
                 reads=["PRM"], writes=[("DERa", l)])
            P.op("dve", lambda e, l=l: e.tensor_scalar(out=DER[0:H, l, KL:KL + 1], in0=DER[0:H, l, KL:KL + 1], scalar1=-1.0, scalar2=None, op0=ALU.mult),
                 reads=[("DERa", l)], writes=[("DERa", l)])
            P.op("dve", lambda e, l=l: e.memset(SP_[l][:], 0.0), writes=[(("S", l, 0), h) for h in range(H)])
            P.op("dve", lambda e, l=l: e.memset(HAP[l][:], 0.0), writes=[("HA", l, 0)])
            P.op("dve", lambda e, l=l: e.memset(HBP[l][:], 0.0), writes=[("HBh", l, 0)])
            P.op("dve", lambda e, l=l: e.memset(LHP[l][:], 0.0), writes=[("LH", l, 0)])

        wstate = {"gu": 0}

        def next_unit(expect_kind):
            gu = wstate["gu"]
            wstate["gu"] += 1
            u = gu % cfg.NU
            desc = cfg.units[u]
            assert desc[0] == expect_kind, (desc, expect_kind)
            nk = len(desc[-1])
            s = gu % NSLOT
            P.dma("pool", "w%d" % s,
                  lambda e, u=u, s=s, nk=nk: e.dma_start(out=WB[s][:, 0:nk, :],
                                                        in_=wstream[u, :, 0:nk * 128].rearrange("p (k j) -> p k j", j=128)),
                  writes=[("WB", s)])
            return WB[s], ("WB", s), desc

        def rmsnorm(nt, wname, dst_bf, dst_res, dst_f32_inplace=False):
            ps = PS[3]
            for kc in range(KD):
                b = kc % 2
                P.op("act", lambda e, kc=kc, b=b: e.activation(out=BT[:, b, 0:nt], in_=X[:, kc, 0:nt], func=AF.Square),
                     reads=[("X", kc)], writes=[("BT", b)])
                P.op("pe", lambda e, kc=kc, b=b: e.matmul(ps[:, 0:nt], lhsT=ONB[:], rhs=BT[:, b, 0:nt], start=(kc == 0), stop=(kc == KD - 1)),
                     reads=[("BT", b), "ONB"], writes=[psr(3)])
            rs = TMP[12]
            P.op("act", lambda e: e.activation(out=rs[:, 0:nt], in_=ps[:, 0:nt], func=AF.Ln, scale=1.0 / D, bias=EPSC[:, 0:1]),
                 reads=[psr(3), "EPSC"], writes=[tm(12)])
            P.op("act", lambda e: e.activation(out=rs[:, 0:nt], in_=rs[:, 0:nt], func=AF.Exp, scale=-0.5), reads=[tm(12)], writes=[tm(12)])
            for kc in range(KD):
                if dst_f32_inplace:
                    P.op("dve", lambda e, kc=kc: e.scalar_tensor_tensor(out=X[:, kc, 0:nt], in0=X[:, kc, 0:nt], scalar=pc(wname, kc),
                                                                      in1=rs[:, 0:nt], op0=ALU.mult, op1=ALU.mult),
                         reads=[("X", kc), tm(12), "PRM"], writes=[("X", kc)])
                else:
                    P.op("dve", lambda e, kc=kc: e.scalar_tensor_tensor(out=dst_bf[:, kc, 0:nt], in0=X[:, kc, 0:nt], scalar=pc(wname, kc),
                                                                      in1=rs[:, 0:nt], op0=ALU.mult, op1=ALU.mult),
                         reads=[("X", kc), tm(12), "PRM"], writes=[(dst_res, kc)])

        def proj_group(ps_ap, ps_res, wt, wres, ks, src, src_res, nt, mcols=slice(0, 128), kmap=None):
            def fn(e):
                ins = None
                n = len(ks)
                for i, k in enumerate(ks):
                    ins = e.matmul(ps_ap, lhsT=wt[:, i, mcols], rhs=src[:, k, 0:nt], start=(i == 0), stop=(i == n - 1))
                return ins
            P.op("pe", fn, reads=[wres] + [(src_res, k) for k in ks], writes=[ps_res])

        def ffn(l, which, nt):
            rmsnorm(nt, ("ffn%d_norm" % which, l), HB, "HB")
            for f in range(KF):
                pg, pu = PS[f % 2], PS[2 + f % 2]
                wt, wres, d = next_unit("gate")
                proj_group(pg[:, 0:nt], psr(f % 2), wt, wres, d[-1], HB, "HB", nt)
                wt, wres, d = next_unit("up")
                proj_group(pu[:, 0:nt], psr(2 + f % 2), wt, wres, d[-1], HB, "HB", nt)
                tb = f % 2
                P.op("act", lambda e, pg=pg, tb=tb: e.activation(out=TMP[tb][:, 0:nt], in_=pg[:, 0:nt], func=AF.Silu),
                     reads=[psr(f % 2)], writes=[tm(tb)])
                P.op("dve", lambda e, pu=pu, tb=tb, f=f: e.tensor_tensor(out=AB[:, f, 0:nt], in0=TMP[tb][:, 0:nt], in1=pu[:, 0:nt], op=ALU.mult),
                     reads=[tm(tb), psr(2 + f % 2)], writes=[("AB", f)])
            for m in range(KD):
                pd = PS[4 + m % 2]
                pres = psr(4 + m % 2)
                nun = (KF + 15) // 16
                parts = [next_unit("down") for _ in range(nun)]

                tot = sum(len(p[2][-1]) for p in parts)
                i0 = 0
                for wt, wres, d in parts:
                    def fn(e, wt=wt, d=d, i0=i0, pd=pd, tot=tot):
                        ins = None
                        for j, k in enumerate(d[-1]):
                            ins = e.matmul(pd[:, 0:nt], lhsT=wt[:, j, :], rhs=AB[:, k, 0:nt], start=(i0 + j == 0), stop=(i0 + j == tot - 1))
                        return ins
                    P.op("pe", fn, reads=[wres] + [("AB", k) for k in d[-1]], writes=[pres])
                    i0 += len(d[-1])
                P.op("dve", lambda e, m=m, pd=pd: e.scalar_tensor_tensor(out=X[:, m, 0:nt], in0=pd[:, 0:nt], scalar=0.5, in1=X[:, m, 0:nt],
                                                                       op0=ALU.mult, op1=ALU.add),
                     reads=[pres, ("X", m)], writes=[("X", m)])

        def hist_a(l, slot):
            return (HAP[l], ("HA", l, 0)) if slot == 0 else (HAS[:, slot - 1], ("HAS", slot))

        def hist_b(l, slot):
            return (HBP[l], ("HBh", l, 0)) if slot == 0 else (HBS[:, slot - 1], ("HBS", slot))

        def lru_h(l, slot):
            return (LHP[l], ("LH", l, 0)) if slot == 0 else (LHS[:, slot - 1], ("LHS", slot))

        def dstate(l, slot):
            return (SP_[l], ("S", l, 0)) if slot == 0 else (HL[:, :, 64 + 128 * (slot - 1):64 + 128 * slot], ("SS", slot))

        def conv_chunk(ps, pres, nt, segs, L, hist_fn, l, ch, wname, bias_name, out_t, out_res):
            nseg = len(segs)
            Le = L + 3
            rawv = RAW[:, 0:nseg * Le].rearrange("p (s l) -> p s l", l=Le)
            P.op("act", lambda e: e.activation(out=rawv[:, :, 3:Le], in_=ps[:, 0:nt].rearrange("p (s l) -> p s l", l=L), func=AF.Copy),
                 reads=[pres], writes=["RAWd"])
            for si, slot in enumerate(segs):
                hb, hres = hist_fn(l, slot)
                P.op("dve", lambda e, si=si, hb=hb: e.tensor_copy(out=rawv[:, si, 0:3], in_=hb[:, ch, :]),
                     reads=[hres], writes=[("RAWh", si)])
                P.op("dve", lambda e, si=si, hb=hb: e.tensor_copy(out=hb[:, ch, :], in_=rawv[:, si, L:Le]),
                     reads=["RAWd", ("RAWh", si)], writes=[hres])
            woff, _ = cfg.pcol[(wname, l)]
            outv = out_t[:, 0:nt].rearrange("p (s l) -> p s l", l=L)
            rd = ["RAWd"] + [("RAWh", si) for si in range(nseg)] + ["PRM"]
            if bias_name is not None:
                P.op("dve", lambda e: e.tensor_scalar(out=outv, in0=rawv[:, :, 0:L], scalar1=PRM[:, woff + ch * 4:woff + ch * 4 + 1],
                                                     scalar2=pc((bias_name, l), ch), op0=ALU.mult, op1=ALU.add),
                     reads=rd, writes=[out_res])
            else:
                P.op("dve", lambda e: e.tensor_scalar(out=outv, in0=rawv[:, :, 0:L], scalar1=PRM[:, woff + ch * 4:woff + ch * 4 + 1],
                                                     scalar2=None, op0=ALU.mult),
                     reads=rd, writes=[out_res])
            for i in range(1, 4):
                P.op("dve", lambda e, i=i: e.scalar_tensor_tensor(out=outv, in0=rawv[:, :, i:i + L],
                                                                scalar=PRM[:, woff + ch * 4 + i:woff + ch * 4 + i + 1],
                                                                in1=outv, op0=ALU.mult, op1=ALU.add),
                     reads=rd + [out_res], writes=[out_res])

        def mixer(l, nt, segs, L, last):
            nseg = len(segs)
            GT = [TMP[0], TMP[1], TMP[12], SQF]
            GTR = [tm(0), tm(1), tm(12), ["SQF"]]
            rmsnorm(nt, ("mix_norm", l), HB, "HB")
            P.dma("sp", "c_xs", lambda e: e.dma_start(out=xsp[:, :, 0:nt].rearrange("k p t -> p k t"), in_=X[:, :, 0:nt]),
                  reads=[("X", k) for k in range(KD)], writes=["xsp"])
            if last:
                P.dma("sp", "c_st0", lambda e: e.dma_start(out=HAS[:], in_=sca[l]), writes=[("HAS", 1), ("HAS", 2)])
                P.dma("sp", "c_st1", lambda e: e.dma_start(out=LHS[:], in_=slru[l]), writes=[("LHS", 1), ("LHS", 2)])
                P.dma("sp", "c_st2", lambda e: e.dma_start(out=HBS[:], in_=scb[l]), writes=[("HBS", 1), ("HBS", 2)])
                for s in range(2):
                    P.dma("sp", "c_st%d" % (3 + s), lambda e, s=s: e.dma_start(out=HL[:, :, 64 + 128 * s:192 + 128 * s], in_=sdl[l, s]),
                          writes=[(("SS", s + 1), h) for h in range(H)] + [("HL", c, 0) for c in range(KL)])
            XCs = [TMP[0], TMP[14]]
            XCr = [tm(0), tm(14)]
            SGs = [TMP[1], TMP[15]]
            SGr = [tm(1), tm(15)]
            Rt, IGt, At, A2t, Bt = TMP[1], TMP[2], TMP[3], TMP[4], TMP[5]
            RSA = TMP[6]
            stages = []

            def xa_proj(c, bk, par):
                wt, wres, d = next_unit("xa")
                proj_group(PS[bk][:, 0:nt], psr(bk), wt, wres, d[-1], HB, "HB", nt)

            def xa_a1(c, bk, par):
                XC, rXC = XCs[par], XCr[par]
                conv_chunk(PS[bk], psr(bk), nt, segs, L, hist_a, l, c, "conv_a_w", "conv_a_b", XC, rXC)
                P.op("act", lambda e: e.activation(out=BT[:, 2, 0:nt], in_=XC[:, 0:nt], func=AF.Copy), reads=[rXC], writes=[("BT", 2)])

            def xa_a2(c, bk, par):
                gi = (l * 2 + 0) * KL + c
                P.op("pe", lambda e: e.matmul(PS[0][:, 0:nt], lhsT=GW[:, gi, :], rhs=BT[:, 2, 0:nt], start=True, stop=True),
                     reads=[("BT", 2), "GW"], writes=[psr(0)])
                gi2 = (l * 2 + 1) * KL + c
                P.op("pe", lambda e: e.matmul(PS[2][:, 0:nt], lhsT=GW[:, gi2, :], rhs=BT[:, 2, 0:nt], start=True, stop=True),
                     reads=[("BT", 2), "GW"], writes=[psr(2)])
                P.op("act", lambda e: e.activation(out=Rt[:, 0:nt], in_=PS[0][:, 0:nt], func=AF.Sigmoid, bias=pc(("rg_b", l), c)),
                     reads=[psr(0), "PRM"], writes=[tm(1)])
                P.op("act", lambda e: e.activation(out=IGt[:, 0:nt], in_=PS[2][:, 0:nt], func=AF.Sigmoid, bias=pc(("ig_b", l), c)),
                     reads=[psr(2), "PRM"], writes=[tm(2)])
                P.op("act", lambda e: e.activation(out=At[:, 0:nt], in_=Rt[:, 0:nt], func=AF.Exp, scale=DER[:, l, c:c + 1]),
                     reads=[tm(1), ("DER", l)], writes=[tm(3)])
                P.op("act", lambda e: e.activation(out=A2t[:, 0:nt], in_=At[:, 0:nt], func=AF.Square), reads=[tm(3)], writes=[tm(4)])
                P.op("act", lambda e: e.activation(out=A2t[:, 0:nt], in_=A2t[:, 0:nt], func=AF.Ln, scale=-1.0, bias=ONEC[:, 0:1]),
                     reads=[tm(4), "EPSC"], writes=[tm(4)])
                P.op("act", lambda e: e.activation(out=A2t[:, 0:nt], in_=A2t[:, 0:nt], func=AF.Exp, scale=0.5), reads=[tm(4)], writes=[tm(4)])

            def xa_b(c, bk, par):
                XC, rXC = XCs[par], XCr[par]
                P.op("dve", lambda e: e.tensor_tensor(out=Bt[:, 0:nt], in0=IGt[:, 0:nt], in1=XC[:, 0:nt], op=ALU.mult),
                     reads=[tm(2), rXC], writes=[tm(5)])
                P.op("dve", lambda e: e.tensor_tensor(out=Bt[:, 0:nt], in0=Bt[:, 0:nt], in1=A2t[:, 0:nt], op=ALU.mult),
                     reads=[tm(5), tm(4)], writes=[tm(5)])
                for si, slot in enumerate(segs):
                    hb, hres = lru_h(l, slot)
                    cs = slice(si * L, (si + 1) * L)
                    P.op("dve", lambda e, cs=cs, hb=hb: e.tensor_tensor_scan(out=HL[:, c, cs], data0=At[:, cs], data1=Bt[:, cs],
                                                                         initial=hb[:, c:c + 1], op0=ALU.mult, op1=ALU.add),
                         reads=[tm(3), tm(5), hres], writes=[("HL", c, si)])
                    P.op("dve", lambda e, hb=hb, si=si: e.tensor_copy(out=hb[:, c:c + 1], in_=HL[:, c, (si + 1) * L - 1:(si + 1) * L]),
                         reads=[("HL", c, si)], writes=[hres])
                if c == KL - 1:
                    lru_stats()

            def lru_stats():
                for c in range(KL):
                    b = c % 2
                    P.op("act", lambda e, c=c, b=b: e.activation(out=BT[:, b, 0:nt], in_=HL[:, c, 0:nt], func=AF.Square),
                         reads=[("HL", c, si) for si in range(nseg)], writes=[("BT", b)])
                    P.op("pe", lambda e, c=c, b=b: e.matmul(PS[3][:, 0:nt], lhsT=ONB[:], rhs=BT[:, b, 0:nt], start=(c == 0), stop=(c == KL - 1)),
                         reads=[("BT", b), "ONB"], writes=[psr(3)])
                P.op("act", lambda e: e.activation(out=RSA[:, 0:nt], in_=PS[3][:, 0:nt], func=AF.Ln, scale=1.0 / cfg.LW, bias=EPSC[:, 0:1]),
                     reads=[psr(3), "EPSC"], writes=[tm(6)])
                P.op("act", lambda e: e.activation(out=RSA[:, 0:nt], in_=RSA[:, 0:nt], func=AF.Exp, scale=-0.5), reads=[tm(6)], writes=[tm(6)])
            for c in range(KL):
                stages.append((xa_proj, xa_a1, xa_a2, xa_b, c))

            def qkv_proj(j, bk, par):
                wt, wres, d = next_unit("qkv")
                proj_group(PS[bk][:, 0:nt], psr(bk), wt, wres, d[-1], HB, "HB", nt)

            def qkv_a1(j, bk, par):
                XC, rXC, SG, rSG = XCs[par], XCr[par], SGs[par], SGr[par]
                conv_chunk(PS[bk], psr(bk), nt, segs, L, hist_b, l, j, "conv_b_w", None, XC, rXC)

            def qkv_a2(j, bk, par):
                XC, rXC, SG, rSG = XCs[par], XCr[par], SGs[par], SGr[par]
                P.op("act", lambda e: e.activation(out=SG[:, 0:nt], in_=XC[:, 0:nt], func=AF.Exp, scale=-1.0), reads=[rXC], writes=[rSG])
                P.op("act", lambda e: e.activation(out=SG[:, 0:nt], in_=SG[:, 0:nt], func=AF.Ln, bias=ONEC[:, 0:1]), reads=[rSG, "EPSC"], writes=[rSG])
                P.op("act", lambda e: e.activation(out=SG[:, 0:nt], in_=SG[:, 0:nt], func=AF.Exp, scale=-1.0), reads=[rSG], writes=[rSG])
                if j < 2 * H:
                    nb = [3, 0][par]
                    P.op("dve", lambda e: e.tensor_tensor(out=XC[:, 0:nt], in0=XC[:, 0:nt], in1=SG[:, 0:nt], op=ALU.mult), reads=[rXC, rSG], writes=[rXC])
                    P.op("act", lambda e: e.activation(out=SQF[:, 0:nt], in_=XC[:, 0:nt], func=AF.Square), reads=[rXC], writes=["SQF"])
                    P.op("pe", lambda e: e.matmul(PS[nb][:, 0:nt], lhsT=ONES, rhs=SQF[:, 0:nt], start=True, stop=True),
                         reads=["SQF", "CST"], writes=[psr(nb)])
                else:
                    P.op("dve", lambda e: e.tensor_tensor(out=AB[:, j, 0:nt], in0=XC[:, 0:nt], in1=SG[:, 0:nt], op=ALU.mult),
                         reads=[rXC, rSG], writes=[("AB", j)])

            def qkv_b(j, bk, par):
                if j >= 2 * H:
                    return
                XC, rXC, SG, rSG = XCs[par], XCr[par], SGs[par], SGr[par]
                nb = [3, 0][par]
                P.op("act", lambda e: e.activation(out=SG[:, 0:nt], in_=PS[nb][:, 0:nt], func=AF.Ln, bias=EPSC[:, 0:1]),
                     reads=[psr(nb), "EPSC"], writes=[rSG])
                P.op("act", lambda e: e.activation(out=SG[:, 0:nt], in_=SG[:, 0:nt], func=AF.Exp, scale=-0.5), reads=[rSG], writes=[rSG])
                sc = (128.0 ** -0.5) if j < H else 1.0
                P.op("dve", lambda e: e.scalar_tensor_tensor(out=AB[:, j, 0:nt], in0=XC[:, 0:nt], scalar=sc, in1=SG[:, 0:nt],
                                                           op0=ALU.mult, op1=ALU.mult),
                     reads=[rXC, rSG], writes=[("AB", j)])
            for j in range(NQ):
                stages.append((qkv_proj, qkv_a1, qkv_a2, qkv_b, j))

            def tail_proj(_, bk, par):
                wt, wres, d = next_unit("tail")
                proj_group(PS[1][0:H, 0:nt], psr(1), wt, wres, d[-1], HB, "HB", nt, mcols=slice(0, H))
                proj_group(PS[2][0:H, 0:nt], psr(2), wt, wres, d[-1], HB, "HB", nt, mcols=slice(H, 2 * H))

            def tail_b(_, bk, par):
                P.op("act", lambda e: e.activation(out=GT[0][0:H, 0:nt], in_=PS[1][0:H, 0:nt], func=AF.Sigmoid), reads=[psr(1)], writes=[GTR[0]])
                P.op("act", lambda e: e.activation(out=GT[1][0:H, 0:nt], in_=PS[2][0:H, 0:nt], func=AF.Exp, bias=pc(("dt_bias", l), 0, H)),
                     reads=[psr(2), "PRM"], writes=[GTR[1]])
                P.op("act", lambda e: e.activation(out=GT[1][0:H, 0:nt], in_=GT[1][0:H, 0:nt], func=AF.Ln, bias=ONEC[0:H, 0:1]),
                     reads=[GTR[1], "EPSC"], writes=[GTR[1]])
                P.op("dve", lambda e: e.tensor_scalar(out=GT[1][0:H, 0:nt], in0=GT[1][0:H, 0:nt], scalar1=DER[0:H, l, KL:KL + 1], scalar2=None, op0=ALU.mult),
                     reads=[GTR[1], ("DERa", l)], writes=[GTR[1]])
            stages.append((tail_proj, None, None, tail_b, 0))

            def run_pipeline(stages):
                n = len(stages)

                def call(i, k):
                    if 0 <= i < n and stages[i][k] is not None:
                        stages[i][k](stages[i][4], 4 + i % 2, i % 2)
                call(0, 0)
                call(1, 0)
                call(0, 1)
                call(0, 2)
                for i in range(n):
                    call(i + 2, 0)
                    call(i + 1, 1)
                    call(i, 3)
                    call(i + 1, 2)
            run_pipeline(stages)

            C = min(L, 128)
            nch = L // C
            nsq = max(1, int(np.ceil(np.log2(C))) - 1)
            HG = max(1, min(4, H // 2))
            assert H // HG == 2 and H % HG == 0 and KD >= 10
            Gsets = [[TMP[2], TMP[3], TMP[4], TMP[5], TMP[8], TMP[9], TMP[10], TMP[11], TMP[13]], [X[:, k, :] for k in range(9)]]
            GRsets = [[tm(2), tm(3), tm(4), tm(5), tm(8), tm(9), tm(10), tm(11), tm(13)], [[("X", k)] for k in range(9)]]
            SMs = [TMP[7], X[:, 9, :]]
            SMr = [tm(7), [("X", 9)]]

            def v3(t):
                return t[:, 0:HG * 128].rearrange("p (h c) -> p h c", c=128)

            def bcl(ap2, n):
                return ap2.unsqueeze(2).to_broadcast([ap2.shape[0], ap2.shape[1], n])

            def bcm(ap2, n):
                a = [list(x) for x in ap2.ap]
                return APc(ap2.tensor, ap2.offset, [a[0], [0, n], a[1]])
            def do_chunk(si, slot, ci, cidx):
                Sb, Sres = dstate(l, slot)
                SM, rSM = SMs[cidx % 2], SMr[cidx % 2]
                pre = []

                def OP(eng, fn, reads=(), writes=()):
                    pre.append((eng, fn, _flat(list(reads)), _flat(list(writes))))
                c0 = si * L + ci * C
                cs = slice(c0, c0 + C)
                OP("dve", lambda e, cs=cs: e.tensor_tensor_scan(out=GT[2][0:H, cs], data0=ONES[0:H, 0:C], data1=GT[1][0:H, cs],
                                                                 initial=0.0, op0=ALU.mult, op1=ALU.add),
                     reads=[GTR[1], "CST"], writes=[GTR[2]])
                OP("act", lambda e, cs=cs: e.activation(out=GT[3][0:H, cs], in_=GT[2][0:H, cs], func=AF.Exp), reads=[GTR[2]], writes=[GTR[3]])
                pss = PS[2]
                b3 = psr(2)

                def trfn(e, cs=cs):
                    e.transpose(out=pss[0:C, 0:H], in_=GT[0][0:H, cs], identity=IDENT[0:H, 0:H])
                    return e.transpose(out=pss[0:C, 8:8 + H], in_=GT[1][0:H, cs], identity=IDENT[0:H, 0:H])
                OP("pe", trfn, reads=[GTR[0], GTR[1], "CST"], writes=[b3])
                OP("dve", lambda e: e.tensor_copy(out=SM[0:C, 0:H], in_=pss[0:C, 0:H]), reads=[b3], writes=[rSM])
                OP("dve", lambda e: e.tensor_copy(out=SM[0:C, H:2 * H], in_=pss[0:C, 8:8 + H]), reads=[b3], writes=[rSM])

                def cumfn(e):
                    e.matmul(pss[0:C, 16:16 + H], lhsT=UTRI[0:C, 0:C], rhs=SM[0:C, H:2 * H], start=True, stop=True)
                    return e.matmul(pss[:, 24:24 + H], lhsT=ONES[0:C, :], rhs=SM[0:C, H:2 * H], start=True, stop=True)
                OP("pe", cumfn, reads=[rSM, "CST"], writes=[b3])
                OP("dve", lambda e: e.tensor_copy(out=SM[0:C, 2 * H:3 * H], in_=pss[0:C, 16:16 + H]), reads=[b3], writes=[rSM])
                OP("act", lambda e: e.activation(out=SM[0:C, 3 * H:4 * H], in_=SM[0:C, 2 * H:3 * H], func=AF.Exp), reads=[rSM], writes=[rSM])
                OP("dve", lambda e: e.tensor_tensor(out=SM[0:C, 4 * H:5 * H], in0=pss[0:C, 24:24 + H], in1=SM[0:C, 2 * H:3 * H], op=ALU.subtract),
                     reads=[b3, rSM], writes=[rSM])
                OP("act", lambda e: e.activation(out=SM[0:C, 4 * H:5 * H], in_=SM[0:C, 4 * H:5 * H], func=AF.Exp), reads=[rSM], writes=[rSM])
                OP("dve", lambda e: e.tensor_tensor(out=SM[0:C, 5 * H:6 * H], in0=SM[0:C, 0:H], in1=SM[0:C, 3 * H:4 * H], op=ALU.mult),
                     reads=[rSM], writes=[rSM])
                OP("act", lambda e: e.activation(out=SM[:, 6 * H:7 * H], in_=pss[:, 24:24 + H], func=AF.Exp), reads=[b3], writes=[rSM])
                def do_group(h0, gset):
                    ops = []

                    def OP(eng, fn, reads=(), writes=()):
                        ops.append((eng, fn, _flat(list(reads)), _flat(list(writes))))
                    G, GR = Gsets[gset], GRsets[gset]
                    hs = list(range(h0, h0 + HG))
                    ia, ib, ic = (0, 1, 2) if gset == 0 else (3, 4, 5)
                    Ba, Bb, Bc_ = v3(PS[ia]), v3(PS[ib]), v3(PS[ic])
                    ra, rb_, rc = psr(ia), psr(ib), psr(ic)
                    PBt = PSBs[gset]
                    PTK = PBt[:, 0:HG * 128].rearrange("p (h c) -> p h c", c=128)
                    PTV = PBt[:, 512:512 + HG * 128].rearrange("p (h c) -> p h c", c=128)
                    r7 = psr(6 + gset)
                    Gv = [v3(g) for g in G]
                    rK = [("AB", H + h) for h in hs]
                    rQ = [("AB", h) for h in hs]
                    rV = [("AB", 2 * H + h) for h in hs]

                    def smc(k):
                        return SM[0:C, k * H + h0:k * H + h0 + HG]

                    def mm_each(fnh):
                        def fn(e):
                            ins = None
                            for gi, h in enumerate(hs):
                                ins = fnh(e, gi, h)
                            return ins
                        return fn
                    W3 = (slice(0, C), slice(0, HG), slice(0, C))
                    WF = (slice(0, C), slice(0, HG), slice(0, 128))
                    WT_ = (slice(0, 128), slice(0, HG), slice(0, C))
                    WS_ = (slice(0, 128), slice(0, HG), slice(0, 128))
                    Dm, DT = Gv[0], Gv[1]
                    OP("pe", mm_each(lambda e, gi, h: e.matmul(Bc_[0:C, gi, 0:C], lhsT=IDENT[0:H, h:h + 1].to_broadcast([H, C]), rhs=GT[2][0:H, cs], start=True, stop=True)),
                       reads=[GTR[2], "CST"], writes=[rc])
                    OP("pe", mm_each(lambda e, gi, h: e.matmul(Ba[0:C, gi, 0:C], lhsT=AB[:, H + h, cs], rhs=AB[:, H + h, cs], start=True, stop=True)),
                       reads=[rK], writes=[ra])
                    OP("pe", mm_each(lambda e, gi, h: e.matmul(Bb[0:C, gi, 0:C], lhsT=AB[:, H + h, cs], rhs=AB[:, h, cs], start=True, stop=True)),
                       reads=[rK, rQ], writes=[rb_])
                    OP("dve", lambda e: e.tensor_tensor(out=Dm[W3], in0=Bc_[W3], in1=bcl(smc(2), C), op=ALU.subtract), reads=[rc, rSM], writes=[GR[0]])
                    OP("dve", lambda e: e.tensor_scalar(out=DT[W3], in0=Dm[W3], scalar1=0.0, scalar2=None, op0=ALU.min), reads=[GR[0]], writes=[GR[1]])
                    OP("dve", lambda e: e.tensor_scalar(out=Dm[W3], in0=Dm[W3], scalar1=0.0, scalar2=None, op0=ALU.max), reads=[GR[0]], writes=[GR[0]])
                    OP("act", lambda e: e.activation(out=Dm[W3], in_=Dm[W3], func=AF.Exp, scale=-1.0), reads=[GR[0]], writes=[GR[0]])
                    OP("act", lambda e: e.activation(out=DT[W3], in_=DT[W3], func=AF.Exp), reads=[GR[1]], writes=[GR[1]])
                    OP("dve", lambda e: e.tensor_tensor(out=Dm[W3], in0=Dm[W3], in1=bcm(MSL[0:C, 0:C], HG), op=ALU.mult), reads=[GR[0], "CST"], writes=[GR[0]])
                    OP("dve", lambda e: e.tensor_tensor(out=DT[W3], in0=DT[W3], in1=bcm(MUI[0:C, 0:C], HG), op=ALU.mult), reads=[GR[1], "CST"], writes=[GR[1]])
                    OP("dve", lambda e: e.tensor_tensor(out=Dm[W3], in0=Dm[W3], in1=bcl(smc(0), C), op=ALU.mult), reads=[GR[0], rSM], writes=[GR[0]])
                    OP("dve", lambda e: e.tensor_tensor(out=DT[W3], in0=Bb[W3], in1=DT[W3], op=ALU.mult), reads=[rb_, GR[1]], writes=[GR[1]])
                    Ac, rAc, An, rAn = Gv[2], GR[2], Gv[3], GR[3]
                    Bc, rBc, Bn, rBn = Gv[4], GR[4], Gv[5], GR[5]
                    Qc, rQc, Qn, rQn = Gv[6], GR[6], Gv[7], GR[7]
                    OP("dve", lambda e, Ac=Ac: e.tensor_tensor(out=Ac[W3], in0=Ba[W3], in1=Dm[W3], op=ALU.mult), reads=[ra, GR[0]], writes=[rAc])
                    OP("pe", mm_each(lambda e, gi, h, Ac=Ac: e.transpose(out=Ba[0:C, gi, 0:C], in_=Ac[0:C, gi, 0:C], identity=IDENT[0:C, 0:C])),
                       reads=[rAc, "CST"], writes=[ra])
                    OP("act", lambda e, Bc=Bc: e.activation(out=Bc[W3], in_=Ba[W3], func=AF.Copy), reads=[ra], writes=[rBc])
                    OP("dve", lambda e, Qc=Qc: e.tensor_tensor(out=Qc[W3], in0=bcm(IDENT[0:C, 0:C], HG), in1=Ba[W3], op=ALU.subtract), reads=[ra, "CST"], writes=[rQc])
                    for jq in range(1, nsq + 1):
                        OP("pe", mm_each(lambda e, gi, h, Bc=Bc, Ac=Ac: e.matmul(Ba[0:C, gi, 0:C], lhsT=Bc[0:C, gi, 0:C], rhs=Ac[0:C, gi, 0:C], start=True, stop=True)),
                           reads=[rBc, rAc], writes=[ra])
                        if jq < nsq:
                            OP("pe", mm_each(lambda e, gi, h, Bc=Bc, Ac=Ac: e.matmul(Bc_[0:C, gi, 0:C], lhsT=Ac[0:C, gi, 0:C], rhs=Bc[0:C, gi, 0:C], start=True, stop=True)),
                               reads=[rBc, rAc], writes=[rc])
                        OP("act", lambda e, An=An: e.activation(out=An[W3], in_=Ba[W3], func=AF.Copy), reads=[ra], writes=[rAn])
                        if jq < nsq:
                            OP("dve", lambda e, Bn=Bn: e.tensor_copy(out=Bn[W3], in_=Bc_[W3]), reads=[rc], writes=[rBn])
                        OP("pe", mm_each(lambda e, gi, h, An=An, Qc=Qc: e.matmul(Bb[0:C, gi, 0:C], lhsT=An[0:C, gi, 0:C], rhs=Qc[0:C, gi, 0:C], start=True, stop=True)),
                           reads=[rAn, rQc], writes=[rb_])
                        OP("dve", lambda e, Qn=Qn, Qc=Qc: e.tensor_tensor(out=Qn[W3], in0=Qc[W3], in1=Bb[W3], op=ALU.add), reads=[rQc, rb_], writes=[rQn])
                        Ac, rAc, An, rAn = An, rAn, Ac, rAc
                        Bc, rBc, Bn, rBn = Bn, rBn, Bc, rBc
                        Qc, rQc, Qn, rQn = Qn, rQn, Qc, rQc
                    RK, rRK, KE, rKE = Gv[2], GR[2], Gv[3], GR[3]
                    Ut, rUt, WK, rWK = Gv[4], GR[4], Gv[5], GR[5]
                    QD, rQD = Qn, rQn
                    VB, rVB = Gv[0], GR[0]
                    Wt, rWt = Gv[8], GR[8]
                    QK, rQK = DT, GR[1]

                    def tkv(e):
                        ins = None
                        for gi, h in enumerate(hs):
                            e.transpose(out=PTK[0:C, gi, :], in_=AB[:, H + h, cs], identity=IDB[:])
                            ins = e.transpose(out=PTV[0:C, gi, :], in_=AB[:, 2 * H + h, cs], identity=IDB[:])
                        return ins
                    OP("pe", tkv, reads=[rK, rV, "IDB"], writes=[r7])
                    OP("dve", lambda e: e.tensor_tensor(out=RK[WF], in0=PTK[WF], in1=bcl(smc(5), 128), op=ALU.mult), reads=[r7, rSM], writes=[rRK])
                    OP("dve", lambda e: e.tensor_tensor(out=KE[WF], in0=PTK[WF], in1=bcl(smc(4), 128), op=ALU.mult), reads=[r7, rSM], writes=[rKE])
                    OP("dve", lambda e: e.tensor_tensor(out=VB[WF], in0=PTV[WF], in1=bcl(smc(0), 128), op=ALU.mult), reads=[r7, rSM], writes=[rVB])
                    OP("pe", mm_each(lambda e, gi, h: e.matmul(Ba[0:C, gi, :], lhsT=Qc[0:C, gi, 0:C], rhs=VB[0:C, gi, :], start=True, stop=True)),
                       reads=[rQc, rVB], writes=[ra])
                    OP("act", lambda e: e.activation(out=Ut[WF], in_=Ba[WF], func=AF.Copy), reads=[ra], writes=[rUt])
                    OP("pe", mm_each(lambda e, gi, h: e.matmul(Bc_[:, gi, 0:C], lhsT=RK[0:C, gi, :], rhs=Qc[0:C, gi, 0:C], start=True, stop=True)),
                       reads=[rRK, rQc], writes=[rc])
                    OP("act", lambda e: e.activation(out=WK[WT_], in_=Bc_[WT_], func=AF.Copy), reads=[rc], writes=[rWK])
                    OP("pe", mm_each(lambda e, gi, h: e.matmul(Bb[:, gi, 0:C], lhsT=IDENT[0:H, h:h + 1].to_broadcast([H, 128]), rhs=GT[3][0:H, cs], start=True, stop=True)),
                       reads=[GTR[3], "CST"], writes=[rb_])
                    OP("dve", lambda e: e.tensor_tensor(out=QD[WT_], in0=AB[:, h0:h0 + HG, cs], in1=Bb[WT_], op=ALU.mult), reads=[rQ, rb_], writes=[rQD])
                    rS = [(Sres, h) for h in hs]
                    Sg = Sb[:, h0:h0 + HG, :]
                    OP("pe", mm_each(lambda e, gi, h: e.matmul(Ba[0:C, gi, :], lhsT=WK[:, gi, 0:C], rhs=Sb[:, h, :], start=True, stop=True)),
                       reads=[rWK, rS], writes=[ra])
                    OP("dve", lambda e: e.tensor_tensor(out=Wt[WF], in0=Ut[WF], in1=Ba[WF], op=ALU.subtract), reads=[rUt, ra], writes=[rWt])

                    def ofn(e):
                        ins = None
                        for gi, h in enumerate(hs):
                            e.matmul(Bc_[:, gi, 0:C], lhsT=Sb[:, h, :], rhs=QD[:, gi, 0:C], start=True, stop=False)
                            ins = e.matmul(Bc_[:, gi, 0:C], lhsT=Wt[0:C, gi, :], rhs=QK[0:C, gi, 0:C], start=False, stop=True)
                        return ins
                    OP("pe", ofn, reads=[rS, rQD, rWt, rQK], writes=[rc])
                    OP("act", lambda e: e.activation(out=OO[:, h0:h0 + HG, cs], in_=Bc_[WT_], func=AF.Copy), reads=[rc],
                       writes=[("OO", h, si, ci) for h in hs])
                    OP("pe", mm_each(lambda e, gi, h: e.matmul(Bb[:, gi, :], lhsT=KE[0:C, gi, :], rhs=Wt[0:C, gi, :], start=True, stop=True)),
                       reads=[rKE, rWt], writes=[rb_])
                    OP("dve", lambda e: e.tensor_tensor(out=Sg, in0=Sg, in1=bcl(SM[:, 6 * H + h0:6 * H + h0 + HG], 128), op=ALU.mult), reads=[rS, rSM], writes=[rS])
                    OP("dve", lambda e: e.tensor_tensor(out=Sg, in0=Sg, in1=Bb[WS_], op=ALU.add), reads=[rS, rb_], writes=[rS])
                    return ops
                return [pre + do_group(0, 0)] + [do_group(h0, gi) for gi, h0 in list(enumerate(range(0, H, HG)))[1:]], len(pre)

            allops = []
            cidx = 0
            for si_, slot_ in enumerate(segs):
                for ci_ in range(nch):
                    lists, npre = do_chunk(si_, slot_, ci_, cidx)
                    period = len(lists[0])
                    for gi_, lst in enumerate(lists):
                        off = 0 if gi_ == 0 else npre + 16
                        for k_, op_ in enumerate(lst):
                            allops.append((cidx * period + off + k_, gi_, op_))
                    cidx += 1
            allops.sort(key=lambda t: (t[0], t[1]))
            for _, _, (eng_, fn_, rd_, wr_) in allops:
                P.op(eng_, fn_, reads=rd_, writes=wr_)
            oo_res = lambda h: [("OO", h, si, ci) for si in range(nseg) for ci in range(nch)]

            for c in range(KL):
                wt, wres, d = next_unit("ga")
                po = PS[4 + c % 2]
                pres = psr(4 + c % 2)
                proj_group(po[:, 0:nt], pres, wt, wres, d[-1], HB, "HB", nt)
                G1, G2 = TMP[0], TMP[1]
                P.op("act", lambda e, po=po: e.activation(out=G1[:, 0:nt], in_=po[:, 0:nt], func=AF.Square), reads=[pres], writes=[tm(0)])
                P.op("dve", lambda e: e.tensor_scalar(out=G1[:, 0:nt], in0=G1[:, 0:nt], scalar1=0.044715, scalar2=1.0, op0=ALU.mult, op1=ALU.add),
                     reads=[tm(0)], writes=[tm(0)])
                P.op("dve", lambda e, po=po: e.tensor_tensor(out=G1[:, 0:nt], in0=G1[:, 0:nt], in1=po[:, 0:nt], op=ALU.mult), reads=[tm(0), pres], writes=[tm(0)])
                P.op("act", lambda e: e.activation(out=G1[:, 0:nt], in_=G1[:, 0:nt], func=AF.Sigmoid, scale=1.5957691216057308), reads=[tm(0)], writes=[tm(0)])
                P.op("dve", lambda e, po=po: e.tensor_tensor(out=G1[:, 0:nt], in0=G1[:, 0:nt], in1=po[:, 0:nt], op=ALU.mult), reads=[tm(0), pres], writes=[tm(0)])
                P.op("dve", lambda e, c=c: e.scalar_tensor_tensor(out=G2[:, 0:nt], in0=HL[:, c, 0:nt], scalar=pc(("norm_a", l), c), in1=RSA[:, 0:nt],
                                                               op0=ALU.mult, op1=ALU.mult),
                     reads=[("HL", c, si) for si in range(nseg)] + [tm(6), "PRM"], writes=[tm(1)])
                P.op("dve", lambda e, c=c: e.tensor_tensor(out=AB[:, NQ + c, 0:nt], in0=G1[:, 0:nt], in1=G2[:, 0:nt], op=ALU.mult),
                     reads=[tm(0), tm(1)], writes=[("AB", NQ + c)])
            for h in range(H):
                Z1, Z2 = TMP[0], TMP[1]
                P.op("act", lambda e, h=h: e.activation(out=SQF[:, 0:nt], in_=OO[:, h, 0:nt], func=AF.Square), reads=oo_res(h), writes=["SQF"])
                P.op("pe", lambda e: e.matmul(PS[3][:, 0:nt], lhsT=ONES, rhs=SQF[:, 0:nt], start=True, stop=True), reads=["SQF", "CST"], writes=[psr(3)])
                wt, wres, d = next_unit("z")
                po = PS[4 + h % 2]
                pres = psr(4 + h % 2)
                proj_group(po[:, 0:nt], pres, wt, wres, d[-1], HB, "HB", nt)
                P.op("act", lambda e: e.activation(out=Z2[:, 0:nt], in_=PS[3][:, 0:nt], func=AF.Ln, scale=1.0 / 128.0, bias=EPSC[:, 0:1]),
                     reads=[psr(3), "EPSC"], writes=[tm(1)])
                P.op("act", lambda e: e.activation(out=Z2[:, 0:nt], in_=Z2[:, 0:nt], func=AF.Exp, scale=-0.5), reads=[tm(1)], writes=[tm(1)])
                P.op("dve", lambda e, h=h: e.scalar_tensor_tensor(out=Z2[:, 0:nt], in0=OO[:, h, 0:nt], scalar=pc(("norm_b", l), 0), in1=Z2[:, 0:nt],
                                                               op0=ALU.mult, op1=ALU.mult),
                     reads=oo_res(h) + [tm(1), "PRM"], writes=[tm(1)])
                P.op("act", lambda e, po=po: e.activation(out=Z1[:, 0:nt], in_=po[:, 0:nt], func=AF.Exp, scale=-1.0), reads=[pres], writes=[tm(0)])
                P.op("act", lambda e: e.activation(out=Z1[:, 0:nt], in_=Z1[:, 0:nt], func=AF.Ln, bias=ONEC[:, 0:1]), reads=[tm(0), "EPSC"], writes=[tm(0)])
                P.op("act", lambda e: e.activation(out=Z1[:, 0:nt], in_=Z1[:, 0:nt], func=AF.Exp, scale=-1.0), reads=[tm(0)], writes=[tm(0)])
                P.op("dve", lambda e, po=po: e.tensor_tensor(out=Z1[:, 0:nt], in0=Z1[:, 0:nt], in1=po[:, 0:nt], op=ALU.mult), reads=[tm(0), pres], writes=[tm(0)])
                P.op("dve", lambda e, h=h: e.tensor_tensor(out=AB[:, NQ + KL + h, 0:nt], in0=Z1[:, 0:nt], in1=Z2[:, 0:nt], op=ALU.mult),
                     reads=[tm(0), tm(1)], writes=[("AB", NQ + KL + h)])
            P.dma("sp", "c_xl", lambda e: e.dma_start(out=X[:, :, 0:nt], in_=xsp[:, :, 0:nt].rearrange("k p t -> p k t")),
                  reads=["xsp"], writes=[("X", k) for k in range(KD)])
            for m in range(KD):
                wt, wres, d = next_unit("wout")
                pd = PS[4 + m % 2]
                pres = psr(4 + m % 2)

                def fn(e, wt=wt, pd=pd):
                    ins = None
                    for k in range(KD):
                        ins = e.matmul(pd[:, 0:nt], lhsT=wt[:, k, :], rhs=AB[:, NQ + k, 0:nt], start=(k == 0), stop=(k == KD - 1))
                    return ins
                P.op("pe", fn, reads=[wres] + [("AB", NQ + k) for k in range(KD)], writes=[pres])
                P.op("dve", lambda e, m=m, pd=pd: e.tensor_tensor(out=X[:, m, 0:nt], in0=X[:, m, 0:nt], in1=pd[:, 0:nt], op=ALU.add),
                     reads=[pres, ("X", m)], writes=[("X", m)])
            if last:
                for slot in range(3):
                    hb, hres = hist_a(l, slot)
                    P.dma("sp", "c_o0_%d_%d" % (l, slot), lambda e, hb=hb, slot=slot: e.dma_start(out=o_ca[l, slot], in_=hb[:, :, :]), reads=[hres], writes=[("o_ca", l, slot)])
                    hb, hres = lru_h(l, slot)
                    P.dma("sp", "c_o1_%d_%d" % (l, slot), lambda e, hb=hb, slot=slot: e.dma_start(out=o_lru[l, slot], in_=hb[:, :]), reads=[hres], writes=[("o_lru", l, slot)])
                    hb, hres = hist_b(l, slot)
                    P.dma("sp", "c_o2_%d_%d" % (l, slot), lambda e, hb=hb, slot=slot: e.dma_start(out=o_cb[l, slot], in_=hb[:, :, :]), reads=[hres], writes=[("o_cb", l, slot)])
                    sbuf_, sres = dstate(l, slot)
                    P.dma("sp", "c_o3_%d_%d" % (l, slot), lambda e, sbuf_=sbuf_, slot=slot: e.dma_start(out=o_dl[l, slot], in_=sbuf_[:, :, :]),
                          reads=[(sres, h) for h in range(H)], writes=[("o_dl", l, slot)])


        tiles = []
        for i in range(cfg.NBIG):
            tiles.append(("big", i * T, T, [0], T, False))
        tiles.append(("small", cfg.NBIG * T, 48, [0, 1, 2], 16, True))
        for kind, t0, nt, segs, L, last in tiles:
            if kind == "big":
                P.dma("sp", "c_x", lambda e, t0=t0, nt=nt: e.dma_start(out=X[:, :, 0:nt], in_=xp[:, :, t0:t0 + nt].rearrange("k p t -> p k t")),
                      writes=[("X", k) for k in range(KD)])
            else:
                P.dma("sp", "c_x", lambda e, t0=t0: e.dma_start(out=X[:, :, 0:16], in_=xp[:, :, t0:t0 + 16].rearrange("k p t -> p k t")),
                      writes=[("X", k) for k in range(KD)])
                P.dma("sp", "c_x2", lambda e: e.dma_start(out=X[:, :, 16:48], in_=xs.rearrange("k p t -> p k t")),
                      writes=[("X", k) for k in range(KD)])
            for l in range(DEPTH):
                ffn(l, 1, nt)
                mixer(l, nt, segs, L, last)
                ffn(l, 2, nt)
            rmsnorm(nt, ("final_norm",), None, None, dst_f32_inplace=True)
            if kind == "big":
                P.dma("sp", "c_y", lambda e, t0=t0, nt=nt: e.dma_start(out=yp[:, :, t0:t0 + nt].rearrange("k p t -> p k t"), in_=X[:, :, 0:nt]),
                      reads=[("X", k) for k in range(KD)], writes=[("yp", t0)])
            else:
                P.dma("sp", "c_y", lambda e, t0=t0: e.dma_start(out=yp[:, :, t0:t0 + 16].rearrange("k p t -> p k t"), in_=X[:, :, 0:16]),
                      reads=[("X", k) for k in range(KD)], writes=[("yp", t0)])
                P.dma("sp", "c_y2", lambda e: e.dma_start(out=ys.rearrange("k p t -> p k t"), in_=X[:, :, 16:48]),
                      reads=[("X", k) for k in range(KD)], writes=["ys"])
        assert wstate["gu"] == cfg.NU * len(tiles)
        P.emit(st)
    return nc


def _blk(w, ks, cols, UW):
    out = np.zeros((128, UW), np.float32)
    for i, k in enumerate(ks):
        blk = w[k * 128:(k + 1) * 128, cols]
        out[:, i * 128:i * 128 + blk.shape[1]] = blk
    return out


def prepare(cfg, inp):
    f32 = np.float32
    D, KD, KF, KL, H, NQ, DEPTH, LW = cfg.D, cfg.KD, cfg.KF, cfg.KL, cfg.H, cfg.NQ, cfg.DEPTH, cfg.LW
    g = {k: np.asarray(v, f32) for k, v in inp.items()}
    ws = np.zeros((cfg.NU, 128, cfg.UW), f32)
    o2 = 2 * LW
    o3 = o2 + NQ * 128
    o4 = o3 + H * 128
    for u, d in enumerate(cfg.units):
        kind = d[0]
        if kind in ("gate", "up", "down"):
            _, l, which, idx, ks = d
            wsel = {("gate", 1): g["ffn1_w_gate"], ("up", 1): g["ffn1_w_up"], ("down", 1): g["ffn1_w_down"],
                    ("gate", 2): g["ffn2_w_gate"], ("up", 2): g["ffn2_w_up"], ("down", 2): g["ffn2_w_down"]}[(kind, which)]
            ws[u] = _blk(wsel[l], ks, slice(idx * 128, (idx + 1) * 128), cfg.UW)
        else:
            _, l, idx, ks = d
            if kind == "xa":
                ws[u] = _blk(g["w_in"][l], ks, slice(idx * 128, (idx + 1) * 128), cfg.UW)
            elif kind == "ga":
                ws[u] = _blk(g["w_in"][l], ks, slice(LW + idx * 128, LW + (idx + 1) * 128), cfg.UW)
            elif kind == "qkv":
                ws[u] = _blk(g["w_in"][l], ks, slice(o2 + idx * 128, o2 + (idx + 1) * 128), cfg.UW)
            elif kind == "z":
                ws[u] = _blk(g["w_in"][l], ks, slice(o3 + idx * 128, o3 + (idx + 1) * 128), cfg.UW)
            elif kind == "tail":
                ws[u] = _blk(g["w_in"][l], ks, slice(o4, o4 + 2 * H), cfg.UW)
            elif kind == "wout":
                ws[u] = _blk(g["w_out"][l], ks, slice(idx * 128, (idx + 1) * 128), cfg.UW)
    prm = np.zeros((128, cfg.NP), f32)

    def put(name, arr):
        off, w = cfg.pcol[name]
        prm[:arr.shape[0], off:off + w] = arr

    def pk(v):
        return v.reshape(-1, 128).T
    for l in range(DEPTH):
        put(("ffn1_norm", l), pk(g["ffn1_norm"][l]))
        put(("mix_norm", l), pk(g["mix_norm"][l]))
        put(("ffn2_norm", l), pk(g["ffn2_norm"][l]))
        put(("conv_a_w", l), g["conv_a_w"][l].reshape(4, KL, 128).transpose(2, 1, 0).reshape(128, KL * 4))
        put(("conv_a_b", l), pk(g["conv_a_b"][l]))
        put(("rg_b", l), pk(g["rg_b"][l]))
        put(("ig_b", l), pk(g["ig_b"][l]))
        put(("lam", l), pk(g["lru_lambda"][l]))
        put(("norm_a", l), pk(g["norm_a"][l]))
        put(("conv_b_w", l), g["conv_b_w"][l].reshape(4, NQ, 128).transpose(2, 1, 0).reshape(128, NQ * 4))
        put(("norm_b", l), g["norm_b"][l].reshape(128, 1))
        put(("a_log", l), g["a_log"][l].reshape(H, 1))
        put(("dt_bias", l), g["dt_bias"][l].reshape(H, 1))
    put(("final_norm",), pk(g["final_norm"]))
    gw = np.zeros((DEPTH, 2, 128, KL, 128), f32)
    for l in range(DEPTH):
        for gi, name in enumerate(("rg_w", "ig_w")):
            w = g[name][l]
            for c in range(KL):
                gw[l, gi, 0:64, c, 0:64] = w[2 * c]
                gw[l, gi, 64:128, c, 64:128] = w[2 * c + 1]
    cst = np.zeros((128, 6, 128), f32)
    ii = np.arange(128)
    cst[:, 0, :] = np.eye(128)
    cst[:, 1, :] = (ii[:, None] > ii[None, :])
    cst[:, 2, :] = (ii[None, :] >= ii[:, None])
    cst[:, 3, :] = (ii[:, None] <= ii[None, :])
    cst[:, 4, :] = 1.0
    shared = {"wstream": ws, "prm": prm, "gw": gw, "cst": cst}
    in_maps = []
    for c in range(cfg.NCORES):
        m = dict(shared)
        if c < cfg.BATCH:
            stream = np.concatenate([g["meta_tokens"], g["x_prompt"][c]], axis=0)
            m["xp"] = np.ascontiguousarray(stream.T.reshape(KD, 128, cfg.NTOK))
        else:
            m["xp"] = np.zeros((KD, 128, cfg.NTOK), f32)
        xsm = g["x_sample"][2 * c:2 * c + 2].reshape(32, D)
        m["xs"] = np.ascontiguousarray(xsm.T.reshape(KD, 128, 32))
        sl = slice(2 * c, 2 * c + 2)
        m["sca"] = np.ascontiguousarray(g["state_conv_a"][:, sl].reshape(DEPTH, 2, 3, KL, 128).transpose(0, 4, 1, 3, 2))
        m["slru"] = np.ascontiguousarray(g["state_lru"][:, sl].reshape(DEPTH, 2, KL, 128).transpose(0, 3, 1, 2))
        m["scb"] = np.ascontiguousarray(g["state_conv_b"][:, sl].reshape(DEPTH, 2, 3, NQ, 128).transpose(0, 4, 1, 3, 2))
        m["sdl"] = np.ascontiguousarray(g["state_delta"][:, sl].transpose(0, 1, 3, 2, 4))
        in_maps.append(m)
    return in_maps


def assemble(cfg, res):
    f32 = np.float32
    D, KD, KL, H, NQ, DEPTH, LW = cfg.D, cfg.KD, cfg.KL, cfg.H, cfg.NQ, cfg.DEPTH, cfg.LW
    B, DB = cfg.BATCH, cfg.DEC_BATCH
    y_prompt = np.zeros((B, cfg.SEQ, D), f32)
    y_sample = np.zeros((DB, 16, D), f32)
    p_ca = np.zeros((DEPTH, B, 3, LW), f32)
    p_lru = np.zeros((DEPTH, B, LW), f32)
    p_cb = np.zeros((DEPTH, B, 3, NQ * 128), f32)
    p_dl = np.zeros((DEPTH, B, H, 128, 128), f32)
    s_ca = np.zeros((DEPTH, DB, 3, LW), f32)
    s_lru = np.zeros((DEPTH, DB, LW), f32)
    s_cb = np.zeros((DEPTH, DB, 3, NQ * 128), f32)
    s_dl = np.zeros((DEPTH, DB, H, 128, 128), f32)
    for c, r in enumerate(res):
        ypc = np.asarray(r["yp"]).reshape(D, cfg.NTOK).T
        if c < B:
            y_prompt[c] = ypc[cfg.NMETA:]
        ysc = np.asarray(r["ys"]).reshape(D, 32).T.reshape(2, 16, D)
        y_sample[2 * c:2 * c + 2] = ysc
        ca = np.asarray(r["o_ca"]).transpose(0, 1, 4, 3, 2).reshape(DEPTH, 3, 3, LW)
        lr = np.asarray(r["o_lru"]).transpose(0, 1, 3, 2).reshape(DEPTH, 3, LW)
        cb = np.asarray(r["o_cb"]).transpose(0, 1, 4, 3, 2).reshape(DEPTH, 3, 3, NQ * 128)
        dl = np.asarray(r["o_dl"]).transpose(0, 1, 3, 2, 4)
        if c < B:
            p_ca[:, c], p_lru[:, c], p_cb[:, c], p_dl[:, c] = ca[:, 0], lr[:, 0], cb[:, 0], dl[:, 0]
        for s in range(2):
            b = 2 * c + s
            s_ca[:, b], s_lru[:, b], s_cb[:, b], s_dl[:, b] = ca[:, 1 + s], lr[:, 1 + s], cb[:, 1 + s], dl[:, 1 + s]
    return (y_prompt, y_sample, p_ca, p_lru, p_cb, p_dl, s_ca, s_lru, s_cb, s_dl)


def run(cfg, inputs, trace=False):
    nc = build_program(cfg)
    in_maps = prepare(cfg, inputs)
    res = run_bass_kernel_spmd(nc, in_maps, core_ids=list(range(cfg.NCORES)), trace=trace)
    return assemble(cfg, res.results), res


def kernel(**inputs):
    cfg = Cfg()
    out, _ = run(cfg, inputs)
    return out
```

```python
import contextlib
import numpy as np
import concourse.bass as bass
import concourse.mybir as mybir
from concourse.bass_utils import run_bass_kernel_spmd
from concourse.ap import AP as APc

F32 = mybir.dt.float32
BF16 = mybir.dt.bfloat16
ALU = mybir.AluOpType
AF = mybir.ActivationFunctionType

ENGS = ("pe", "dve", "act", "pool", "sp")
EPS = 1e-6


def _flat(xs):
    out = []
    for x in xs:
        if isinstance(x, list):
            out.extend(_flat(x))
        else:
            out.append(x)
    return out


def tm(k):
    return [("ts", k, i) for i in range(4)]


def psr(b):
    return [("bank", b)]


class Prog:
    def __init__(self, nc):
        self.nc = nc
        self.ops = {e: [] for e in ENGS}
        self.cnt = {e: 0 for e in ("pe", "dve", "act", "pool")}
        self.clock = {e: {} for e in ENGS}
        self.vc = {}
        self.last_w = {}
        self.readers = {}
        self.chans = []

    def _need(self, eng, deps):
        waits = {}
        ck = self.clock[eng]
        for (tl, c) in deps:
            if ck.get(tl, 0) >= c:
                continue
            if waits.get(tl, 0) < c:
                waits[tl] = c
        for tl, c in waits.items():
            snap = self.vc.get((tl, c))
            if snap:
                for k, v in snap.items():
                    if ck.get(k, 0) < v:
                        ck[k] = v
            if ck.get(tl, 0) < c:
                ck[tl] = c
        return sorted(waits.items())

    def _deps(self, reads, writes):
        deps = []
        for r in reads:
            lw = self.last_w.get(r)
            if lw:
                deps.append(lw)
        for w in writes:
            lw = self.last_w.get(w)
            if lw:
                deps.append(lw)
            for tl, c in self.readers.get(w, {}).items():
                deps.append((tl, c))
        return deps

    def _commit(self, tl, c, reads, writes):
        for r in reads:
            self.readers.setdefault(r, {})[tl] = c
        for w in writes:
            self.last_w[w] = (tl, c)
            self.readers[w] = {}

    def op(self, eng, fn, reads=(), writes=()):
        reads, writes = _flat(reads), _flat(writes)
        writes = writes + [r for r in reads if isinstance(r, tuple) and r[0] == "bank"]
        deps = self._deps(reads, writes)
        if eng == "pe":
            deps = [d for d in deps if d[0] != "pe"]
        waits = self._need(eng, deps)
        self.cnt[eng] += 1
        c = self.cnt[eng]
        if eng == "pe":
            self.clock[eng][eng] = c
        snap = dict(self.clock[eng])
        snap[eng] = c
        self.vc[(eng, c)] = snap
        self.ops[eng].append((waits, fn, (eng, 1)))
        self._commit(eng, c, reads, writes)

    def dma(self, queue, chan, fn, reads=(), writes=()):
        if chan not in self.cnt:
            self.cnt[chan] = 0
            self.chans.append(chan)
        reads, writes = _flat(reads), _flat(writes)
        deps = self._deps(reads, writes)
        waits = self._need(queue, deps)
        self.cnt[chan] += 16
        c = self.cnt[chan]
        snap = dict(self.clock[queue])
        snap[chan] = c
        self.vc[(chan, c)] = snap
        self.ops[queue].append((waits, fn, (chan, 16)))
        self._commit(chan, c, reads, writes)

    def emit(self, st):
        nc = self.nc
        names = ["pe", "dve", "act", "pool"] + self.chans
        final = [(tl, self.cnt[tl]) for tl in names if self.cnt.get(tl, 0) > 0]
        sems = {}
        for i, n in enumerate(names):
            sems[n] = st.enter_context(nc.semaphore("s%d" % i))
        block = st.enter_context(nc.Block())
        handles = {"pe": block.tensor, "dve": block.vector, "act": block.scalar,
                   "pool": block.gpsimd, "sp": block.sync}

        def make(engname):
            oplist = self.ops[engname]

            def body(e):
                for waits, fn, inc in oplist:
                    for tl, c in waits:
                        e.wait_ge(sems[tl], c)
                    ins = fn(e)
                    ins.then_inc(sems[inc[0]], inc[1])
                if engname == "sp":
                    for tl, c in final:
                        e.wait_ge(sems[tl], c)
            return body

        for engname in ENGS:
            handles[engname](make(engname))


class Cfg:
    def __init__(self, D=2048, DFF=5632, SEQ=8192, BATCH=2, DEC_BATCH=16, DEPTH=2, NCORES=8):
        self.D, self.DFF, self.SEQ, self.BATCH, self.DEC_BATCH, self.DEPTH = D, DFF, SEQ, BATCH, DEC_BATCH, DEPTH
        self.NCORES = NCORES
        self.NMETA = 16
        self.DEC_SEQ = 16
        self.LW = D // 2
        self.H = (D - self.LW) // 128
        self.KD, self.KF, self.KL = D // 128, DFF // 128, self.LW // 128
        self.NQ = 3 * self.H
        self.N_IN = 2 * self.LW + self.NQ * 128 + self.H * 128 + 2 * self.H
        self.T = 512
        self.NTOK = self.NMETA + SEQ
        assert SEQ % self.T == 0 and DEC_BATCH == 2 * NCORES and BATCH <= NCORES
        self.NBIG = SEQ // self.T
        self.UW = 16 * 128
        self.units = []
        for l in range(DEPTH):
            self.units += self._ffn_units(l, 1)
            for c in range(self.KL):
                self.units.append(("xa", l, c, list(range(self.KD))))
            for j in range(self.NQ):
                self.units.append(("qkv", l, j, list(range(self.KD))))
            self.units.append(("tail", l, 0, list(range(self.KD))))
            for c in range(self.KL):
                self.units.append(("ga", l, c, list(range(self.KD))))
            for h in range(self.H):
                self.units.append(("z", l, h, list(range(self.KD))))
            for m in range(self.KD):
                self.units.append(("wout", l, m, list(range(self.KD))))
            self.units += self._ffn_units(l, 2)
        self.NU = len(self.units)
        self.pcol = {}
        n = 0

        def add(name, w):
            nonlocal n
            self.pcol[name] = (n, w)
            n += w
        for l in range(DEPTH):
            add(("ffn1_norm", l), self.KD)
            add(("mix_norm", l), self.KD)
            add(("ffn2_norm", l), self.KD)
            add(("conv_a_w", l), self.KL * 4)
            add(("conv_a_b", l), self.KL)
            add(("rg_b", l), self.KL)
            add(("ig_b", l), self.KL)
            add(("lam", l), self.KL)
            add(("norm_a", l), self.KL)
            add(("conv_b_w", l), self.NQ * 4)
            add(("norm_b", l), 1)
            add(("a_log", l), 1)
            add(("dt_bias", l), 1)
        add(("final_norm",), self.KD)
        self.NP = n

    def _ffn_units(self, l, which):
        us = []
        for f in range(self.KF):
            us.append(("gate", l, which, f, list(range(self.KD))))
            us.append(("up", l, which, f, list(range(self.KD))))
        for m in range(self.KD):
            ks = list(range(self.KF))
            for i in range(0, self.KF, 16):
                us.append(("down", l, which, m, ks[i:i + 16]))
        return us


def build_program(cfg):
    nc = bass.Bass("TRN2", target_bir_lowering=False)
    D, KD, KF, KL, H, NQ, T, DEPTH = cfg.D, cfg.KD, cfg.KF, cfg.KL, cfg.H, cfg.NQ, cfg.T, cfg.DEPTH
    NTOK, NP = cfg.NTOK, cfg.NP

    def din(name, shape):
        return nc.dram_tensor(name, list(shape), F32, kind="ExternalInput").ap()

    def dout(name, shape):
        return nc.dram_tensor(name, list(shape), F32, kind="ExternalOutput").ap()

    xp = din("xp", [KD, 128, NTOK])
    xs = din("xs", [KD, 128, 32])
    sca = din("sca", [DEPTH, 128, 2, KL, 3])
    slru = din("slru", [DEPTH, 128, 2, KL])
    scb = din("scb", [DEPTH, 128, 2, NQ, 3])
    sdl = din("sdl", [DEPTH, 2, 128, H, 128])
    wstream = din("wstream", [cfg.NU, 128, cfg.UW])
    prm_d = din("prm", [128, NP])
    gw_d = din("gw", [DEPTH, 2, 128, KL, 128])
    cst_d = din("cst", [128, 6, 128])
    yp = dout("yp", [KD, 128, NTOK])
    ys = dout("ys", [KD, 128, 32])
    o_ca = dout("o_ca", [DEPTH, 3, 128, KL, 3])
    o_lru = dout("o_lru", [DEPTH, 3, 128, KL])
    o_cb = dout("o_cb", [DEPTH, 3, 128, NQ, 3])
    o_dl = dout("o_dl", [DEPTH, 3, 128, H, 128])

    P = Prog(nc)
    st = contextlib.ExitStack()
    with st:
        def sb(name, shape, dt=F32):
            return st.enter_context(nc.sbuf_tensor(name, list(shape), dt))

        X = sb("X", [128, KD, T])
        HB = sb("HB", [128, KD, T], BF16)
        NAB = max(KF, NQ + KD)
        AB = sb("AB", [128, NAB, T], BF16)
        NSLOT = 4
        WB = [sb("WB%d" % i, [128, 16, 128], BF16) for i in range(NSLOT)]
        HL = sb("HL", [128, KL, T])
        OO = sb("OO", [128, H, T])
        RAW = sb("RAW", [128, 520])
        NTMP = 16
        TMP = [sb("TMP%d" % i, [128, T]) for i in range(NTMP)]
        BT = sb("BT", [128, 3, T], BF16)
        SQF = sb("SQF", [128, T])
        SP_ = [sb("SP%d" % l, [128, H, 128]) for l in range(DEPTH)]
        HAP = [sb("HAP%d" % l, [128, KL, 3]) for l in range(DEPTH)]
        HBP = [sb("HBP%d" % l, [128, NQ, 3]) for l in range(DEPTH)]
        LHP = [sb("LHP%d" % l, [128, KL]) for l in range(DEPTH)]
        HAS = sb("HAS", [128, 2, KL, 3])
        HBS = sb("HBS", [128, 2, NQ, 3])
        LHS = sb("LHS", [128, 2, KL])
        PRM = sb("PRM", [128, NP])
        DER = sb("DER", [128, DEPTH, KL + 1])
        GW = sb("GW", [128, DEPTH * 2 * KL, 128], BF16)
        CST = sb("CST", [128, 6, 128])
        IDB = sb("IDB", [128, 128], BF16)
        ONB = sb("ONB", [128, 128], BF16)

        PS = [st.enter_context(nc.psum_tensor("PS%d" % i, [128, 512], F32)) for i in range(7)]
        PSB = st.enter_context(nc.psum_tensor("PSB", [128, 1024], BF16))

        IDENT, MSL, MUI, UTRI, ONES = (CST[:, i, :] for i in range(5))

        def pc(name, j=0, rows=128):
            off, w = cfg.pcol[name]
            return PRM[0:rows, off + j:off + j + 1]

        EPSC = sb("EPSC", [128, 1])
        ONEC = sb("ONEC", [128, 1])
        P.op("dve", lambda e: e.memset(EPSC[:], EPS), writes=["EPSC"])
        P.op("dve", lambda e: e.memset(ONEC[:], 1.0), writes=["EPSC"])
        P.dma("sp", "c_prm", lambda e: e.dma_start(out=PRM[:], in_=prm_d[:, :]), writes=["PRM"])
        P.dma("sp", "c_cst", lambda e: e.dma_start(out=CST[:], in_=cst_d[:, :, :]), writes=["CST"])
        P.dma("pool", "c_gw", lambda e: e.dma_start(
            out=GW[:].rearrange("p (a k) j -> p a k j", k=KL),
            in_=gw_d.rearrange("l g p k j -> p (l g) k j")), writes=["GW"])
        P.op("dve", lambda e: e.tensor_copy(out=IDB[:], in_=IDENT), reads=["CST"], writes=["IDB"])
        P.op("dve", lambda e: e.tensor_copy(out=ONB[:], in_=ONES), reads=["CST"], writes=["ONB"])
        for l in range(DEPTH):
            lo, _ = cfg.pcol[("lam", l)]
            P.op("act", lambda e, l=l, lo=lo: e.activation(out=DER[:, l, 0:KL], in_=PRM[:, lo:lo + KL], func=AF.Exp, scale=-1.0),
                 reads=["PRM"], writes=[("DER", l)])
            P.op("act", lambda e, l=l: e.activation(out=DER[:, l, 0:KL], in_=DER[:, l, 0:KL], func=AF.Ln, bias=ONEC[:, 0:1]),
                 reads=[("DER", l), "EPSC"], writes=[("DER", l)])
            P.op("dve", lambda e, l=l: e.tensor_scalar(out=DER[:, l, 0:KL], in0=DER[:, l, 0:KL], scalar1=-8.0, scalar2=None, op0=ALU.mult),
                 reads=[("DER", l)], writes=[("DER", l)])
            ao, _ = cfg.pcol[("a_log", l)]
            P.op("act", lambda e, l=l, ao=ao: e.activation(out=DER[0:H, l, KL:KL + 1], in_=PRM[0:H, ao:ao + 1], func=AF.Exp),
                 reads=["PRM"], writes=[("DERa", l)])
            P.op("dve", lambda e, l=l: e.tensor_scalar(out=DER[0:H, l, KL:KL + 1], in0=DER[0:H, l, KL:KL + 1], scalar1=-1.0, scalar2=None, op0=ALU.mult),
                 reads=[("DERa", l)], writes=[("DERa", l)])
            P.op("dve", lambda e, l=l: e.memset(SP_[l][:], 0.0), writes=[(("S", l, 0), h) for h in range(H)])
            P.op("dve", lambda e, l=l: e.memset(HAP[l][:], 0.0), writes=[("HA", l, 0)])
            P.op("dve", lambda e, l=l: e.memset(HBP[l][:], 0.0), writes=[("HBh", l, 0)])
            P.op("dve", lambda e, l=l: e.memset(LHP[l][:], 0.0), writes=[("LH", l, 0)])

        wstate = {"gu": 0}

        def next_unit(expect_kind):
            gu = wstate["gu"]
            wstate["gu"] += 1
            u = gu % cfg.NU
            desc = cfg.units[u]
            assert desc[0] == expect_kind, (desc, expect_kind)
            nk = len(desc[-1])
            s = gu % NSLOT
            P.dma("pool", "w%d" % s,
                  lambda e, u=u, s=s, nk=nk: e.dma_start(out=WB[s][:, 0:nk, :],
                                                        in_=wstream[u, :, 0:nk * 128].rearrange("p (k j) -> p k j", j=128)),
                  writes=[("WB", s)])
            return WB[s], ("WB", s), desc

        def rmsnorm(nt, wname, dst_bf, dst_res, dst_list=None):
            ps = PS[6]
            for kc in range(KD):
                b = kc % 2
                P.op("act", lambda e, kc=kc, b=b: e.activation(out=BT[:, b, 0:nt], in_=X[:, kc, 0:nt], func=AF.Square),
                     reads=[("X", kc)], writes=[("BT", b)])
                P.op("pe", lambda e, kc=kc, b=b: e.matmul(ps[:, 0:nt], lhsT=ONB[:], rhs=BT[:, b, 0:nt], start=(kc == 0), stop=(kc == KD - 1)),
                     reads=[("BT", b), "ONB"], writes=[psr(6)])
            rs = TMP[12]
            P.op("act", lambda e: e.activation(out=rs[:, 0:nt], in_=ps[:, 0:nt], func=AF.Ln, scale=1.0 / D, bias=EPSC[:, 0:1]),
                 reads=[psr(6), "EPSC"], writes=[tm(12)])
            P.op("act", lambda e: e.activation(out=rs[:, 0:nt], in_=rs[:, 0:nt], func=AF.Exp, scale=-0.5), reads=[tm(12)], writes=[tm(12)])
            for kc in range(KD):
                if dst_list is not None:
                    P.op("dve", lambda e, kc=kc: e.scalar_tensor_tensor(out=dst_list[kc][0][:, 0:nt], in0=X[:, kc, 0:nt], scalar=pc(wname, kc),
                                                                      in1=rs[:, 0:nt], op0=ALU.mult, op1=ALU.mult),
                         reads=[("X", kc), tm(12), "PRM"], writes=[dst_list[kc][1]])
                else:
                    P.op("dve", lambda e, kc=kc: e.scalar_tensor_tensor(out=dst_bf[:, kc, 0:nt], in0=X[:, kc, 0:nt], scalar=pc(wname, kc),
                                                                      in1=rs[:, 0:nt], op0=ALU.mult, op1=ALU.mult),
                         reads=[("X", kc), tm(12), "PRM"], writes=[(dst_res, kc)])

        def proj_group(ps_ap, ps_res, wt, wres, ks, src, src_res, nt, mcols=slice(0, 128), kmap=None):
            def fn(e):
                ins = None
                n = len(ks)
                for i, k in enumerate(ks):
                    ins = e.matmul(ps_ap, lhsT=wt[:, i, mcols], rhs=src[:, k, 0:nt], start=(i == 0), stop=(i == n - 1))
                return ins
            P.op("pe", fn, reads=[wres] + [(src_res, k) for k in ks], writes=[ps_res])

        def ffn(l, which, nt):
            rmsnorm(nt, ("ffn%d_norm" % which, l), HB, "HB")
            for f in range(KF):
                pg, pu = PS[f % 2], PS[2 + f % 2]
                wt, wres, d = next_unit("gate")
                proj_group(pg[:, 0:nt], psr(f % 2), wt, wres, d[-1], HB, "HB", nt)
                wt, wres, d = next_unit("up")
                proj_group(pu[:, 0:nt], psr(2 + f % 2), wt, wres, d[-1], HB, "HB", nt)
                tb = f % 2
                P.op("act", lambda e, pg=pg, tb=tb: e.activation(out=TMP[tb][:, 0:nt], in_=pg[:, 0:nt], func=AF.Silu),
                     reads=[psr(f % 2)], writes=[tm(tb)])
                P.op("dve", lambda e, pu=pu, tb=tb, f=f: e.tensor_tensor(out=AB[:, f, 0:nt], in0=TMP[tb][:, 0:nt], in1=pu[:, 0:nt], op=ALU.mult),
                     reads=[tm(tb), psr(2 + f % 2)], writes=[("AB", f)])
            for m in range(KD):
                pd = PS[4 + m % 2]
                pres = psr(4 + m % 2)
                nun = (KF + 15) // 16
                parts = [next_unit("down") for _ in range(nun)]

                tot = sum(len(p[2][-1]) for p in parts)
                i0 = 0
                for wt, wres, d in parts:
                    def fn(e, wt=wt, d=d, i0=i0, pd=pd, tot=tot):
                        ins = None
                        for j, k in enumerate(d[-1]):
                            ins = e.matmul(pd[:, 0:nt], lhsT=wt[:, j, :], rhs=AB[:, k, 0:nt], start=(i0 + j == 0), stop=(i0 + j == tot - 1))
                        return ins
                    P.op("pe", fn, reads=[wres] + [("AB", k) for k in d[-1]], writes=[pres])
                    i0 += len(d[-1])
                P.op("dve", lambda e, m=m, pd=pd: e.scalar_tensor_tensor(out=X[:, m, 0:nt], in0=pd[:, 0:nt], scalar=0.5, in1=X[:, m, 0:nt],
                                                                       op0=ALU.mult, op1=ALU.add),
                     reads=[pres, ("X", m)], writes=[("X", m)])

        def hist_a(l, slot):
            return (HAP[l], ("HA", l, 0)) if slot == 0 else (HAS[:, slot - 1], ("HAS", slot))

        def hist_b(l, slot):
            return (HBP[l], ("HBh", l, 0)) if slot == 0 else (HBS[:, slot - 1], ("HBS", slot))

        def lru_h(l, slot):
            return (LHP[l], ("LH", l, 0)) if slot == 0 else (LHS[:, slot - 1], ("LHS", slot))

        def dstate(l, slot):
            return (SP_[l], ("S", l, 0)) if slot == 0 else (HL[:, :, 64 + 128 * (slot - 1):64 + 128 * slot], ("SS", slot))

        def conv_chunk(ps, pres, nt, segs, L, hist_fn, l, ch, wname, bias_name, out_t, out_res):
            nseg = len(segs)
            Le = L + 3
            rawv = RAW[:, 0:nseg * Le].rearrange("p (s l) -> p s l", l=Le)
            P.op("act", lambda e: e.activation(out=rawv[:, :, 3:Le], in_=ps[:, 0:nt].rearrange("p (s l) -> p s l", l=L), func=AF.Copy),
                 reads=[pres], writes=["RAWd"])
            for si, slot in enumerate(segs):
                hb, hres = hist_fn(l, slot)
                P.op("dve", lambda e, si=si, hb=hb: e.tensor_copy(out=rawv[:, si, 0:3], in_=hb[:, ch, :]),
                     reads=[hres], writes=[("RAWh", si)])
                P.op("dve", lambda e, si=si, hb=hb: e.tensor_copy(out=hb[:, ch, :], in_=rawv[:, si, L:Le]),
                     reads=["RAWd", ("RAWh", si)], writes=[hres])
            woff, _ = cfg.pcol[(wname, l)]
            outv = out_t[:, 0:nt].rearrange("p (s l) -> p s l", l=L)
            rd = ["RAWd"] + [("RAWh", si) for si in range(nseg)] + ["PRM"]
            if bias_name is not None:
                P.op("dve", lambda e: e.tensor_scalar(out=outv, in0=rawv[:, :, 0:L], scalar1=PRM[:, woff + ch * 4:woff + ch * 4 + 1],
                                                     scalar2=pc((bias_name, l), ch), op0=ALU.mult, op1=ALU.add),
                     reads=rd, writes=[out_res])
            else:
                P.op("dve", lambda e: e.tensor_scalar(out=outv, in0=rawv[:, :, 0:L], scalar1=PRM[:, woff + ch * 4:woff + ch * 4 + 1],
                                                     scalar2=None, op0=ALU.mult),
                     reads=rd, writes=[out_res])
            for i in range(1, 4):
                P.op("dve", lambda e, i=i: e.scalar_tensor_tensor(out=outv, in0=rawv[:, :, i:i + L],
                                                                scalar=PRM[:, woff + ch * 4 + i:woff + ch * 4 + i + 1],
                                                                in1=outv, op0=ALU.mult, op1=ALU.add),
                     reads=rd + [out_res], writes=[out_res])

        def mixer(l, nt, segs, L, last):
            nseg = len(segs)
            GT = [TMP[0], TMP[1], TMP[12], SQF]
            GTR = [tm(0), tm(1), tm(12), ["SQF"]]
            rmsnorm(nt, ("mix_norm", l), HB, "HB")
            if last:
                P.dma("sp", "c_st0", lambda e: e.dma_start(out=HAS[:], in_=sca[l]), writes=[("HAS", 1), ("HAS", 2)])
                P.dma("sp", "c_st1", lambda e: e.dma_start(out=LHS[:], in_=slru[l]), writes=[("LHS", 1), ("LHS", 2)])
                P.dma("sp", "c_st2", lambda e: e.dma_start(out=HBS[:], in_=scb[l]), writes=[("HBS", 1), ("HBS", 2)])
                for s in range(2):
                    P.dma("sp", "c_st%d" % (3 + s), lambda e, s=s: e.dma_start(out=HL[:, :, 64 + 128 * s:192 + 128 * s], in_=sdl[l, s]),
                          writes=[(("SS", s + 1), h) for h in range(H)] + [("HL", c, 0) for c in range(KL)])
            XCs = [TMP[0], TMP[14]]
            XCr = [tm(0), tm(14)]
            SGs = [TMP[1], TMP[15]]
            SGr = [tm(1), tm(15)]
            Rt, IGt, At, A2t, Bt = TMP[1], TMP[2], TMP[3], TMP[4], TMP[5]
            RSA = TMP[6]
            stages = []

            def xa_proj(c, bk, par):
                wt, wres, d = next_unit("xa")
                proj_group(PS[bk][:, 0:nt], psr(bk), wt, wres, d[-1], HB, "HB", nt)

            def xa_a1(c, bk, par):
                XC, rXC = XCs[par], XCr[par]
                conv_chunk(PS[bk], psr(bk), nt, segs, L, hist_a, l, c, "conv_a_w", "conv_a_b", XC, rXC)
                P.op("act", lambda e: e.activation(out=BT[:, 2, 0:nt], in_=XC[:, 0:nt], func=AF.Copy), reads=[rXC], writes=[("BT", 2)])

            def xa_a2(c, bk, par):
                gi = (l * 2 + 0) * KL + c
                P.op("pe", lambda e: e.matmul(PS[0][:, 0:nt], lhsT=GW[:, gi, :], rhs=BT[:, 2, 0:nt], start=True, stop=True),
                     reads=[("BT", 2), "GW"], writes=[psr(0)])
                gi2 = (l * 2 + 1) * KL + c
                P.op("pe", lambda e: e.matmul(PS[2][:, 0:nt], lhsT=GW[:, gi2, :], rhs=BT[:, 2, 0:nt], start=True, stop=True),
                     reads=[("BT", 2), "GW"], writes=[psr(2)])
                P.op("act", lambda e: e.activation(out=Rt[:, 0:nt], in_=PS[0][:, 0:nt], func=AF.Sigmoid, bias=pc(("rg_b", l), c)),
                     reads=[psr(0), "PRM"], writes=[tm(1)])
                P.op("act", lambda e: e.activation(out=IGt[:, 0:nt], in_=PS[2][:, 0:nt], func=AF.Sigmoid, bias=pc(("ig_b", l), c)),
                     reads=[psr(2), "PRM"], writes=[tm(2)])
                P.op("act", lambda e: e.activation(out=At[:, 0:nt], in_=Rt[:, 0:nt], func=AF.Exp, scale=DER[:, l, c:c + 1]),
                     reads=[tm(1), ("DER", l)], writes=[tm(3)])
                P.op("act", lambda e: e.activation(out=A2t[:, 0:nt], in_=At[:, 0:nt], func=AF.Square), reads=[tm(3)], writes=[tm(4)])
                P.op("act", lambda e: e.activation(out=A2t[:, 0:nt], in_=A2t[:, 0:nt], func=AF.Ln, scale=-1.0, bias=ONEC[:, 0:1]),
                     reads=[tm(4), "EPSC"], writes=[tm(4)])
                P.op("act", lambda e: e.activation(out=A2t[:, 0:nt], in_=A2t[:, 0:nt], func=AF.Exp, scale=0.5), reads=[tm(4)], writes=[tm(4)])

            def xa_b(c, bk, par):
                XC, rXC = XCs[par], XCr[par]
                P.op("dve", lambda e: e.tensor_tensor(out=Bt[:, 0:nt], in0=IGt[:, 0:nt], in1=XC[:, 0:nt], op=ALU.mult),
                     reads=[tm(2), rXC], writes=[tm(5)])
                P.op("dve", lambda e: e.tensor_tensor(out=Bt[:, 0:nt], in0=Bt[:, 0:nt], in1=A2t[:, 0:nt], op=ALU.mult),
                     reads=[tm(5), tm(4)], writes=[tm(5)])
                for si, slot in enumerate(segs):
                    hb, hres = lru_h(l, slot)
                    cs = slice(si * L, (si + 1) * L)
                    P.op("dve", lambda e, cs=cs, hb=hb: e.tensor_tensor_scan(out=HL[:, c, cs], data0=At[:, cs], data1=Bt[:, cs],
                                                                         initial=hb[:, c:c + 1], op0=ALU.mult, op1=ALU.add),
                         reads=[tm(3), tm(5), hres], writes=[("HL", c, si)])
                    P.op("dve", lambda e, hb=hb, si=si: e.tensor_copy(out=hb[:, c:c + 1], in_=HL[:, c, (si + 1) * L - 1:(si + 1) * L]),
                         reads=[("HL", c, si)], writes=[hres])
                if c == KL - 1:
                    lru_stats()

            def lru_stats():
                for c in range(KL):
                    b = c % 2
                    P.op("act", lambda e, c=c, b=b: e.activation(out=BT[:, b, 0:nt], in_=HL[:, c, 0:nt], func=AF.Square),
                         reads=[("HL", c, si) for si in range(nseg)], writes=[("BT", b)])
                    P.op("pe", lambda e, c=c, b=b: e.matmul(PS[3][:, 0:nt], lhsT=ONB[:], rhs=BT[:, b, 0:nt], start=(c == 0), stop=(c == KL - 1)),
                         reads=[("BT", b), "ONB"], writes=[psr(3)])
                P.op("act", lambda e: e.activation(out=RSA[:, 0:nt], in_=PS[3][:, 0:nt], func=AF.Ln, scale=1.0 / cfg.LW, bias=EPSC[:, 0:1]),
                     reads=[psr(3), "EPSC"], writes=[tm(6)])
                P.op("act", lambda e: e.activation(out=RSA[:, 0:nt], in_=RSA[:, 0:nt], func=AF.Exp, scale=-0.5), reads=[tm(6)], writes=[tm(6)])
            for c in range(KL):
                stages.append((xa_proj, xa_a1, xa_a2, xa_b, c))

            def qkv_proj(j, bk, par):
                wt, wres, d = next_unit("qkv")
                proj_group(PS[bk][:, 0:nt], psr(bk), wt, wres, d[-1], HB, "HB", nt)

            def qkv_a1(j, bk, par):
                XC, rXC, SG, rSG = XCs[par], XCr[par], SGs[par], SGr[par]
                conv_chunk(PS[bk], psr(bk), nt, segs, L, hist_b, l, j, "conv_b_w", None, XC, rXC)

            def qkv_a2(j, bk, par):
                XC, rXC, SG, rSG = XCs[par], XCr[par], SGs[par], SGr[par]
                P.op("act", lambda e: e.activation(out=SG[:, 0:nt], in_=XC[:, 0:nt], func=AF.Exp, scale=-1.0), reads=[rXC], writes=[rSG])
                P.op("act", lambda e: e.activation(out=SG[:, 0:nt], in_=SG[:, 0:nt], func=AF.Ln, bias=ONEC[:, 0:1]), reads=[rSG, "EPSC"], writes=[rSG])
                P.op("act", lambda e: e.activation(out=SG[:, 0:nt], in_=SG[:, 0:nt], func=AF.Exp, scale=-1.0), reads=[rSG], writes=[rSG])
                if j < 2 * H:
                    nb = [6, 0][par]
                    P.op("dve", lambda e: e.tensor_tensor(out=XC[:, 0:nt], in0=XC[:, 0:nt], in1=SG[:, 0:nt], op=ALU.mult), reads=[rXC, rSG], writes=[rXC])
                    P.op("act", lambda e: e.activation(out=SQF[:, 0:nt], in_=XC[:, 0:nt], func=AF.Square), reads=[rXC], writes=["SQF"])
                    P.op("pe", lambda e: e.matmul(PS[nb][:, 0:nt], lhsT=ONES, rhs=SQF[:, 0:nt], start=True, stop=True),
                         reads=["SQF", "CST"], writes=[psr(nb)])
                else:
                    P.op("dve", lambda e: e.tensor_tensor(out=AB[:, j, 0:nt], in0=XC[:, 0:nt], in1=SG[:, 0:nt], op=ALU.mult),
                         reads=[rXC, rSG], writes=[("AB", j)])

            def qkv_b(j, bk, par):
                if j >= 2 * H:
                    return
                XC, rXC, SG, rSG = XCs[par], XCr[par], SGs[par], SGr[par]
                nb = [6, 0][par]
                P.op("act", lambda e: e.activation(out=SG[:, 0:nt], in_=PS[nb][:, 0:nt], func=AF.Ln, bias=EPSC[:, 0:1]),
                     reads=[psr(nb), "EPSC"], writes=[rSG])
                P.op("act", lambda e: e.activation(out=SG[:, 0:nt], in_=SG[:, 0:nt], func=AF.Exp, scale=-0.5), reads=[rSG], writes=[rSG])
                sc = (128.0 ** -0.5) if j < H else 1.0
                P.op("dve", lambda e: e.scalar_tensor_tensor(out=AB[:, j, 0:nt], in0=XC[:, 0:nt], scalar=sc, in1=SG[:, 0:nt],
                                                           op0=ALU.mult, op1=ALU.mult),
                     reads=[rXC, rSG], writes=[("AB", j)])
            for j in range(NQ):
                stages.append((qkv_proj, qkv_a1, qkv_a2, qkv_b, j))

            def tail_proj(_, bk, par):
                wt, wres, d = next_unit("tail")
                proj_group(PS[1][0:H, 0:nt], psr(1), wt, wres, d[-1], HB, "HB", nt, mcols=slice(0, H))
                proj_group(PS[3][0:H, 0:nt], psr(3), wt, wres, d[-1], HB, "HB", nt, mcols=slice(H, 2 * H))

            def tail_b(_, bk, par):
                P.op("act", lambda e: e.activation(out=GT[0][0:H, 0:nt], in_=PS[1][0:H, 0:nt], func=AF.Sigmoid), reads=[psr(1)], writes=[GTR[0]])
                P.op("act", lambda e: e.activation(out=GT[1][0:H, 0:nt], in_=PS[3][0:H, 0:nt], func=AF.Exp, bias=pc(("dt_bias", l), 0, H)),
                     reads=[psr(3), "PRM"], writes=[GTR[1]])
                P.op("act", lambda e: e.activation(out=GT[1][0:H, 0:nt], in_=GT[1][0:H, 0:nt], func=AF.Ln, bias=ONEC[0:H, 0:1]),
                     reads=[GTR[1], "EPSC"], writes=[GTR[1]])
                P.op("dve", lambda e: e.tensor_scalar(out=GT[1][0:H, 0:nt], in0=GT[1][0:H, 0:nt], scalar1=DER[0:H, l, KL:KL + 1], scalar2=None, op0=ALU.mult),
                     reads=[GTR[1], ("DERa", l)], writes=[GTR[1]])
            stages.append((tail_proj, None, None, tail_b, 0))

            def run_pipeline(stages):
                n = len(stages)

                def call(i, k):
                    if 0 <= i < n and stages[i][k] is not None:
                        stages[i][k](stages[i][4], 4 + i % 2, i % 2)
                call(0, 0)
                call(1, 0)
                call(0, 1)
                call(0, 2)
                for i in range(n):
                    call(i + 2, 0)
                    call(i + 1, 1)
                    call(i, 3)
                    call(i + 1, 2)
            run_pipeline(stages)

            C = min(L, 128)
            nch = L // C
            nsq = max(1, int(np.ceil(np.log2(C))) - 1)
            HG = min(4, H)
            G = [TMP[2], TMP[3], TMP[4], TMP[5], TMP[8], TMP[9], TMP[10], TMP[11], TMP[13]]
            GR = [tm(2), tm(3), tm(4), tm(5), tm(8), tm(9), tm(10), tm(11), tm(13)]
            SM = TMP[7]
            rSM = tm(7)

            def v3(t):
                return t[:, 0:HG * 128].rearrange("p (h c) -> p h c", c=128)

            def bcl(ap2, n):
                return ap2.unsqueeze(2).to_broadcast([ap2.shape[0], ap2.shape[1], n])

            def bcm(ap2, n):
                a = [list(x) for x in ap2.ap]
                return APc(ap2.tensor, ap2.offset, [a[0], [0, n], a[1]])
            def do_chunk(si, slot, ci):
                Sb, Sres = dstate(l, slot)
                c0 = si * L + ci * C
                cs = slice(c0, c0 + C)
                P.op("dve", lambda e, cs=cs: e.tensor_tensor_scan(out=GT[2][0:H, cs], data0=ONES[0:H, 0:C], data1=GT[1][0:H, cs],
                                                                 initial=0.0, op0=ALU.mult, op1=ALU.add),
                     reads=[GTR[1], "CST"], writes=[GTR[2]])
                P.op("act", lambda e, cs=cs: e.activation(out=GT[3][0:H, cs], in_=GT[2][0:H, cs], func=AF.Exp), reads=[GTR[2]], writes=[GTR[3]])
                pss = PS[3]
                b3 = psr(3)

                def trfn(e, cs=cs):
                    e.transpose(out=pss[0:C, 0:H], in_=GT[0][0:H, cs], identity=IDENT[0:H, 0:H])
                    return e.transpose(out=pss[0:C, 8:8 + H], in_=GT[1][0:H, cs], identity=IDENT[0:H, 0:H])
                P.op("pe", trfn, reads=[GTR[0], GTR[1], "CST"], writes=[b3])
                P.op("dve", lambda e: e.tensor_copy(out=SM[0:C, 0:H], in_=pss[0:C, 0:H]), reads=[b3], writes=[rSM])
                P.op("dve", lambda e: e.tensor_copy(out=SM[0:C, H:2 * H], in_=pss[0:C, 8:8 + H]), reads=[b3], writes=[rSM])

                def cumfn(e):
                    e.matmul(pss[0:C, 16:16 + H], lhsT=UTRI[0:C, 0:C], rhs=SM[0:C, H:2 * H], start=True, stop=True)
                    return e.matmul(pss[:, 24:24 + H], lhsT=ONES[0:C, :], rhs=SM[0:C, H:2 * H], start=True, stop=True)
                P.op("pe", cumfn, reads=[rSM, "CST"], writes=[b3])
                P.op("dve", lambda e: e.tensor_copy(out=SM[0:C, 2 * H:3 * H], in_=pss[0:C, 16:16 + H]), reads=[b3], writes=[rSM])
                P.op("act", lambda e: e.activation(out=SM[0:C, 3 * H:4 * H], in_=SM[0:C, 2 * H:3 * H], func=AF.Exp), reads=[rSM], writes=[rSM])
                P.op("dve", lambda e: e.tensor_tensor(out=SM[0:C, 4 * H:5 * H], in0=pss[0:C, 24:24 + H], in1=SM[0:C, 2 * H:3 * H], op=ALU.subtract),
                     reads=[b3, rSM], writes=[rSM])
                P.op("act", lambda e: e.activation(out=SM[0:C, 4 * H:5 * H], in_=SM[0:C, 4 * H:5 * H], func=AF.Exp), reads=[rSM], writes=[rSM])
                P.op("dve", lambda e: e.tensor_tensor(out=SM[0:C, 5 * H:6 * H], in0=SM[0:C, 0:H], in1=SM[0:C, 3 * H:4 * H], op=ALU.mult),
                     reads=[rSM], writes=[rSM])
                P.op("act", lambda e: e.activation(out=SM[:, 6 * H:7 * H], in_=pss[:, 24:24 + H], func=AF.Exp), reads=[b3], writes=[rSM])
                def do_group(h0):
                    hs = list(range(h0, h0 + HG))
                    B0, B1, B2, B3, B4, B5, B6 = (v3(PS[i]) for i in range(7))
                    r0, r1, r2, r3, r4, r5, r6 = (psr(i) for i in range(7))
                    PTK = PSB[:, 0:HG * 128].rearrange("p (h c) -> p h c", c=128)
                    PTV = PSB[:, 512:512 + HG * 128].rearrange("p (h c) -> p h c", c=128)
                    r7 = psr(7)
                    Gv = [v3(g) for g in G]
                    rK = [("AB", H + h) for h in hs]
                    rQ = [("AB", h) for h in hs]
                    rV = [("AB", 2 * H + h) for h in hs]

                    def smc(k):
                        return SM[0:C, k * H + h0:k * H + h0 + HG]

                    def mm_each(bank, fnh):
                        def fn(e):
                            ins = None
                            for gi, h in enumerate(hs):
                                ins = fnh(e, gi, h)
                            return ins
                        return fn
                    P.op("pe", mm_each(0, lambda e, gi, h: e.matmul(B0[0:C, gi, 0:C], lhsT=AB[:, H + h, cs], rhs=AB[:, H + h, cs], start=True, stop=True)),
                         reads=[rK], writes=[r0])
                    P.op("pe", mm_each(1, lambda e, gi, h: e.matmul(B1[0:C, gi, 0:C], lhsT=AB[:, H + h, cs], rhs=AB[:, h, cs], start=True, stop=True)),
                         reads=[rK, rQ], writes=[r1])
                    P.op("pe", mm_each(2, lambda e, gi, h: e.matmul(B2[0:C, gi, 0:C], lhsT=IDENT[0:H, h:h + 1].to_broadcast([H, C]), rhs=GT[2][0:H, cs], start=True, stop=True)),
                         reads=[GTR[2], "CST"], writes=[r2])
                    Dm, DT = Gv[0], Gv[1]
                    W3 = (slice(0, C), slice(0, HG), slice(0, C))
                    P.op("dve", lambda e: e.tensor_tensor(out=Dm[W3], in0=B2[W3], in1=bcl(smc(2), C), op=ALU.subtract), reads=[r2, rSM], writes=[GR[0]])
                    P.op("dve", lambda e: e.tensor_scalar(out=DT[W3], in0=Dm[W3], scalar1=0.0, scalar2=None, op0=ALU.min), reads=[GR[0]], writes=[GR[1]])
                    P.op("dve", lambda e: e.tensor_scalar(out=Dm[W3], in0=Dm[W3], scalar1=0.0, scalar2=None, op0=ALU.max), reads=[GR[0]], writes=[GR[0]])
                    P.op("act", lambda e: e.activation(out=Dm[W3], in_=Dm[W3], func=AF.Exp, scale=-1.0), reads=[GR[0]], writes=[GR[0]])
                    P.op("act", lambda e: e.activation(out=DT[W3], in_=DT[W3], func=AF.Exp), reads=[GR[1]], writes=[GR[1]])
                    P.op("dve", lambda e: e.tensor_tensor(out=Dm[W3], in0=Dm[W3], in1=bcm(MSL[0:C, 0:C], HG), op=ALU.mult), reads=[GR[0], "CST"], writes=[GR[0]])
                    P.op("dve", lambda e: e.tensor_tensor(out=DT[W3], in0=DT[W3], in1=bcm(MUI[0:C, 0:C], HG), op=ALU.mult), reads=[GR[1], "CST"], writes=[GR[1]])
                    P.op("dve", lambda e: e.tensor_tensor(out=Dm[W3], in0=Dm[W3], in1=bcl(smc(0), C), op=ALU.mult), reads=[GR[0], rSM], writes=[GR[0]])
                    P.op("dve", lambda e: e.tensor_tensor(out=DT[W3], in0=B1[W3], in1=DT[W3], op=ALU.mult), reads=[r1, GR[1]], writes=[GR[1]])
                    Ac, rAc, An, rAn = Gv[2], GR[2], Gv[3], GR[3]
                    Bc, rBc, Bn, rBn = Gv[4], GR[4], Gv[5], GR[5]
                    Qc, rQc, Qn, rQn = Gv[6], GR[6], Gv[7], GR[7]
                    P.op("dve", lambda e, Ac=Ac: e.tensor_tensor(out=Ac[W3], in0=B0[W3], in1=Dm[W3], op=ALU.mult), reads=[r0, GR[0]], writes=[rAc])
                    P.op("pe", mm_each(4, lambda e, gi, h, Ac=Ac: e.transpose(out=B4[0:C, gi, 0:C], in_=Ac[0:C, gi, 0:C], identity=IDENT[0:C, 0:C])),
                         reads=[rAc, "CST"], writes=[r4])
                    P.op("act", lambda e, Bc=Bc: e.activation(out=Bc[W3], in_=B4[W3], func=AF.Copy), reads=[r4], writes=[rBc])
                    P.op("dve", lambda e, Qc=Qc: e.tensor_tensor(out=Qc[W3], in0=bcm(IDENT[0:C, 0:C], HG), in1=B4[W3], op=ALU.subtract), reads=[r4, "CST"], writes=[rQc])
                    for jq in range(1, nsq + 1):
                        P.op("pe", mm_each(4, lambda e, gi, h, Bc=Bc, Ac=Ac: e.matmul(B4[0:C, gi, 0:C], lhsT=Bc[0:C, gi, 0:C], rhs=Ac[0:C, gi, 0:C], start=True, stop=True)),
                             reads=[rBc, rAc], writes=[r4])
                        if jq < nsq:
                            P.op("pe", mm_each(5, lambda e, gi, h, Bc=Bc, Ac=Ac: e.matmul(B5[0:C, gi, 0:C], lhsT=Ac[0:C, gi, 0:C], rhs=Bc[0:C, gi, 0:C], start=True, stop=True)),
                                 reads=[rBc, rAc], writes=[r5])
                        P.op("act", lambda e, An=An: e.activation(out=An[W3], in_=B4[W3], func=AF.Copy), reads=[r4], writes=[rAn])
                        if jq < nsq:
                            P.op("dve", lambda e, Bn=Bn: e.tensor_copy(out=Bn[W3], in_=B5[W3]), reads=[r5], writes=[rBn])
                        P.op("pe", mm_each(6, lambda e, gi, h, An=An, Qc=Qc: e.matmul(B6[0:C, gi, 0:C], lhsT=An[0:C, gi, 0:C], rhs=Qc[0:C, gi, 0:C], start=True, stop=True)),
                             reads=[rAn, rQc], writes=[r6])
                        P.op("dve", lambda e, Qn=Qn, Qc=Qc: e.tensor_tensor(out=Qn[W3], in0=Qc[W3], in1=B6[W3], op=ALU.add), reads=[rQc, r6], writes=[rQn])
                        Ac, rAc, An, rAn = An, rAn, Ac, rAc
                        Bc, rBc, Bn, rBn = Bn, rBn, Bc, rBc
                        Qc, rQc, Qn, rQn = Qn, rQn, Qc, rQc
                    RK, rRK, KE, rKE = Gv[2], GR[2], Gv[3], GR[3]
                    Ut, rUt, WK, rWK = Gv[4], GR[4], Gv[5], GR[5]
                    QD, rQD = Qn, rQn
                    VB, rVB = Gv[0], GR[0]
                    Wt, rWt = Gv[8], GR[8]
                    QK, rQK = DT, GR[1]

                    def tkv(e):
                        ins = None
                        for gi, h in enumerate(hs):
                            e.transpose(out=PTK[0:C, gi, :], in_=AB[:, H + h, cs], identity=IDB[:])
                            ins = e.transpose(out=PTV[0:C, gi, :], in_=AB[:, 2 * H + h, cs], identity=IDB[:])
                        return ins
                    P.op("pe", tkv, reads=[rK, rV, "IDB", rAc, rAn], writes=[r7])
                    WF = (slice(0, C), slice(0, HG), slice(0, 128))
                    P.op("dve", lambda e: e.tensor_tensor(out=RK[WF], in0=PTK[WF], in1=bcl(smc(5), 128), op=ALU.mult), reads=[r7, rSM], writes=[rRK])
                    P.op("dve", lambda e: e.tensor_tensor(out=KE[WF], in0=PTK[WF], in1=bcl(smc(4), 128), op=ALU.mult), reads=[r7, rSM], writes=[rKE])
                    P.op("dve", lambda e: e.tensor_tensor(out=VB[WF], in0=PTV[WF], in1=bcl(smc(0), 128), op=ALU.mult), reads=[r7, rSM], writes=[rVB])
                    P.op("pe", mm_each(0, lambda e, gi, h, Qc=Qc: e.matmul(B0[0:C, gi, :], lhsT=Qc[0:C, gi, 0:C], rhs=VB[0:C, gi, :], start=True, stop=True)),
                         reads=[rQc, rVB], writes=[r0])
                    P.op("act", lambda e: e.activation(out=Ut[WF], in_=B0[WF], func=AF.Copy), reads=[r0], writes=[rUt])
                    P.op("pe", mm_each(3, lambda e, gi, h, Qc=Qc: e.matmul(B3[:, gi, 0:C], lhsT=RK[0:C, gi, :], rhs=Qc[0:C, gi, 0:C], start=True, stop=True)),
                         reads=[rRK, rQc], writes=[r3])
                    WT_ = (slice(0, 128), slice(0, HG), slice(0, C))
                    P.op("act", lambda e: e.activation(out=WK[WT_], in_=B3[WT_], func=AF.Copy), reads=[r3], writes=[rWK])
                    P.op("pe", mm_each(2, lambda e, gi, h: e.matmul(B2[:, gi, 0:C], lhsT=IDENT[0:H, h:h + 1].to_broadcast([H, 128]), rhs=GT[3][0:H, cs], start=True, stop=True)),
                         reads=[GTR[3], "CST"], writes=[r2])
                    P.op("dve", lambda e: e.tensor_tensor(out=QD[WT_], in0=AB[:, h0:h0 + HG, cs], in1=B2[WT_], op=ALU.mult), reads=[rQ, r2], writes=[rQD])
                    rS = [(Sres, h) for h in hs]
                    Sg = Sb[:, h0:h0 + HG, :]
                    P.op("pe", mm_each(4, lambda e, gi, h: e.matmul(B4[0:C, gi, :], lhsT=WK[:, gi, 0:C], rhs=Sb[:, h, :], start=True, stop=True)),
                         reads=[rWK, rS], writes=[r4])
                    P.op("dve", lambda e: e.tensor_tensor(out=Wt[WF], in0=Ut[WF], in1=B4[WF], op=ALU.subtract), reads=[rUt, r4], writes=[rWt])

                    def ofn(e):
                        ins = None
                        for gi, h in enumerate(hs):
                            e.matmul(B5[:, gi, 0:C], lhsT=Sb[:, h, :], rhs=QD[:, gi, 0:C], start=True, stop=False)
                            ins = e.matmul(B5[:, gi, 0:C], lhsT=Wt[0:C, gi, :], rhs=QK[0:C, gi, 0:C], start=False, stop=True)
                        return ins
                    P.op("pe", ofn, reads=[rS, rQD, rWt, rQK], writes=[r5])
                    P.op("act", lambda e, cs=cs: e.activation(out=OO[:, h0:h0 + HG, cs], in_=B5[WT_], func=AF.Copy), reads=[r5],
                         writes=[("OO", h, si, ci) for h in hs])
                    P.op("pe", mm_each(6, lambda e, gi, h: e.matmul(B6[:, gi, :], lhsT=KE[0:C, gi, :], rhs=Wt[0:C, gi, :], start=True, stop=True)),
                         reads=[rKE, rWt], writes=[r6])
                    WS_ = (slice(0, 128), slice(0, HG), slice(0, 128))
                    P.op("dve", lambda e: e.tensor_tensor(out=Sg, in0=Sg, in1=bcl(SM[:, 6 * H + h0:6 * H + h0 + HG], 128), op=ALU.mult), reads=[rS, rSM], writes=[rS])
                    P.op("dve", lambda e: e.tensor_tensor(out=Sg, in0=Sg, in1=B6[WS_], op=ALU.add), reads=[rS, r6], writes=[rS])
                for h0_ in range(0, H, HG):
                    do_group(h0_)
            for si_, slot_ in enumerate(segs):
                for ci_ in range(nch):
                    do_chunk(si_, slot_, ci_)
            oo_res = lambda h: [("OO", h, si, ci) for si in range(nseg) for ci in range(nch)]

            for c in range(KL):
                wt, wres, d = next_unit("ga")
                po = PS[4 + c % 2]
                pres = psr(4 + c % 2)
                proj_group(po[:, 0:nt], pres, wt, wres, d[-1], HB, "HB", nt)
                G1, G2 = TMP[0], TMP[1]
                P.op("act", lambda e, po=po: e.activation(out=G1[:, 0:nt], in_=po[:, 0:nt], func=AF.Square), reads=[pres], writes=[tm(0)])
                P.op("dve", lambda e: e.tensor_scalar(out=G1[:, 0:nt], in0=G1[:, 0:nt], scalar1=0.044715, scalar2=1.0, op0=ALU.mult, op1=ALU.add),
                     reads=[tm(0)], writes=[tm(0)])
                P.op("dve", lambda e, po=po: e.tensor_tensor(out=G1[:, 0:nt], in0=G1[:, 0:nt], in1=po[:, 0:nt], op=ALU.mult), reads=[tm(0), pres], writes=[tm(0)])
                P.op("act", lambda e: e.activation(out=G1[:, 0:nt], in_=G1[:, 0:nt], func=AF.Sigmoid, scale=1.5957691216057308), reads=[tm(0)], writes=[tm(0)])
                P.op("dve", lambda e, po=po: e.tensor_tensor(out=G1[:, 0:nt], in0=G1[:, 0:nt], in1=po[:, 0:nt], op=ALU.mult), reads=[tm(0), pres], writes=[tm(0)])
                P.op("dve", lambda e, c=c: e.scalar_tensor_tensor(out=G2[:, 0:nt], in0=HL[:, c, 0:nt], scalar=pc(("norm_a", l), c), in1=RSA[:, 0:nt],
                                                               op0=ALU.mult, op1=ALU.mult),
                     reads=[("HL", c, si) for si in range(nseg)] + [tm(6), "PRM"], writes=[tm(1)])
                P.op("dve", lambda e, c=c: e.tensor_tensor(out=AB[:, NQ + c, 0:nt], in0=G1[:, 0:nt], in1=G2[:, 0:nt], op=ALU.mult),
                     reads=[tm(0), tm(1)], writes=[("AB", NQ + c)])
            for h in range(H):
                Z1, Z2 = TMP[0], TMP[1]
                P.op("act", lambda e, h=h: e.activation(out=SQF[:, 0:nt], in_=OO[:, h, 0:nt], func=AF.Square), reads=oo_res(h), writes=["SQF"])
                P.op("pe", lambda e: e.matmul(PS[6][:, 0:nt], lhsT=ONES, rhs=SQF[:, 0:nt], start=True, stop=True), reads=["SQF", "CST"], writes=[psr(6)])
                wt, wres, d = next_unit("z")
                po = PS[4 + h % 2]
                pres = psr(4 + h % 2)
                proj_group(po[:, 0:nt], pres, wt, wres, d[-1], HB, "HB", nt)
                P.op("act", lambda e: e.activation(out=Z2[:, 0:nt], in_=PS[6][:, 0:nt], func=AF.Ln, scale=1.0 / 128.0, bias=EPSC[:, 0:1]),
                     reads=[psr(6), "EPSC"], writes=[tm(1)])
                P.op("act", lambda e: e.activation(out=Z2[:, 0:nt], in_=Z2[:, 0:nt], func=AF.Exp, scale=-0.5), reads=[tm(1)], writes=[tm(1)])
                P.op("dve", lambda e, h=h: e.scalar_tensor_tensor(out=Z2[:, 0:nt], in0=OO[:, h, 0:nt], scalar=pc(("norm_b", l), 0), in1=Z2[:, 0:nt],
                                                               op0=ALU.mult, op1=ALU.mult),
                     reads=oo_res(h) + [tm(1), "PRM"], writes=[tm(1)])
                P.op("act", lambda e, po=po: e.activation(out=Z1[:, 0:nt], in_=po[:, 0:nt], func=AF.Exp, scale=-1.0), reads=[pres], writes=[tm(0)])
                P.op("act", lambda e: e.activation(out=Z1[:, 0:nt], in_=Z1[:, 0:nt], func=AF.Ln, bias=ONEC[:, 0:1]), reads=[tm(0), "EPSC"], writes=[tm(0)])
                P.op("act", lambda e: e.activation(out=Z1[:, 0:nt], in_=Z1[:, 0:nt], func=AF.Exp, scale=-1.0), reads=[tm(0)], writes=[tm(0)])
                P.op("dve", lambda e, po=po: e.tensor_tensor(out=Z1[:, 0:nt], in0=Z1[:, 0:nt], in1=po[:, 0:nt], op=ALU.mult), reads=[tm(0), pres], writes=[tm(0)])
                P.op("dve", lambda e, h=h: e.tensor_tensor(out=AB[:, NQ + KL + h, 0:nt], in0=Z1[:, 0:nt], in1=Z2[:, 0:nt], op=ALU.mult),
                     reads=[tm(0), tm(1)], writes=[("AB", NQ + KL + h)])
            for m in range(KD):
                wt, wres, d = next_unit("wout")
                pd = PS[4 + m % 2]
                pres = psr(4 + m % 2)

                def fn(e, wt=wt, pd=pd):
                    ins = None
                    for k in range(KD):
                        ins = e.matmul(pd[:, 0:nt], lhsT=wt[:, k, :], rhs=AB[:, NQ + k, 0:nt], start=(k == 0), stop=(k == KD - 1))
                    return ins
                P.op("pe", fn, reads=[wres] + [("AB", NQ + k) for k in range(KD)], writes=[pres])
                P.op("dve", lambda e, m=m, pd=pd: e.tensor_tensor(out=X[:, m, 0:nt], in0=X[:, m, 0:nt], in1=pd[:, 0:nt], op=ALU.add),
                     reads=[pres, ("X", m)], writes=[("X", m)])
            if last:
                for slot in range(3):
                    hb, hres = hist_a(l, slot)
                    P.dma("sp", "c_o0_%d_%d" % (l, slot), lambda e, hb=hb, slot=slot: e.dma_start(out=o_ca[l, slot], in_=hb[:, :, :]), reads=[hres], writes=[("o_ca", l, slot)])
                    hb, hres = lru_h(l, slot)
                    P.dma("sp", "c_o1_%d_%d" % (l, slot), lambda e, hb=hb, slot=slot: e.dma_start(out=o_lru[l, slot], in_=hb[:, :]), reads=[hres], writes=[("o_lru", l, slot)])
                    hb, hres = hist_b(l, slot)
                    P.dma("sp", "c_o2_%d_%d" % (l, slot), lambda e, hb=hb, slot=slot: e.dma_start(out=o_cb[l, slot], in_=hb[:, :, :]), reads=[hres], writes=[("o_cb", l, slot)])
                    sbuf_, sres = dstate(l, slot)
                    P.dma("sp", "c_o3_%d_%d" % (l, slot), lambda e, sbuf_=sbuf_, slot=slot: e.dma_start(out=o_dl[l, slot], in_=sbuf_[:, :, :]),
                          reads=[(sres, h) for h in range(H)], writes=[("o_dl", l, slot)])


        tiles = []
        for i in range(cfg.NBIG):
            tiles.append(("big", i * T, T, [0], T, False))
        tiles.append(("small", cfg.NBIG * T, 48, [0, 1, 2], 16, True))
        for kind, t0, nt, segs, L, last in tiles:
            if kind == "big":
                P.dma("sp", "c_x", lambda e, t0=t0, nt=nt: e.dma_start(out=X[:, :, 0:nt], in_=xp[:, :, t0:t0 + nt].rearrange("k p t -> p k t")),
                      writes=[("X", k) for k in range(KD)])
            else:
                P.dma("sp", "c_x", lambda e, t0=t0: e.dma_start(out=X[:, :, 0:16], in_=xp[:, :, t0:t0 + 16].rearrange("k p t -> p k t")),
                      writes=[("X", k) for k in range(KD)])
                P.dma("sp", "c_x2", lambda e: e.dma_start(out=X[:, :, 16:48], in_=xs.rearrange("k p t -> p k t")),
                      writes=[("X", k) for k in range(KD)])
            for l in range(DEPTH):
                ffn(l, 1, nt)
                mixer(l, nt, segs, L, last)
                ffn(l, 2, nt)
            ydst = [(HL[:, k, :], [("HL", k, si) for si in range(3)]) for k in range(KL)] + \
                   [(OO[:, h, :], [("OO", h, si, ci) for si in range(3) for ci in range(4)]) for h in range(H)]
            rmsnorm(nt, ("final_norm",), None, None, dst_list=ydst)
            rdA = [r for k in range(KL) for r in ydst[k][1]]
            rdB = [r for k in range(KL, KD) for r in ydst[k][1]]
            if kind == "big":
                P.dma("sp", "c_y", lambda e, t0=t0, nt=nt: e.dma_start(out=yp[0:KL, :, t0:t0 + nt].rearrange("k p t -> p k t"), in_=HL[:, :, 0:nt]),
                      reads=rdA, writes=[("yp", t0, 0)])
                P.dma("sp", "c_yb", lambda e, t0=t0, nt=nt: e.dma_start(out=yp[KL:KD, :, t0:t0 + nt].rearrange("k p t -> p k t"), in_=OO[:, :, 0:nt]),
                      reads=rdB, writes=[("yp", t0, 1)])
            else:
                P.dma("sp", "c_y", lambda e, t0=t0: e.dma_start(out=yp[0:KL, :, t0:t0 + 16].rearrange("k p t -> p k t"), in_=HL[:, :, 0:16]),
                      reads=rdA, writes=[("yp", t0, 0)])
                P.dma("sp", "c_yb", lambda e, t0=t0: e.dma_start(out=yp[KL:KD, :, t0:t0 + 16].rearrange("k p t -> p k t"), in_=OO[:, :, 0:16]),
                      reads=rdB, writes=[("yp", t0, 1)])
                P.dma("sp", "c_y2", lambda e: e.dma_start(out=ys[0:KL].rearrange("k p t -> p k t"), in_=HL[:, :, 16:48]),
                      reads=rdA, writes=[("ys", 0)])
                P.dma("sp", "c_y2b", lambda e: e.dma_start(out=ys[KL:KD].rearrange("k p t -> p k t"), in_=OO[:, :, 16:48]),
                      reads=rdB, writes=[("ys", 1)])
        assert wstate["gu"] == cfg.NU * len(tiles)
        P.emit(st)
    return nc


def _blk(w, ks, cols, UW):
    out = np.zeros((128, UW), np.float32)
    for i, k in enumerate(ks):
        blk = w[k * 128:(k + 1) * 128, cols]
        out[:, i * 128:i * 128 + blk.shape[1]] = blk
    return out


def prepare(cfg, inp):
    f32 = np.float32
    D, KD, KF, KL, H, NQ, DEPTH, LW = cfg.D, cfg.KD, cfg.KF, cfg.KL, cfg.H, cfg.NQ, cfg.DEPTH, cfg.LW
    g = {k: np.asarray(v, f32) for k, v in inp.items()}
    ws = np.zeros((cfg.NU, 128, cfg.UW), f32)
    o2 = 2 * LW
    o3 = o2 + NQ * 128
    o4 = o3 + H * 128
    for u, d in enumerate(cfg.units):
        kind = d[0]
        if kind in ("gate", "up", "down"):
            _, l, which, idx, ks = d
            wsel = {("gate", 1): g["ffn1_w_gate"], ("up", 1): g["ffn1_w_up"], ("down", 1): g["ffn1_w_down"],
                    ("gate", 2): g["ffn2_w_gate"], ("up", 2): g["ffn2_w_up"], ("down", 2): g["ffn2_w_down"]}[(kind, which)]
            ws[u] = _blk(wsel[l], ks, slice(idx * 128, (idx + 1) * 128), cfg.UW)
        else:
            _, l, idx, ks = d
            if kind == "xa":
                ws[u] = _blk(g["w_in"][l], ks, slice(idx * 128, (idx + 1) * 128), cfg.UW)
            elif kind == "ga":
                ws[u] = _blk(g["w_in"][l], ks, slice(LW + idx * 128, LW + (idx + 1) * 128), cfg.UW)
            elif kind == "qkv":
                ws[u] = _blk(g["w_in"][l], ks, slice(o2 + idx * 128, o2 + (idx + 1) * 128), cfg.UW)
            elif kind == "z":
                ws[u] = _blk(g["w_in"][l], ks, slice(o3 + idx * 128, o3 + (idx + 1) * 128), cfg.UW)
            elif kind == "tail":
                ws[u] = _blk(g["w_in"][l], ks, slice(o4, o4 + 2 * H), cfg.UW)
            elif kind == "wout":
                ws[u] = _blk(g["w_out"][l], ks, slice(idx * 128, (idx + 1) * 128), cfg.UW)
    prm = np.zeros((128, cfg.NP), f32)

    def put(name, arr):
        off, w = cfg.pcol[name]
        prm[:arr.shape[0], off:off + w] = arr

    def pk(v):
        return v.reshape(-1, 128).T
    for l in range(DEPTH):
        put(("ffn1_norm", l), pk(g["ffn1_norm"][l]))
        put(("mix_norm", l), pk(g["mix_norm"][l]))
        put(("ffn2_norm", l), pk(g["ffn2_norm"][l]))
        put(("conv_a_w", l), g["conv_a_w"][l].reshape(4, KL, 128).transpose(2, 1, 0).reshape(128, KL * 4))
        put(("conv_a_b", l), pk(g["conv_a_b"][l]))
        put(("rg_b", l), pk(g["rg_b"][l]))
        put(("ig_b", l), pk(g["ig_b"][l]))
        put(("lam", l), pk(g["lru_lambda"][l]))
        put(("norm_a", l), pk(g["norm_a"][l]))
        put(("conv_b_w", l), g["conv_b_w"][l].reshape(4, NQ, 128).transpose(2, 1, 0).reshape(128, NQ * 4))
        put(("norm_b", l), g["norm_b"][l].reshape(128, 1))
        put(("a_log", l), g["a_log"][l].reshape(H, 1))
        put(("dt_bias", l), g["dt_bias"][l].reshape(H, 1))
    put(("final_norm",), pk(g["final_norm"]))
    gw = np.zeros((DEPTH, 2, 128, KL, 128), f32)
    for l in range(DEPTH):
        for gi, name in enumerate(("rg_w", "ig_w")):
            w = g[name][l]
            for c in range(KL):
                gw[l, gi, 0:64, c, 0:64] = w[2 * c]
                gw[l, gi, 64:128, c, 64:128] = w[2 * c + 1]
    cst = np.zeros((128, 6, 128), f32)
    ii = np.arange(128)
    cst[:, 0, :] = np.eye(128)
    cst[:, 1, :] = (ii[:, None] > ii[None, :])
    cst[:, 2, :] = (ii[None, :] >= ii[:, None])
    cst[:, 3, :] = (ii[:, None] <= ii[None, :])
    cst[:, 4, :] = 1.0
    shared = {"wstream": ws, "prm": prm, "gw": gw, "cst": cst}
    in_maps = []
    for c in range(cfg.NCORES):
        m = dict(shared)
        if c < cfg.BATCH:
            stream = np.concatenate([g["meta_tokens"], g["x_prompt"][c]], axis=0)
            m["xp"] = np.ascontiguousarray(stream.T.reshape(KD, 128, cfg.NTOK))
        else:
            m["xp"] = np.zeros((KD, 128, cfg.NTOK), f32)
        xsm = g["x_sample"][2 * c:2 * c + 2].reshape(32, D)
        m["xs"] = np.ascontiguousarray(xsm.T.reshape(KD, 128, 32))
        sl = slice(2 * c, 2 * c + 2)
        m["sca"] = np.ascontiguousarray(g["state_conv_a"][:, sl].reshape(DEPTH, 2, 3, KL, 128).transpose(0, 4, 1, 3, 2))
        m["slru"] = np.ascontiguousarray(g["state_lru"][:, sl].reshape(DEPTH, 2, KL, 128).transpose(0, 3, 1, 2))
        m["scb"] = np.ascontiguousarray(g["state_conv_b"][:, sl].reshape(DEPTH, 2, 3, NQ, 128).transpose(0, 4, 1, 3, 2))
        m["sdl"] = np.ascontiguousarray(g["state_delta"][:, sl].transpose(0, 1, 3, 2, 4))
        in_maps.append(m)
    return in_maps


def assemble(cfg, res):
    f32 = np.float32
    D, KD, KL, H, NQ, DEPTH, LW = cfg.D, cfg.KD, cfg.KL, cfg.H, cfg.NQ, cfg.DEPTH, cfg.LW
    B, DB = cfg.BATCH, cfg.DEC_BATCH
    y_prompt = np.zeros((B, cfg.SEQ, D), f32)
    y_sample = np.zeros((DB, 16, D), f32)
    p_ca = np.zeros((DEPTH, B, 3, LW), f32)
    p_lru = np.zeros((DEPTH, B, LW), f32)
    p_cb = np.zeros((DEPTH, B, 3, NQ * 128), f32)
    p_dl = np.zeros((DEPTH, B, H, 128, 128), f32)
    s_ca = np.zeros((DEPTH, DB, 3, LW), f32)
    s_lru = np.zeros((DEPTH, DB, LW), f32)
    s_cb = np.zeros((DEPTH, DB, 3, NQ * 128), f32)
    s_dl = np.zeros((DEPTH, DB, H, 128, 128), f32)
    for c, r in enumerate(res):
        ypc = np.asarray(r["yp"]).reshape(D, cfg.NTOK).T
        if c < B:
            y_prompt[c] = ypc[cfg.NMETA:]
        ysc = np.asarray(r["ys"]).reshape(D, 32).T.reshape(2, 16, D)
        y_sample[2 * c:2 * c + 2] = ysc
        ca = np.asarray(r["o_ca"]).transpose(0, 1, 4, 3, 2).reshape(DEPTH, 3, 3, LW)
        lr = np.asarray(r["o_lru"]).transpose(0, 1, 3, 2).reshape(DEPTH, 3, LW)
        cb = np.asarray(r["o_cb"]).transpose(0, 1, 4, 3, 2).reshape(DEPTH, 3, 3, NQ * 128)
        dl = np.asarray(r["o_dl"]).transpose(0, 1, 3, 2, 4)
        if c < B:
            p_ca[:, c], p_lru[:, c], p_cb[:, c], p_dl[:, c] = ca[:, 0], lr[:, 0], cb[:, 0], dl[:, 0]
        for s in range(2):
            b = 2 * c + s
            s_ca[:, b], s_lru[:, b], s_cb[:, b], s_dl[:, b] = ca[:, 1 + s], lr[:, 1 + s], cb[:, 1 + s], dl[:, 1 + s]
    return (y_prompt, y_sample, p_ca, p_lru, p_cb, p_dl, s_ca, s_lru, s_cb, s_dl)


def run(cfg, inputs, trace=False):
    nc = build_program(cfg)
    in_maps = prepare(cfg, inputs)
    res = run_bass_kernel_spmd(nc, in_maps, core_ids=list(range(cfg.NCORES)), trace=trace)
    return assemble(cfg, res.results), res


def kernel(**inputs):
    cfg = Cfg()
    out, _ = run(cfg, inputs)
    return out
```

```python
import contextlib
import numpy as np
import concourse.bass as bass
import concourse.mybir as mybir
from concourse.bass_utils import run_bass_kernel_spmd
from concourse.ap import AP as APc

F32 = mybir.dt.float32
BF16 = mybir.dt.bfloat16
ALU = mybir.AluOpType
AF = mybir.ActivationFunctionType

ENGS = ("pe", "dve", "act", "pool", "sp")
EPS = 1e-6


def _flat(xs):
    out = []
    for x in xs:
        if isinstance(x, list):
            out.extend(_flat(x))
        else:
            out.append(x)
    return out


def tm(k):
    return [("ts", k, i) for i in range(4)]


def psr(b):
    return [("bank", b)]


class Prog:
    def __init__(self, nc):
        self.nc = nc
        self.ops = {e: [] for e in ENGS}
        self.cnt = {e: 0 for e in ("pe", "dve", "act", "pool")}
        self.clock = {e: {} for e in ENGS}
        self.vc = {}
        self.last_w = {}
        self.readers = {}
        self.chans = []

    def _need(self, eng, deps):
        waits = {}
        ck = self.clock[eng]
        for (tl, c) in deps:
            if ck.get(tl, 0) >= c:
                continue
            if waits.get(tl, 0) < c:
                waits[tl] = c
        for tl, c in waits.items():
            snap = self.vc.get((tl, c))
            if snap:
                for k, v in snap.items():
                    if ck.get(k, 0) < v:
                        ck[k] = v
            if ck.get(tl, 0) < c:
                ck[tl] = c
        return sorted(waits.items())

    def _deps(self, reads, writes):
        deps = []
        for r in reads:
            lw = self.last_w.get(r)
            if lw:
                deps.append(lw)
        for w in writes:
            lw = self.last_w.get(w)
            if lw:
                deps.append(lw)
            for tl, c in self.readers.get(w, {}).items():
                deps.append((tl, c))
        return deps

    def _commit(self, tl, c, reads, writes):
        for r in reads:
            self.readers.setdefault(r, {})[tl] = c
        for w in writes:
            self.last_w[w] = (tl, c)
            self.readers[w] = {}

    def op(self, eng, fn, reads=(), writes=()):
        reads, writes = _flat(reads), _flat(writes)
        writes = writes + [r for r in reads if isinstance(r, tuple) and r[0] == "bank"]
        deps = self._deps(reads, writes)
        if eng == "pe":
            deps = [d for d in deps if d[0] != "pe"]
        waits = self._need(eng, deps)
        self.cnt[eng] += 1
        c = self.cnt[eng]
        if eng == "pe":
            self.clock[eng][eng] = c
        snap = dict(self.clock[eng])
        snap[eng] = c
        self.vc[(eng, c)] = snap
        self.ops[eng].append((waits, fn, (eng, 1)))
        self._commit(eng, c, reads, writes)

    def dma(self, queue, chan, fn, reads=(), writes=()):
        if chan not in self.cnt:
            self.cnt[chan] = 0
            self.chans.append(chan)
        reads, writes = _flat(reads), _flat(writes)
        deps = self._deps(reads, writes)
        waits = self._need(queue, deps)
        self.cnt[chan] += 16
        c = self.cnt[chan]
        snap = dict(self.clock[queue])
        snap[chan] = c
        self.vc[(chan, c)] = snap
        self.ops[queue].append((waits, fn, (chan, 16)))
        self._commit(chan, c, reads, writes)

    def emit(self, st):
        nc = self.nc
        names = ["pe", "dve", "act", "pool"] + self.chans
        final = [(tl, self.cnt[tl]) for tl in names if self.cnt.get(tl, 0) > 0]
        sems = {}
        for i, n in enumerate(names):
            sems[n] = st.enter_context(nc.semaphore("s%d" % i))
        block = st.enter_context(nc.Block())
        handles = {"pe": block.tensor, "dve": block.vector, "act": block.scalar,
                   "pool": block.gpsimd, "sp": block.sync}

        def make(engname):
            oplist = self.ops[engname]

            def body(e):
                for waits, fn, inc in oplist:
                    for tl, c in waits:
                        e.wait_ge(sems[tl], c)
                    ins = fn(e)
                    ins.then_inc(sems[inc[0]], inc[1])
                if engname == "sp":
                    for tl, c in final:
                        e.wait_ge(sems[tl], c)
            return body

        for engname in ENGS:
            handles[engname](make(engname))


class Cfg:
    def __init__(self, D=2048, DFF=5632, SEQ=8192, BATCH=2, DEC_BATCH=16, DEPTH=2, NCORES=8):
        self.D, self.DFF, self.SEQ, self.BATCH, self.DEC_BATCH, self.DEPTH = D, DFF, SEQ, BATCH, DEC_BATCH, DEPTH
        self.NCORES = NCORES
        self.NMETA = 16
        self.DEC_SEQ = 16
        self.LW = D // 2
        self.H = (D - self.LW) // 128
        self.KD, self.KF, self.KL = D // 128, DFF // 128, self.LW // 128
        self.NQ = 3 * self.H
        self.N_IN = 2 * self.LW + self.NQ * 128 + self.H * 128 + 2 * self.H
        self.T = 512
        self.NTOK = self.NMETA + SEQ
        assert SEQ % self.T == 0 and DEC_BATCH == 2 * NCORES and BATCH <= NCORES
        self.NBIG = SEQ // self.T
        self.UW = 16 * 128
        self.units = []
        for l in range(DEPTH):
            self.units += self._ffn_units(l, 1)
            for c in range(self.KL):
                self.units.append(("xa", l, c, list(range(self.KD))))
            for j in range(self.NQ):
                self.units.append(("qkv", l, j, list(range(self.KD))))
            self.units.append(("tail", l, 0, list(range(self.KD))))
            for c in range(self.KL):
                self.units.append(("ga", l, c, list(range(self.KD))))
            for h in range(self.H):
                self.units.append(("z", l, h, list(range(self.KD))))
            for m in range(self.KD):
                self.units.append(("wout", l, m, list(range(self.KD))))
            self.units += self._ffn_units(l, 2)
        self.NU = len(self.units)
        self.pcol = {}
        n = 0

        def add(name, w):
            nonlocal n
            self.pcol[name] = (n, w)
            n += w
        for l in range(DEPTH):
            add(("ffn1_norm", l), self.KD)
            add(("mix_norm", l), self.KD)
            add(("ffn2_norm", l), self.KD)
            add(("conv_a_w", l), self.KL * 4)
            add(("conv_a_b", l), self.KL)
            add(("rg_b", l), self.KL)
            add(("ig_b", l), self.KL)
            add(("lam", l), self.KL)
            add(("norm_a", l), self.KL)
            add(("conv_b_w", l), self.NQ * 4)
            add(("norm_b", l), 1)
            add(("a_log", l), 1)
            add(("dt_bias", l), 1)
        add(("final_norm",), self.KD)
        self.NP = n

    def _ffn_units(self, l, which):
        us = []
        for f in range(self.KF):
            us.append(("gate", l, which, f, list(range(self.KD))))
            us.append(("up", l, which, f, list(range(self.KD))))
        for m in range(self.KD):
            ks = list(range(self.KF))
            for i in range(0, self.KF, 16):
                us.append(("down", l, which, m, ks[i:i + 16]))
        return us


def build_program(cfg):
    nc = bass.Bass("TRN2", target_bir_lowering=False)
    D, KD, KF, KL, H, NQ, T, DEPTH = cfg.D, cfg.KD, cfg.KF, cfg.KL, cfg.H, cfg.NQ, cfg.T, cfg.DEPTH
    NTOK, NP = cfg.NTOK, cfg.NP

    def din(name, shape):
        return nc.dram_tensor(name, list(shape), F32, kind="ExternalInput").ap()

    def dout(name, shape):
        return nc.dram_tensor(name, list(shape), F32, kind="ExternalOutput").ap()

    xp = din("xp", [KD, 128, NTOK])
    xs = din("xs", [KD, 128, 32])
    sca = din("sca", [DEPTH, 128, 2, KL, 3])
    slru = din("slru", [DEPTH, 128, 2, KL])
    scb = din("scb", [DEPTH, 128, 2, NQ, 3])
    sdl = din("sdl", [DEPTH, 2, 128, H, 128])
    wstream = din("wstream", [cfg.NU, 128, cfg.UW])
    prm_d = din("prm", [128, NP])
    gw_d = din("gw", [DEPTH, 2, 128, KL, 128])
    cst_d = din("cst", [128, 6, 128])
    xsp = nc.dram_tensor("xsp", [KD, 128, T], F32, kind="Internal").ap()
    yp = dout("yp", [KD, 128, NTOK])
    ys = dout("ys", [KD, 128, 32])
    o_ca = dout("o_ca", [DEPTH, 3, 128, KL, 3])
    o_lru = dout("o_lru", [DEPTH, 3, 128, KL])
    o_cb = dout("o_cb", [DEPTH, 3, 128, NQ, 3])
    o_dl = dout("o_dl", [DEPTH, 3, 128, H, 128])

    P = Prog(nc)
    st = contextlib.ExitStack()
    with st:
        def sb(name, shape, dt=F32):
            return st.enter_context(nc.sbuf_tensor(name, list(shape), dt))

        X = sb("X", [128, KD, T])
        HB = sb("HB", [128, KD, T], BF16)
        NAB = max(KF, NQ + KD)
        AB = sb("AB", [128, NAB, T], BF16)
        NSLOT = 4
        WB = [sb("WB%d" % i, [128, 16, 128], BF16) for i in range(NSLOT)]
        HL = sb("HL", [128, KL, T])
        OO = sb("OO", [128, H, T])
        RAW = sb("RAW", [128, 520])
        NTMP = 16
        TMP = [sb("TMP%d" % i, [128, T]) for i in range(NTMP)]
        BT = sb("BT", [128, 3, T], BF16)
        SQF = sb("SQF", [128, T])
        SP_ = [sb("SP%d" % l, [128, H, 128]) for l in range(DEPTH)]
        HAP = [sb("HAP%d" % l, [128, KL, 3]) for l in range(DEPTH)]
        HBP = [sb("HBP%d" % l, [128, NQ, 3]) for l in range(DEPTH)]
        LHP = [sb("LHP%d" % l, [128, KL]) for l in range(DEPTH)]
        HAS = sb("HAS", [128, 2, KL, 3])
        HBS = sb("HBS", [128, 2, NQ, 3])
        LHS = sb("LHS", [128, 2, KL])
        PRM = sb("PRM", [128, NP])
        DER = sb("DER", [128, DEPTH, KL + 1])
        GW = sb("GW", [128, DEPTH * 2 * KL, 128], BF16)
        CST = sb("CST", [128, 6, 128])
        IDB = sb("IDB", [128, 128], BF16)
        ONB = sb("ONB", [128, 128], BF16)

        PS = [st.enter_context(nc.psum_tensor("PS%d" % i, [128, 512], F32)) for i in range(6)]
        PSBs = [st.enter_context(nc.psum_tensor("PSB%d" % i, [128, 1024], BF16)) for i in range(2)]

        IDENT, MSL, MUI, UTRI, ONES = (CST[:, i, :] for i in range(5))

        def pc(name, j=0, rows=128):
            off, w = cfg.pcol[name]
            return PRM[0:rows, off + j:off + j + 1]

        EPSC = sb("EPSC", [128, 1])
        ONEC = sb("ONEC", [128, 1])
        P.op("dve", lambda e: e.memset(EPSC[:], EPS), writes=["EPSC"])
        P.op("dve", lambda e: e.memset(ONEC[:], 1.0), writes=["EPSC"])
        P.dma("sp", "c_prm", lambda e: e.dma_start(out=PRM[:], in_=prm_d[:, :]), writes=["PRM"])
        P.dma("sp", "c_cst", lambda e: e.dma_start(out=CST[:], in_=cst_d[:, :, :]), writes=["CST"])
        P.dma("pool", "c_gw", lambda e: e.dma_start(
            out=GW[:].rearrange("p (a k) j -> p a k j", k=KL),
            in_=gw_d.rearrange("l g p k j -> p (l g) k j")), writes=["GW"])
        P.op("dve", lambda e: e.tensor_copy(out=IDB[:], in_=IDENT), reads=["CST"], writes=["IDB"])
        P.op("dve", lambda e: e.tensor_copy(out=ONB[:], in_=ONES), reads=["CST"], writes=["ONB"])
        for l in range(DEPTH):
            lo, _ = cfg.pcol[("lam", l)]
            P.op("act", lambda e, l=l, lo=lo: e.activation(out=DER[:, l, 0:KL], in_=PRM[:, lo:lo + KL], func=AF.Exp, scale=-1.0),
                 reads=["PRM"], writes=[("DER", l)])
            P.op("act", lambda e, l=l: e.activation(out=DER[:, l, 0:KL], in_=DER[:, l, 0:KL], func=AF.Ln, bias=ONEC[:, 0:1]),
                 reads=[("DER", l), "EPSC"], writes=[("DER", l)])
            P.op("dve", lambda e, l=l: e.tensor_scalar(out=DER[:, l, 0:KL], in0=DER[:, l, 0:KL], scalar1=-8.0, scalar2=None, op0=ALU.mult),
                 reads=[("DER", l)], writes=[("DER", l)])
            ao, _ = cfg.pcol[("a_log", l)]
            P.op("act", lambda e, l=l, ao=ao: e.activation(out=DER[0:H, l, KL:KL + 1], in_=PRM[0:H, ao:ao + 1], func=AF.Exp),
                 reads=["PRM"], writes=[("DERa", l)])
            P.op("dve", lambda e, l=l: e.tensor_scalar(out=DER[0:H, l, KL:KL + 1], in0=DER[0:H, l, KL:KL + 1], scalar1=-1.0, scalar2=None, op0=ALU.mult),
                 reads=[("DERa", l)], writes=[("DERa", l)])
            P.op("dve", lambda e, l=l: e.memset(SP_[l][:], 0.0), writes=[(("S", l, 0), h) for h in range(H)])
            P.op("dve", lambda e, l=l: e.memset(HAP[l][:], 0.0), writes=[("HA", l, 0)])
            P.op("dve", lambda e, l=l: e.memset(HBP[l][:], 0.0), writes=[("HBh", l, 0)])
            P.op("dve", lambda e, l=l: e.memset(LHP[l][:], 0.0), writes=[("LH", l, 0)])

        wstate = {"gu": 0}

        def next_unit(expect_kind):
            gu = wstate["gu"]
            wstate["gu"] += 1
            u = gu % cfg.NU
            desc = cfg.units[u]
            assert desc[0] == expect_kind, (desc, expect_kind)
            nk = len(desc[-1])
            s = gu % NSLOT
            P.dma("pool", "w%d" % s,
                  lambda e, u=u, s=s, nk=nk: e.dma_start(out=WB[s][:, 0:nk, :],
                                                        in_=wstream[u, :, 0:nk * 128].rearrange("p (k j) -> p k j", j=128)),
                  writes=[("WB", s)])
            return WB[s], ("WB", s), desc

        def rmsnorm(nt, wname, dst_bf, dst_res, dst_list=None):
            ps = PS[3]
            for kc in range(KD):
                b = kc % 2
                P.op("act", lambda e, kc=kc, b=b: e.activation(out=BT[:, b, 0:nt], in_=X[:, kc, 0:nt], func=AF.Square),
                     reads=[("X", kc)], writes=[("BT", b)])
                P.op("pe", lambda e, kc=kc, b=b: e.matmul(ps[:, 0:nt], lhsT=ONB[:], rhs=BT[:, b, 0:nt], start=(kc == 0), stop=(kc == KD - 1)),
                     reads=[("BT", b), "ONB"], writes=[psr(3)])
            rs = TMP[12]
            P.op("act", lambda e: e.activation(out=rs[:, 0:nt], in_=ps[:, 0:nt], func=AF.Ln, scale=1.0 / D, bias=EPSC[:, 0:1]),
                 reads=[psr(3), "EPSC"], writes=[tm(12)])
            P.op("act", lambda e: e.activation(out=rs[:, 0:nt], in_=rs[:, 0:nt], func=AF.Exp, scale=-0.5), reads=[tm(12)], writes=[tm(12)])
            for kc in range(KD):
                if dst_list is not None:
                    P.op("dve", lambda e, kc=kc: e.scalar_tensor_tensor(out=dst_list[kc][0][:, 0:nt], in0=X[:, kc, 0:nt], scalar=pc(wname, kc),
                                                                      in1=rs[:, 0:nt], op0=ALU.mult, op1=ALU.mult),
                         reads=[("X", kc), tm(12), "PRM"], writes=[dst_list[kc][1]])
                else:
                    P.op("dve", lambda e, kc=kc: e.scalar_tensor_tensor(out=dst_bf[:, kc, 0:nt], in0=X[:, kc, 0:nt], scalar=pc(wname, kc),
                                                                      in1=rs[:, 0:nt], op0=ALU.mult, op1=ALU.mult),
                         reads=[("X", kc), tm(12), "PRM"], writes=[(dst_res, kc)])

        def proj_group(ps_ap, ps_res, wt, wres, ks, src, src_res, nt, mcols=slice(0, 128), kmap=None):
            def fn(e):
                ins = None
                n = len(ks)
                for i, k in enumerate(ks):
                    ins = e.matmul(ps_ap, lhsT=wt[:, i, mcols], rhs=src[:, k, 0:nt], start=(i == 0), stop=(i == n - 1))
                return ins
            P.op("pe", fn, reads=[wres] + [(src_res, k) for k in ks], writes=[ps_res])

        def ffn(l, which, nt):
            rmsnorm(nt, ("ffn%d_norm" % which, l), HB, "HB")
            for f in range(KF):
                pg, pu = PS[f % 2], PS[2 + f % 2]
                wt, wres, d = next_unit("gate")
                proj_group(pg[:, 0:nt], psr(f % 2), wt, wres, d[-1], HB, "HB", nt)
                wt, wres, d = next_unit("up")
                proj_group(pu[:, 0:nt], psr(2 + f % 2), wt, wres, d[-1], HB, "HB", nt)
                tb = f % 2
                P.op("act", lambda e, pg=pg, tb=tb: e.activation(out=TMP[tb][:, 0:nt], in_=pg[:, 0:nt], func=AF.Silu),
                     reads=[psr(f % 2)], writes=[tm(tb)])
                P.op("dve", lambda e, pu=pu, tb=tb, f=f: e.tensor_tensor(out=AB[:, f, 0:nt], in0=TMP[tb][:, 0:nt], in1=pu[:, 0:nt], op=ALU.mult),
                     reads=[tm(tb), psr(2 + f % 2)], writes=[("AB", f)])
            for m in range(KD):
                pd = PS[4 + m % 2]
                pres = psr(4 + m % 2)
                nun = (KF + 15) // 16
                parts = [next_unit("down") for _ in range(nun)]

                tot = sum(len(p[2][-1]) for p in parts)
                i0 = 0
                for wt, wres, d in parts:
                    def fn(e, wt=wt, d=d, i0=i0, pd=pd, tot=tot):
                        ins = None
                        for j, k in enumerate(d[-1]):
                            ins = e.matmul(pd[:, 0:nt], lhsT=wt[:, j, :], rhs=AB[:, k, 0:nt], start=(i0 + j == 0), stop=(i0 + j == tot - 1))
                        return ins
                    P.op("pe", fn, reads=[wres] + [("AB", k) for k in d[-1]], writes=[pres])
                    i0 += len(d[-1])
                P.op("dve", lambda e, m=m, pd=pd: e.scalar_tensor_tensor(out=X[:, m, 0:nt], in0=pd[:, 0:nt], scalar=0.5, in1=X[:, m, 0:nt],
                                                                       op0=ALU.mult, op1=ALU.add),
                     reads=[pres, ("X", m)], writes=[("X", m)])

        def hist_a(l, slot):
            return (HAP[l], ("HA", l, 0)) if slot == 0 else (HAS[:, slot - 1], ("HAS", slot))

        def hist_b(l, slot):
            return (HBP[l], ("HBh", l, 0)) if slot == 0 else (HBS[:, slot - 1], ("HBS", slot))

        def lru_h(l, slot):
            return (LHP[l], ("LH", l, 0)) if slot == 0 else (LHS[:, slot - 1], ("LHS", slot))

        def dstate(l, slot):
            return (SP_[l], ("S", l, 0)) if slot == 0 else (HL[:, :, 64 + 128 * (slot - 1):64 + 128 * slot], ("SS", slot))

        def conv_chunk(ps, pres, nt, segs, L, hist_fn, l, ch, wname, bias_name, out_t, out_res):
            nseg = len(segs)
            Le = L + 3
            rawv = RAW[:, 0:nseg * Le].rearrange("p (s l) -> p s l", l=Le)
            P.op("act", lambda e: e.activation(out=rawv[:, :, 3:Le], in_=ps[:, 0:nt].rearrange("p (s l) -> p s l", l=L), func=AF.Copy),
                 reads=[pres], writes=["RAWd"])
            for si, slot in enumerate(segs):
                hb, hres = hist_fn(l, slot)
                P.op("dve", lambda e, si=si, hb=hb: e.tensor_copy(out=rawv[:, si, 0:3], in_=hb[:, ch, :]),
                     reads=[hres], writes=[("RAWh", si)])
                P.op("dve", lambda e, si=si, hb=hb: e.tensor_copy(out=hb[:, ch, :], in_=rawv[:, si, L:Le]),
                     reads=["RAWd", ("RAWh", si)], writes=[hres])
            woff, _ = cfg.pcol[(wname, l)]
            outv = out_t[:, 0:nt].rearrange("p (s l) -> p s l", l=L)
            rd = ["RAWd"] + [("RAWh", si) for si in range(nseg)] + ["PRM"]
            if bias_name is not None:
                P.op("dve", lambda e: e.tensor_scalar(out=outv, in0=rawv[:, :, 0:L], scalar1=PRM[:, woff + ch * 4:woff + ch * 4 + 1],
                                                     scalar2=pc((bias_name, l), ch), op0=ALU.mult, op1=ALU.add),
                     reads=rd, writes=[out_res])
            else:
                P.op("dve", lambda e: e.tensor_scalar(out=outv, in0=rawv[:, :, 0:L], scalar1=PRM[:, woff + ch * 4:woff + ch * 4 + 1],
                                                     scalar2=None, op0=ALU.mult),
                     reads=rd, writes=[out_res])
            for i in range(1, 4):
                P.op("dve", lambda e, i=i: e.scalar_tensor_tensor(out=outv, in0=rawv[:, :, i:i + L],
                                                                scalar=PRM[:, woff + ch * 4 + i:woff + ch * 4 + i + 1],
                                                                in1=outv, op0=ALU.mult, op1=ALU.add),
                     reads=rd + [out_res], writes=[out_res])

        def mixer(l, nt, segs, L, last):
            nseg = len(segs)
            GT = [TMP[0], TMP[1], TMP[12], SQF]
            GTR = [tm(0), tm(1), tm(12), ["SQF"]]
            rmsnorm(nt, ("mix_norm", l), HB, "HB")
            P.dma("sp", "c_xs", lambda e: e.dma_start(out=xsp[:, :, 0:nt].rearrange("k p t -> p k t"), in_=X[:, :, 0:nt]),
                  reads=[("X", k) for k in range(KD)], writes=["xsp"])
            if last:
                P.dma("sp", "c_st0", lambda e: e.dma_start(out=HAS[:], in_=sca[l]), writes=[("HAS", 1), ("HAS", 2)])
                P.dma("sp", "c_st1", lambda e: e.dma_start(out=LHS[:], in_=slru[l]), writes=[("LHS", 1), ("LHS", 2)])
                P.dma("sp", "c_st2", lambda e: e.dma_start(out=HBS[:], in_=scb[l]), writes=[("HBS", 1), ("HBS", 2)])
                for s in range(2):
                    P.dma("sp", "c_st%d" % (3 + s), lambda e, s=s: e.dma_start(out=HL[:, :, 64 + 128 * s:192 + 128 * s], in_=sdl[l, s]),
                          writes=[(("SS", s + 1), h) for h in range(H)] + [("HL", c, 0) for c in range(KL)])
            XCs = [TMP[0], TMP[14]]
            XCr = [tm(0), tm(14)]
            SGs = [TMP[1], TMP[15]]
            SGr = [tm(1), tm(15)]
            Rt, IGt, At, A2t, Bt = TMP[1], TMP[2], TMP[3], TMP[4], TMP[5]
            RSA = TMP[6]
            stages = []

            def xa_proj(c, bk, par):
                wt, wres, d = next_unit("xa")
                proj_group(PS[bk][:, 0:nt], psr(bk), wt, wres, d[-1], HB, "HB", nt)

            def xa_a1(c, bk, par):
                XC, rXC = XCs[par], XCr[par]
                conv_chunk(PS[bk], psr(bk), nt, segs, L, hist_a, l, c, "conv_a_w", "conv_a_b", XC, rXC)
                P.op("act", lambda e: e.activation(out=BT[:, 2, 0:nt], in_=XC[:, 0:nt], func=AF.Copy), reads=[rXC], writes=[("BT", 2)])

            def xa_a2(c, bk, par):
                gi = (l * 2 + 0) * KL + c
                P.op("pe", lambda e: e.matmul(PS[0][:, 0:nt], lhsT=GW[:, gi, :], rhs=BT[:, 2, 0:nt], start=True, stop=True),
                     reads=[("BT", 2), "GW"], writes=[psr(0)])
                gi2 = (l * 2 + 1) * KL + c
                P.op("pe", lambda e: e.matmul(PS[2][:, 0:nt], lhsT=GW[:, gi2, :], rhs=BT[:, 2, 0:nt], start=True, stop=True),
                     reads=[("BT", 2), "GW"], writes=[psr(2)])
                P.op("act", lambda e: e.activation(out=Rt[:, 0:nt], in_=PS[0][:, 0:nt], func=AF.Sigmoid, bias=pc(("rg_b", l), c)),
                     reads=[psr(0), "PRM"], writes=[tm(1)])
                P.op("act", lambda e: e.activation(out=IGt[:, 0:nt], in_=PS[2][:, 0:nt], func=AF.Sigmoid, bias=pc(("ig_b", l), c)),
                     reads=[psr(2), "PRM"], writes=[tm(2)])
                P.op("act", lambda e: e.activation(out=At[:, 0:nt], in_=Rt[:, 0:nt], func=AF.Exp, scale=DER[:, l, c:c + 1]),
                     reads=[tm(1), ("DER", l)], writes=[tm(3)])
                P.op("act", lambda e: e.activation(out=A2t[:, 0:nt], in_=At[:, 0:nt], func=AF.Square), reads=[tm(3)], writes=[tm(4)])
                P.op("act", lambda e: e.activation(out=A2t[:, 0:nt], in_=A2t[:, 0:nt], func=AF.Ln, scale=-1.0, bias=ONEC[:, 0:1]),
                     reads=[tm(4), "EPSC"], writes=[tm(4)])
                P.op("act", lambda e: e.activation(out=A2t[:, 0:nt], in_=A2t[:, 0:nt], func=AF.Exp, scale=0.5), reads=[tm(4)], writes=[tm(4)])

            def xa_b(c, bk, par):
                XC, rXC = XCs[par], XCr[par]
                P.op("dve", lambda e: e.tensor_tensor(out=Bt[:, 0:nt], in0=IGt[:, 0:nt], in1=XC[:, 0:nt], op=ALU.mult),
                     reads=[tm(2), rXC], writes=[tm(5)])
                P.op("dve", lambda e: e.tensor_tensor(out=Bt[:, 0:nt], in0=Bt[:, 0:nt], in1=A2t[:, 0:nt], op=ALU.mult),
                     reads=[tm(5), tm(4)], writes=[tm(5)])
                for si, slot in enumerate(segs):
                    hb, hres = lru_h(l, slot)
                    cs = slice(si * L, (si + 1) * L)
                    P.op("dve", lambda e, cs=cs, hb=hb: e.tensor_tensor_scan(out=HL[:, c, cs], data0=At[:, cs], data1=Bt[:, cs],
                                                                         initial=hb[:, c:c + 1], op0=ALU.mult, op1=ALU.add),
                         reads=[tm(3), tm(5), hres], writes=[("HL", c, si)])
                    P.op("dve", lambda e, hb=hb, si=si: e.tensor_copy(out=hb[:, c:c + 1], in_=HL[:, c, (si + 1) * L - 1:(si + 1) * L]),
                         reads=[("HL", c, si)], writes=[hres])
                if c == KL - 1:
                    lru_stats()

            def lru_stats():
                for c in range(KL):
                    b = c % 2
                    P.op("act", lambda e, c=c, b=b: e.activation(out=BT[:, b, 0:nt], in_=HL[:, c, 0:nt], func=AF.Square),
                         reads=[("HL", c, si) for si in range(nseg)], writes=[("BT", b)])
                    P.op("pe", lambda e, c=c, b=b: e.matmul(PS[3][:, 0:nt], lhsT=ONB[:], rhs=BT[:, b, 0:nt], start=(c == 0), stop=(c == KL - 1)),
                         reads=[("BT", b), "ONB"], writes=[psr(3)])
                P.op("act", lambda e: e.activation(out=RSA[:, 0:nt], in_=PS[3][:, 0:nt], func=AF.Ln, scale=1.0 / cfg.LW, bias=EPSC[:, 0:1]),
                     reads=[psr(3), "EPSC"], writes=[tm(6)])
                P.op("act", lambda e: e.activation(out=RSA[:, 0:nt], in_=RSA[:, 0:nt], func=AF.Exp, scale=-0.5), reads=[tm(6)], writes=[tm(6)])
            for c in range(KL):
                stages.append((xa_proj, xa_a1, xa_a2, xa_b, c))

            def qkv_proj(j, bk, par):
                wt, wres, d = next_unit("qkv")
                proj_group(PS[bk][:, 0:nt], psr(bk), wt, wres, d[-1], HB, "HB", nt)

            def qkv_a1(j, bk, par):
                XC, rXC, SG, rSG = XCs[par], XCr[par], SGs[par], SGr[par]
                conv_chunk(PS[bk], psr(bk), nt, segs, L, hist_b, l, j, "conv_b_w", None, XC, rXC)

            def qkv_a2(j, bk, par):
                XC, rXC, SG, rSG = XCs[par], XCr[par], SGs[par], SGr[par]
                P.op("act", lambda e: e.activation(out=SG[:, 0:nt], in_=XC[:, 0:nt], func=AF.Exp, scale=-1.0), reads=[rXC], writes=[rSG])
                P.op("act", lambda e: e.activation(out=SG[:, 0:nt], in_=SG[:, 0:nt], func=AF.Ln, bias=ONEC[:, 0:1]), reads=[rSG, "EPSC"], writes=[rSG])
                P.op("act", lambda e: e.activation(out=SG[:, 0:nt], in_=SG[:, 0:nt], func=AF.Exp, scale=-1.0), reads=[rSG], writes=[rSG])
                if j < 2 * H:
                    nb = [3, 0][par]
                    P.op("dve", lambda e: e.tensor_tensor(out=XC[:, 0:nt], in0=XC[:, 0:nt], in1=SG[:, 0:nt], op=ALU.mult), reads=[rXC, rSG], writes=[rXC])
                    P.op("act", lambda e: e.activation(out=SQF[:, 0:nt], in_=XC[:, 0:nt], func=AF.Square), reads=[rXC], writes=["SQF"])
                    P.op("pe", lambda e: e.matmul(PS[nb][:, 0:nt], lhsT=ONES, rhs=SQF[:, 0:nt], start=True, stop=True),
                         reads=["SQF", "CST"], writes=[psr(nb)])
                else:
                    P.op("dve", lambda e: e.tensor_tensor(out=AB[:, j, 0:nt], in0=XC[:, 0:nt], in1=SG[:, 0:nt], op=ALU.mult),
                         reads=[rXC, rSG], writes=[("AB", j)])

            def qkv_b(j, bk, par):
                if j >= 2 * H:
                    return
                XC, rXC, SG, rSG = XCs[par], XCr[par], SGs[par], SGr[par]
                nb = [3, 0][par]
                P.op("act", lambda e: e.activation(out=SG[:, 0:nt], in_=PS[nb][:, 0:nt], func=AF.Ln, bias=EPSC[:, 0:1]),
                     reads=[psr(nb), "EPSC"], writes=[rSG])
                P.op("act", lambda e: e.activation(out=SG[:, 0:nt], in_=SG[:, 0:nt], func=AF.Exp, scale=-0.5), reads=[rSG], writes=[rSG])
                sc = (128.0 ** -0.5) if j < H else 1.0
                P.op("dve", lambda e: e.scalar_tensor_tensor(out=AB[:, j, 0:nt], in0=XC[:, 0:nt], scalar=sc, in1=SG[:, 0:nt],
                                                           op0=ALU.mult, op1=ALU.mult),
                     reads=[rXC, rSG], writes=[("AB", j)])
            for j in range(NQ):
                stages.append((qkv_proj, qkv_a1, qkv_a2, qkv_b, j))

            def tail_proj(_, bk, par):
                wt, wres, d = next_unit("tail")
                proj_group(PS[1][0:H, 0:nt], psr(1), wt, wres, d[-1], HB, "HB", nt, mcols=slice(0, H))
                proj_group(PS[2][0:H, 0:nt], psr(2), wt, wres, d[-1], HB, "HB", nt, mcols=slice(H, 2 * H))

            def tail_b(_, bk, par):
                P.op("act", lambda e: e.activation(out=GT[0][0:H, 0:nt], in_=PS[1][0:H, 0:nt], func=AF.Sigmoid), reads=[psr(1)], writes=[GTR[0]])
                P.op("act", lambda e: e.activation(out=GT[1][0:H, 0:nt], in_=PS[2][0:H, 0:nt], func=AF.Exp, bias=pc(("dt_bias", l), 0, H)),
                     reads=[psr(2), "PRM"], writes=[GTR[1]])
                P.op("act", lambda e: e.activation(out=GT[1][0:H, 0:nt], in_=GT[1][0:H, 0:nt], func=AF.Ln, bias=ONEC[0:H, 0:1]),
                     reads=[GTR[1], "EPSC"], writes=[GTR[1]])
                P.op("dve", lambda e: e.tensor_scalar(out=GT[1][0:H, 0:nt], in0=GT[1][0:H, 0:nt], scalar1=DER[0:H, l, KL:KL + 1], scalar2=None, op0=ALU.mult),
                     reads=[GTR[1], ("DERa", l)], writes=[GTR[1]])
            stages.append((tail_proj, None, None, tail_b, 0))

            def run_pipeline(stages):
                n = len(stages)

                def call(i, k):
                    if 0 <= i < n and stages[i][k] is not None:
                        stages[i][k](stages[i][4], 4 + i % 2, i % 2)
                call(0, 0)
                call(1, 0)
                call(0, 1)
                call(0, 2)
                for i in range(n):
                    call(i + 2, 0)
                    call(i + 1, 1)
                    call(i, 3)
                    call(i + 1, 2)
            run_pipeline(stages)

            C = min(L, 128)
            nch = L // C
            nsq = max(1, int(np.ceil(np.log2(C))) - 1)
            HG = max(1, min(4, H // 2))
            assert H // HG == 2 and H % HG == 0 and KD >= 10
            Gsets = [[TMP[2], TMP[3], TMP[4], TMP[5], TMP[8], TMP[9], TMP[10], TMP[11], TMP[13]], [X[:, k, :] for k in range(9)]]
            GRsets = [[tm(2), tm(3), tm(4), tm(5), tm(8), tm(9), tm(10), tm(11), tm(13)], [[("X", k)] for k in range(9)]]
            SMs = [TMP[7], X[:, 9, :]]
            SMr = [tm(7), [("X", 9)]]

            def v3(t):
                return t[:, 0:HG * 128].rearrange("p (h c) -> p h c", c=128)

            def bcl(ap2, n):
                return ap2.unsqueeze(2).to_broadcast([ap2.shape[0], ap2.shape[1], n])

            def bcm(ap2, n):
                a = [list(x) for x in ap2.ap]
                return APc(ap2.tensor, ap2.offset, [a[0], [0, n], a[1]])
            def do_chunk(si, slot, ci, cidx):
                Sb, Sres = dstate(l, slot)
                SM, rSM = SMs[cidx % 2], SMr[cidx % 2]
                pre = []

                def OP(eng, fn, reads=(), writes=()):
                    pre.append((eng, fn, _flat(list(reads)), _flat(list(writes))))
                c0 = si * L + ci * C
                cs = slice(c0, c0 + C)
                OP("dve", lambda e, cs=cs: e.tensor_tensor_scan(out=GT[2][0:H, cs], data0=ONES[0:H, 0:C], data1=GT[1][0:H, cs],
                                                                 initial=0.0, op0=ALU.mult, op1=ALU.add),
                     reads=[GTR[1], "CST"], writes=[GTR[2]])
                OP("act", lambda e, cs=cs: e.activation(out=GT[3][0:H, cs], in_=GT[2][0:H, cs], func=AF.Exp), reads=[GTR[2]], writes=[GTR[3]])
                pss = PS[2]
                b3 = psr(2)

                def trfn(e, cs=cs):
                    e.transpose(out=pss[0:C, 0:H], in_=GT[0][0:H, cs], identity=IDENT[0:H, 0:H])
                    return e.transpose(out=pss[0:C, 8:8 + H], in_=GT[1][0:H, cs], identity=IDENT[0:H, 0:H])
                OP("pe", trfn, reads=[GTR[0], GTR[1], "CST"], writes=[b3])
                OP("dve", lambda e: e.tensor_copy(out=SM[0:C, 0:H], in_=pss[0:C, 0:H]), reads=[b3], writes=[rSM])
                OP("dve", lambda e: e.tensor_copy(out=SM[0:C, H:2 * H], in_=pss[0:C, 8:8 + H]), reads=[b3], writes=[rSM])

                def cumfn(e):
                    e.matmul(pss[0:C, 16:16 + H], lhsT=UTRI[0:C, 0:C], rhs=SM[0:C, H:2 * H], start=True, stop=True)
                    return e.matmul(pss[:, 24:24 + H], lhsT=ONES[0:C, :], rhs=SM[0:C, H:2 * H], start=True, stop=True)
                OP("pe", cumfn, reads=[rSM, "CST"], writes=[b3])
                OP("dve", lambda e: e.tensor_copy(out=SM[0:C, 2 * H:3 * H], in_=pss[0:C, 16:16 + H]), reads=[b3], writes=[rSM])
                OP("act", lambda e: e.activation(out=SM[0:C, 3 * H:4 * H], in_=SM[0:C, 2 * H:3 * H], func=AF.Exp), reads=[rSM], writes=[rSM])
                OP("dve", lambda e: e.tensor_tensor(out=SM[0:C, 4 * H:5 * H], in0=pss[0:C, 24:24 + H], in1=SM[0:C, 2 * H:3 * H], op=ALU.subtract),
                     reads=[b3, rSM], writes=[rSM])
                OP("act", lambda e: e.activation(out=SM[0:C, 4 * H:5 * H], in_=SM[0:C, 4 * H:5 * H], func=AF.Exp), reads=[rSM], writes=[rSM])
                OP("dve", lambda e: e.tensor_tensor(out=SM[0:C, 5 * H:6 * H], in0=SM[0:C, 0:H], in1=SM[0:C, 3 * H:4 * H], op=ALU.mult),
                     reads=[rSM], writes=[rSM])
                OP("act", lambda e: e.activation(out=SM[:, 6 * H:7 * H], in_=pss[:, 24:24 + H], func=AF.Exp), reads=[b3], writes=[rSM])
                def do_group(h0, gset):
                    ops = []

                    def OP(eng, fn, reads=(), writes=()):
                        ops.append((eng, fn, _flat(list(reads)), _flat(list(writes))))
                    G, GR = Gsets[gset], GRsets[gset]
                    hs = list(range(h0, h0 + HG))
                    ia, ib, ic = (0, 1, 2) if gset == 0 else (3, 4, 5)
                    Ba, Bb, Bc_ = v3(PS[ia]), v3(PS[ib]), v3(PS[ic])
                    ra, rb_, rc = psr(ia), psr(ib), psr(ic)
                    PBt = PSBs[gset]
                    PTK = PBt[:, 0:HG * 128].rearrange("p (h c) -> p h c", c=128)
                    PTV = PBt[:, 512:512 + HG * 128].rearrange("p (h c) -> p h c", c=128)
                    r7 = psr(6 + gset)
                    Gv = [v3(g) for g in G]
                    rK = [("AB", H + h) for h in hs]
                    rQ = [("AB", h) for h in hs]
                    rV = [("AB", 2 * H + h) for h in hs]

                    def smc(k):
                        return SM[0:C, k * H + h0:k * H + h0 + HG]

                    def mm_each(fnh):
                        def fn(e):
                            ins = None
                            for gi, h in enumerate(hs):
                                ins = fnh(e, gi, h)
                            return ins
                        return fn
                    W3 = (slice(0, C), slice(0, HG), slice(0, C))
                    WF = (slice(0, C), slice(0, HG), slice(0, 128))
                    WT_ = (slice(0, 128), slice(0, HG), slice(0, C))
                    WS_ = (slice(0, 128), slice(0, HG), slice(0, 128))
                    Dm, DT = Gv[0], Gv[1]
                    OP("pe", mm_each(lambda e, gi, h: e.matmul(Bc_[0:C, gi, 0:C], lhsT=IDENT[0:H, h:h + 1].to_broadcast([H, C]), rhs=GT[2][0:H, cs], start=True, stop=True)),
                       reads=[GTR[2], "CST"], writes=[rc])
                    OP("pe", mm_each(lambda e, gi, h: e.matmul(Ba[0:C, gi, 0:C], lhsT=AB[:, H + h, cs], rhs=AB[:, H + h, cs], start=True, stop=True)),
                       reads=[rK], writes=[ra])
                    OP("pe", mm_each(lambda e, gi, h: e.matmul(Bb[0:C, gi, 0:C], lhsT=AB[:, H + h, cs], rhs=AB[:, h, cs], start=True, stop=True)),
                       reads=[rK, rQ], writes=[rb_])
                    OP("dve", lambda e: e.tensor_tensor(out=Dm[W3], in0=Bc_[W3], in1=bcl(smc(2), C), op=ALU.subtract), reads=[rc, rSM], writes=[GR[0]])
                    OP("dve", lambda e: e.tensor_scalar(out=DT[W3], in0=Dm[W3], scalar1=0.0, scalar2=None, op0=ALU.min), reads=[GR[0]], writes=[GR[1]])
                    OP("dve", lambda e: e.tensor_scalar(out=Dm[W3], in0=Dm[W3], scalar1=0.0, scalar2=None, op0=ALU.max), reads=[GR[0]], writes=[GR[0]])
                    OP("act", lambda e: e.activation(out=Dm[W3], in_=Dm[W3], func=AF.Exp, scale=-1.0), reads=[GR[0]], writes=[GR[0]])
                    OP("act", lambda e: e.activation(out=DT[W3], in_=DT[W3], func=AF.Exp), reads=[GR[1]], writes=[GR[1]])
                    OP("dve", lambda e: e.tensor_tensor(out=Dm[W3], in0=Dm[W3], in1=bcm(MSL[0:C, 0:C], HG), op=ALU.mult), reads=[GR[0], "CST"], writes=[GR[0]])
                    OP("dve", lambda e: e.tensor_tensor(out=DT[W3], in0=DT[W3], in1=bcm(MUI[0:C, 0:C], HG), op=ALU.mult), reads=[GR[1], "CST"], writes=[GR[1]])
                    OP("dve", lambda e: e.tensor_tensor(out=Dm[W3], in0=Dm[W3], in1=bcl(smc(0), C), op=ALU.mult), reads=[GR[0], rSM], writes=[GR[0]])
                    OP("dve", lambda e: e.tensor_tensor(out=DT[W3], in0=Bb[W3], in1=DT[W3], op=ALU.mult), reads=[rb_, GR[1]], writes=[GR[1]])
                    Ac, rAc, An, rAn = Gv[2], GR[2], Gv[3], GR[3]
                    Bc, rBc, Bn, rBn = Gv[4], GR[4], Gv[5], GR[5]
                    Qc, rQc, Qn, rQn = Gv[6], GR[6], Gv[7], GR[7]
                    OP("dve", lambda e, Ac=Ac: e.tensor_tensor(out=Ac[W3], in0=Ba[W3], in1=Dm[W3], op=ALU.mult), reads=[ra, GR[0]], writes=[rAc])
                    OP("pe", mm_each(lambda e, gi, h, Ac=Ac: e.transpose(out=Ba[0:C, gi, 0:C], in_=Ac[0:C, gi, 0:C], identity=IDENT[0:C, 0:C])),
                       reads=[rAc, "CST"], writes=[ra])
                    OP("act", lambda e, Bc=Bc: e.activation(out=Bc[W3], in_=Ba[W3], func=AF.Copy), reads=[ra], writes=[rBc])
                    OP("dve", lambda e, Qc=Qc: e.tensor_tensor(out=Qc[W3], in0=bcm(IDENT[0:C, 0:C], HG), in1=Ba[W3], op=ALU.subtract), reads=[ra, "CST"], writes=[rQc])
                    for jq in range(1, nsq + 1):
                        OP("pe", mm_each(lambda e, gi, h, Bc=Bc, Ac=Ac: e.matmul(Ba[0:C, gi, 0:C], lhsT=Bc[0:C, gi, 0:C], rhs=Ac[0:C, gi, 0:C], start=True, stop=True)),
                           reads=[rBc, rAc], writes=[ra])
                        if jq < nsq:
                            OP("pe", mm_each(lambda e, gi, h, Bc=Bc, Ac=Ac: e.matmul(Bc_[0:C, gi, 0:C], lhsT=Ac[0:C, gi, 0:C], rhs=Bc[0:C, gi, 0:C], start=True, stop=True)),
                               reads=[rBc, rAc], writes=[rc])
                        OP("act", lambda e, An=An: e.activation(out=An[W3], in_=Ba[W3], func=AF.Copy), reads=[ra], writes=[rAn])
                        if jq < nsq:
                            OP("dve", lambda e, Bn=Bn: e.tensor_copy(out=Bn[W3], in_=Bc_[W3]), reads=[rc], writes=[rBn])
                        OP("pe", mm_each(lambda e, gi, h, An=An, Qc=Qc: e.matmul(Bb[0:C, gi, 0:C], lhsT=An[0:C, gi, 0:C], rhs=Qc[0:C, gi, 0:C], start=True, stop=True)),
                           reads=[rAn, rQc], writes=[rb_])
                        OP("dve", lambda e, Qn=Qn, Qc=Qc: e.tensor_tensor(out=Qn[W3], in0=Qc[W3], in1=Bb[W3], op=ALU.add), reads=[rQc, rb_], writes=[rQn])
                        Ac, rAc, An, rAn = An, rAn, Ac, rAc
                        Bc, rBc, Bn, rBn = Bn, rBn, Bc, rBc
                        Qc, rQc, Qn, rQn = Qn, rQn, Qc, rQc
                    RK, rRK, KE, rKE = Gv[2], GR[2], Gv[3], GR[3]
                    Ut, rUt, WK, rWK = Gv[4], GR[4], Gv[5], GR[5]
                    QD, rQD = Qn, rQn
                    VB, rVB = Gv[0], GR[0]
                    Wt, rWt = Gv[8], GR[8]
                    QK, rQK = DT, GR[1]

                    def tkv(e):
                        ins = None
                        for gi, h in enumerate(hs):
                            e.transpose(out=PTK[0:C, gi, :], in_=AB[:, H + h, cs], identity=IDB[:])
                            ins = e.transpose(out=PTV[0:C, gi, :], in_=AB[:, 2 * H + h, cs], identity=IDB[:])
                        return ins
                    OP("pe", tkv, reads=[rK, rV, "IDB"], writes=[r7])
                    OP("dve", lambda e: e.tensor_tensor(out=RK[WF], in0=PTK[WF], in1=bcl(smc(5), 128), op=ALU.mult), reads=[r7, rSM], writes=[rRK])
                    OP("dve", lambda e: e.tensor_tensor(out=KE[WF], in0=PTK[WF], in1=bcl(smc(4), 128), op=ALU.mult), reads=[r7, rSM], writes=[rKE])
                    OP("dve", lambda e: e.tensor_tensor(out=VB[WF], in0=PTV[WF], in1=bcl(smc(0), 128), op=ALU.mult), reads=[r7, rSM], writes=[rVB])
                    OP("pe", mm_each(lambda e, gi, h: e.matmul(Ba[0:C, gi, :], lhsT=Qc[0:C, gi, 0:C], rhs=VB[0:C, gi, :], start=True, stop=True)),
                       reads=[rQc, rVB], writes=[ra])
                    OP("act", lambda e: e.activation(out=Ut[WF], in_=Ba[WF], func=AF.Copy), reads=[ra], writes=[rUt])
                    OP("pe", mm_each(lambda e, gi, h: e.matmul(Bc_[:, gi, 0:C], lhsT=RK[0:C, gi, :], rhs=Qc[0:C, gi, 0:C], start=True, stop=True)),
                       reads=[rRK, rQc], writes=[rc])
                    OP("act", lambda e: e.activation(out=WK[WT_], in_=Bc_[WT_], func=AF.Copy), reads=[rc], writes=[rWK])
                    OP("pe", mm_each(lambda e, gi, h: e.matmul(Bb[:, gi, 0:C], lhsT=IDENT[0:H, h:h + 1].to_broadcast([H, 128]), rhs=GT[3][0:H, cs], start=True, stop=True)),
                       reads=[GTR[3], "CST"], writes=[rb_])
                    OP("dve", lambda e: e.tensor_tensor(out=QD[WT_], in0=AB[:, h0:h0 + HG, cs], in1=Bb[WT_], op=ALU.mult), reads=[rQ, rb_], writes=[rQD])
                    rS = [(Sres, h) for h in hs]
                    Sg = Sb[:, h0:h0 + HG, :]
                    OP("pe", mm_each(lambda e, gi, h: e.matmul(Ba[0:C, gi, :], lhsT=WK[:, gi, 0:C], rhs=Sb[:, h, :], start=True, stop=True)),
                       reads=[rWK, rS], writes=[ra])
                    OP("dve", lambda e: e.tensor_tensor(out=Wt[WF], in0=Ut[WF], in1=Ba[WF], op=ALU.subtract), reads=[rUt, ra], writes=[rWt])

                    def ofn(e):
                        ins = None
                        for gi, h in enumerate(hs):
                            e.matmul(Bc_[:, gi, 0:C], lhsT=Sb[:, h, :], rhs=QD[:, gi, 0:C], start=True, stop=False)
                            ins = e.matmul(Bc_[:, gi, 0:C], lhsT=Wt[0:C, gi, :], rhs=QK[0:C, gi, 0:C], start=False, stop=True)
                        return ins
                    OP("pe", ofn, reads=[rS, rQD, rWt, rQK], writes=[rc])
                    OP("act", lambda e: e.activation(out=OO[:, h0:h0 + HG, cs], in_=Bc_[WT_], func=AF.Copy), reads=[rc],
                       writes=[("OO", h, si, ci) for h in hs])
                    OP("pe", mm_each(lambda e, gi, h: e.matmul(Bb[:, gi, :], lhsT=KE[0:C, gi, :], rhs=Wt[0:C, gi, :], start=True, stop=True)),
                       reads=[rKE, rWt], writes=[rb_])
                    OP("dve", lambda e: e.tensor_tensor(out=Sg, in0=Sg, in1=bcl(SM[:, 6 * H + h0:6 * H + h0 + HG], 128), op=ALU.mult), reads=[rS, rSM], writes=[rS])
                    OP("dve", lambda e: e.tensor_tensor(out=Sg, in0=Sg, in1=Bb[WS_], op=ALU.add), reads=[rS, rb_], writes=[rS])
                    return ops
                return [pre + do_group(0, 0)] + [do_group(h0, gi) for gi, h0 in list(enumerate(range(0, H, HG)))[1:]], len(pre)

            allops = []
            cidx = 0
            for si_, slot_ in enumerate(segs):
                for ci_ in range(nch):
                    lists, npre = do_chunk(si_, slot_, ci_, cidx)
                    period = len(lists[0])
                    for gi_, lst in enumerate(lists):
                        off = 0 if gi_ == 0 else npre + 16
                        for k_, op_ in enumerate(lst):
                            allops.append((cidx * period + off + k_, gi_, op_))
                    cidx += 1
            allops.sort(key=lambda t: (t[0], t[1]))
            for _, _, (eng_, fn_, rd_, wr_) in allops:
                P.op(eng_, fn_, reads=rd_, writes=wr_)
            oo_res = lambda h: [("OO", h, si, ci) for si in range(nseg) for ci in range(nch)]

            for c in range(KL):
                wt, wres, d = next_unit("ga")
                po = PS[4 + c % 2]
                pres = psr(4 + c % 2)
                proj_group(po[:, 0:nt], pres, wt, wres, d[-1], HB, "HB", nt)
                G1, G2 = TMP[0], TMP[1]
                P.op("act", lambda e, po=po: e.activation(out=G1[:, 0:nt], in_=po[:, 0:nt], func=AF.Square), reads=[pres], writes=[tm(0)])
                P.op("dve", lambda e: e.tensor_scalar(out=G1[:, 0:nt], in0=G1[:, 0:nt], scalar1=0.044715, scalar2=1.0, op0=ALU.mult, op1=ALU.add),
                     reads=[tm(0)], writes=[tm(0)])
                P.op("dve", lambda e, po=po: e.tensor_tensor(out=G1[:, 0:nt], in0=G1[:, 0:nt], in1=po[:, 0:nt], op=ALU.mult), reads=[tm(0), pres], writes=[tm(0)])
                P.op("act", lambda e: e.activation(out=G1[:, 0:nt], in_=G1[:, 0:nt], func=AF.Sigmoid, scale=1.5957691216057308), reads=[tm(0)], writes=[tm(0)])
                P.op("dve", lambda e, po=po: e.tensor_tensor(out=G1[:, 0:nt], in0=G1[:, 0:nt], in1=po[:, 0:nt], op=ALU.mult), reads=[tm(0), pres], writes=[tm(0)])
                P.op("dve", lambda e, c=c: e.scalar_tensor_tensor(out=G2[:, 0:nt], in0=HL[:, c, 0:nt], scalar=pc(("norm_a", l), c), in1=RSA[:, 0:nt],
                                                               op0=ALU.mult, op1=ALU.mult),
                     reads=[("HL", c, si) for si in range(nseg)] + [tm(6), "PRM"], writes=[tm(1)])
                P.op("dve", lambda e, c=c: e.tensor_tensor(out=AB[:, NQ + c, 0:nt], in0=G1[:, 0:nt], in1=G2[:, 0:nt], op=ALU.mult),
                     reads=[tm(0), tm(1)], writes=[("AB", NQ + c)])
            for h in range(H):
                Z1, Z2 = TMP[0], TMP[1]
                P.op("act", lambda e, h=h: e.activation(out=SQF[:, 0:nt], in_=OO[:, h, 0:nt], func=AF.Square), reads=oo_res(h), writes=["SQF"])
                P.op("pe", lambda e: e.matmul(PS[3][:, 0:nt], lhsT=ONES, rhs=SQF[:, 0:nt], start=True, stop=True), reads=["SQF", "CST"], writes=[psr(3)])
                wt, wres, d = next_unit("z")
                po = PS[4 + h % 2]
                pres = psr(4 + h % 2)
                proj_group(po[:, 0:nt], pres, wt, wres, d[-1], HB, "HB", nt)
                P.op("act", lambda e: e.activation(out=Z2[:, 0:nt], in_=PS[3][:, 0:nt], func=AF.Ln, scale=1.0 / 128.0, bias=EPSC[:, 0:1]),
                     reads=[psr(3), "EPSC"], writes=[tm(1)])
                P.op("act", lambda e: e.activation(out=Z2[:, 0:nt], in_=Z2[:, 0:nt], func=AF.Exp, scale=-0.5), reads=[tm(1)], writes=[tm(1)])
                P.op("dve", lambda e, h=h: e.scalar_tensor_tensor(out=Z2[:, 0:nt], in0=OO[:, h, 0:nt], scalar=pc(("norm_b", l), 0), in1=Z2[:, 0:nt],
                                                               op0=ALU.mult, op1=ALU.mult),
                     reads=oo_res(h) + [tm(1), "PRM"], writes=[tm(1)])
                P.op("act", lambda e, po=po: e.activation(out=Z1[:, 0:nt], in_=po[:, 0:nt], func=AF.Exp, scale=-1.0), reads=[pres], writes=[tm(0)])
                P.op("act", lambda e: e.activation(out=Z1[:, 0:nt], in_=Z1[:, 0:nt], func=AF.Ln, bias=ONEC[:, 0:1]), reads=[tm(0), "EPSC"], writes=[tm(0)])
                P.op("act", lambda e: e.activation(out=Z1[:, 0:nt], in_=Z1[:, 0:nt], func=AF.Exp, scale=-1.0), reads=[tm(0)], writes=[tm(0)])
                P.op("dve", lambda e, po=po: e.tensor_tensor(out=Z1[:, 0:nt], in0=Z1[:, 0:nt], in1=po[:, 0:nt], op=ALU.mult), reads=[tm(0), pres], writes=[tm(0)])
                P.op("dve", lambda e, h=h: e.tensor_tensor(out=AB[:, NQ + KL + h, 0:nt], in0=Z1[:, 0:nt], in1=Z2[:, 0:nt], op=ALU.mult),
                     reads=[tm(0), tm(1)], writes=[("AB", NQ + KL + h)])
            P.dma("sp", "c_xl", lambda e: e.dma_start(out=X[:, :, 0:nt], in_=xsp[:, :, 0:nt].rearrange("k p t -> p k t")),
                  reads=["xsp"], writes=[("X", k) for k in range(KD)])
            for m in range(KD):
                wt, wres, d = next_unit("wout")
                pd = PS[4 + m % 2]
                pres = psr(4 + m % 2)

                def fn(e, wt=wt, pd=pd):
                    ins = None
                    for k in range(KD):
                        ins = e.matmul(pd[:, 0:nt], lhsT=wt[:, k, :], rhs=AB[:, NQ + k, 0:nt], start=(k == 0), stop=(k == KD - 1))
                    return ins
                P.op("pe", fn, reads=[wres] + [("AB", NQ + k) for k in range(KD)], writes=[pres])
                P.op("dve", lambda e, m=m, pd=pd: e.tensor_tensor(out=X[:, m, 0:nt], in0=X[:, m, 0:nt], in1=pd[:, 0:nt], op=ALU.add),
                     reads=[pres, ("X", m)], writes=[("X", m)])
            if last:
                for slot in range(3):
                    hb, hres = hist_a(l, slot)
                    P.dma("sp", "c_o0_%d_%d" % (l, slot), lambda e, hb=hb, slot=slot: e.dma_start(out=o_ca[l, slot], in_=hb[:, :, :]), reads=[hres], writes=[("o_ca", l, slot)])
                    hb, hres = lru_h(l, slot)
                    P.dma("sp", "c_o1_%d_%d" % (l, slot), lambda e, hb=hb, slot=slot: e.dma_start(out=o_lru[l, slot], in_=hb[:, :]), reads=[hres], writes=[("o_lru", l, slot)])
                    hb, hres = hist_b(l, slot)
                    P.dma("sp", "c_o2_%d_%d" % (l, slot), lambda e, hb=hb, slot=slot: e.dma_start(out=o_cb[l, slot], in_=hb[:, :, :]), reads=[hres], writes=[("o_cb", l, slot)])
                    sbuf_, sres = dstate(l, slot)
                    P.dma("sp", "c_o3_%d_%d" % (l, slot), lambda e, sbuf_=sbuf_, slot=slot: e.dma_start(out=o_dl[l, slot], in_=sbuf_[:, :, :]),
                          reads=[(sres, h) for h in range(H)], writes=[("o_dl", l, slot)])


        tiles = []
        for i in range(cfg.NBIG):
            tiles.append(("big", i * T, T, [0], T, False))
        tiles.append(("small", cfg.NBIG * T, 48, [0, 1, 2], 16, True))
        for kind, t0, nt, segs, L, last in tiles:
            if kind == "big":
                P.dma("sp", "c_x", lambda e, t0=t0, nt=nt: e.dma_start(out=X[:, :, 0:nt], in_=xp[:, :, t0:t0 + nt].rearrange("k p t -> p k t")),
                      writes=[("X", k) for k in range(KD)])
            else:
                P.dma("sp", "c_x", lambda e, t0=t0: e.dma_start(out=X[:, :, 0:16], in_=xp[:, :, t0:t0 + 16].rearrange("k p t -> p k t")),
                      writes=[("X", k) for k in range(KD)])
                P.dma("sp", "c_x2", lambda e: e.dma_start(out=X[:, :, 16:48], in_=xs.rearrange("k p t -> p k t")),
                      writes=[("X", k) for k in range(KD)])
            for l in range(DEPTH):
                ffn(l, 1, nt)
                mixer(l, nt, segs, L, last)
                ffn(l, 2, nt)
            ydst = [(HL[:, k, :], [("HL", k, si) for si in range(3)]) for k in range(KL)] + \
                   [(OO[:, h, :], [("OO", h, si, ci) for si in range(3) for ci in range(4)]) for h in range(H)]
            rmsnorm(nt, ("final_norm",), None, None, dst_list=ydst)
            rdA = [r for k in range(KL) for r in ydst[k][1]]
            rdB = [r for k in range(KL, KD) for r in ydst[k][1]]
            if kind == "big":
                P.dma("sp", "c_y", lambda e, t0=t0, nt=nt: e.dma_start(out=yp[0:KL, :, t0:t0 + nt].rearrange("k p t -> p k t"), in_=HL[:, :, 0:nt]),
                      reads=rdA, writes=[("yp", t0, 0)])
                P.dma("sp", "c_yb", lambda e, t0=t0, nt=nt: e.dma_start(out=yp[KL:KD, :, t0:t0 + nt].rearrange("k p t -> p k t"), in_=OO[:, :, 0:nt]),
                      reads=rdB, writes=[("yp", t0, 1)])
            else:
                P.dma("sp", "c_y", lambda e, t0=t0: e.dma_start(out=yp[0:KL, :, t0:t0 + 16].rearrange("k p t -> p k t"), in_=HL[:, :, 0:16]),
                      reads=rdA, writes=[("yp", t0, 0)])
                P.dma("sp", "c_yb", lambda e, t0=t0: e.dma_start(out=yp[KL:KD, :, t0:t0 + 16].rearrange("k p t -> p k t"), in_=OO[:, :, 0:16]),
                      reads=rdB, writes=[("yp", t0, 1)])
                P.dma("sp", "c_y2", lambda e: e.dma_start(out=ys[0:KL].rearrange("k p t -> p k t"), in_=HL[:, :, 16:48]),
                      reads=rdA, writes=[("ys", 0)])
                P.dma("sp", "c_y2b", lambda e: e.dma_start(out=ys[KL:KD].rearrange("k p t -> p k t"), in_=OO[:, :, 16:48]),
                      reads=rdB, writes=[("ys", 1)])
        assert wstate["gu"] == cfg.NU * len(tiles)
        P.emit(st)
    return nc


def _blk(w, ks, cols, UW):
    out = np.zeros((128, UW), np.float32)
    for i, k in enumerate(ks):
        blk = w[k * 128:(k + 1) * 128, cols]
        out[:, i * 128:i * 128 + blk.shape[1]] = blk
    return out


def prepare(cfg, inp):
    f32 = np.float32
    D, KD, KF, KL, H, NQ, DEPTH, LW = cfg.D, cfg.KD, cfg.KF, cfg.KL, cfg.H, cfg.NQ, cfg.DEPTH, cfg.LW
    g = {k: np.asarray(v, f32) for k, v in inp.items()}
    ws = np.zeros((cfg.NU, 128, cfg.UW), f32)
    o2 = 2 * LW
    o3 = o2 + NQ * 128
    o4 = o3 + H * 128
    for u, d in enumerate(cfg.units):
        kind = d[0]
        if kind in ("gate", "up", "down"):
            _, l, which, idx, ks = d
            wsel = {("gate", 1): g["ffn1_w_gate"], ("up", 1): g["ffn1_w_up"], ("down", 1): g["ffn1_w_down"],
                    ("gate", 2): g["ffn2_w_gate"], ("up", 2): g["ffn2_w_up"], ("down", 2): g["ffn2_w_down"]}[(kind, which)]
            ws[u] = _blk(wsel[l], ks, slice(idx * 128, (idx + 1) * 128), cfg.UW)
        else:
            _, l, idx, ks = d
            if kind == "xa":
                ws[u] = _blk(g["w_in"][l], ks, slice(idx * 128, (idx + 1) * 128), cfg.UW)
            elif kind == "ga":
                ws[u] = _blk(g["w_in"][l], ks, slice(LW + idx * 128, LW + (idx + 1) * 128), cfg.UW)
            elif kind == "qkv":
                ws[u] = _blk(g["w_in"][l], ks, slice(o2 + idx * 128, o2 + (idx + 1) * 128), cfg.UW)
            elif kind == "z":
                ws[u] = _blk(g["w_in"][l], ks, slice(o3 + idx * 128, o3 + (idx + 1) * 128), cfg.UW)
            elif kind == "tail":
                ws[u] = _blk(g["w_in"][l], ks, slice(o4, o4 + 2 * H), cfg.UW)
            elif kind == "wout":
                ws[u] = _blk(g["w_out"][l], ks, slice(idx * 128, (idx + 1) * 128), cfg.UW)
    prm = np.zeros((128, cfg.NP), f32)

    def put(name, arr):
        off, w = cfg.pcol[name]
        prm[:arr.shape[0], off:off + w] = arr

    def pk(v):
        return v.reshape(-1, 128).T
    for l in range(DEPTH):
        put(("ffn1_norm", l), pk(g["ffn1_norm"][l]))
        put(("mix_norm", l), pk(g["mix_norm"][l]))
        put(("ffn2_norm", l), pk(g["ffn2_norm"][l]))
        put(("conv_a_w", l), g["conv_a_w"][l].reshape(4, KL, 128).transpose(2, 1, 0).reshape(128, KL * 4))
        put(("conv_a_b", l), pk(g["conv_a_b"][l]))
        put(("rg_b", l), pk(g["rg_b"][l]))
        put(("ig_b", l), pk(g["ig_b"][l]))
        put(("lam", l), pk(g["lru_lambda"][l]))
        put(("norm_a", l), pk(g["norm_a"][l]))
        put(("conv_b_w", l), g["conv_b_w"][l].reshape(4, NQ, 128).transpose(2, 1, 0).reshape(128, NQ * 4))
        put(("norm_b", l), g["norm_b"][l].reshape(128, 1))
        put(("a_log", l), g["a_log"][l].reshape(H, 1))
        put(("dt_bias", l), g["dt_bias"][l].reshape(H, 1))
    put(("final_norm",), pk(g["final_norm"]))
    gw = np.zeros((DEPTH, 2, 128, KL, 128), f32)
    for l in range(DEPTH):
        for gi, name in enumerate(("rg_w", "ig_w")):
            w = g[name][l]
            for c in range(KL):
                gw[l, gi, 0:64, c, 0:64] = w[2 * c]
                gw[l, gi, 64:128, c, 64:128] = w[2 * c + 1]
    cst = np.zeros((128, 6, 128), f32)
    ii = np.arange(128)
    cst[:, 0, :] = np.eye(128)
    cst[:, 1, :] = (ii[:, None] > ii[None, :])
    cst[:, 2, :] = (ii[None, :] >= ii[:, None])
    cst[:, 3, :] = (ii[:, None] <= ii[None, :])
    cst[:, 4, :] = 1.0
    shared = {"wstream": ws, "prm": prm, "gw": gw, "cst": cst}
    in_maps = []
    for c in range(cfg.NCORES):
        m = dict(shared)
        if c < cfg.BATCH:
            stream = np.concatenate([g["meta_tokens"], g["x_prompt"][c]], axis=0)
            m["xp"] = np.ascontiguousarray(stream.T.reshape(KD, 128, cfg.NTOK))
        else:
            m["xp"] = np.zeros((KD, 128, cfg.NTOK), f32)
        xsm = g["x_sample"][2 * c:2 * c + 2].reshape(32, D)
        m["xs"] = np.ascontiguousarray(xsm.T.reshape(KD, 128, 32))
        sl = slice(2 * c, 2 * c + 2)
        m["sca"] = np.ascontiguousarray(g["state_conv_a"][:, sl].reshape(DEPTH, 2, 3, KL, 128).transpose(0, 4, 1, 3, 2))
        m["slru"] = np.ascontiguousarray(g["state_lru"][:, sl].reshape(DEPTH, 2, KL, 128).transpose(0, 3, 1, 2))
        m["scb"] = np.ascontiguousarray(g["state_conv_b"][:, sl].reshape(DEPTH, 2, 3, NQ, 128).transpose(0, 4, 1, 3, 2))
        m["sdl"] = np.ascontiguousarray(g["state_delta"][:, sl].transpose(0, 1, 3, 2, 4))
        in_maps.append(m)
    return in_maps


def assemble(cfg, res):
    f32 = np.float32
    D, KD, KL, H, NQ, DEPTH, LW = cfg.D, cfg.KD, cfg.KL, cfg.H, cfg.NQ, cfg.DEPTH, cfg.LW
    B, DB = cfg.BATCH, cfg.DEC_BATCH
    y_prompt = np.zeros((B, cfg.SEQ, D), f32)
    y_sample = np.zeros((DB, 16, D), f32)
    p_ca = np.zeros((DEPTH, B, 3, LW), f32)
    p_lru = np.zeros((DEPTH, B, LW), f32)
    p_cb = np.zeros((DEPTH, B, 3, NQ * 128), f32)
    p_dl = np.zeros((DEPTH, B, H, 128, 128), f32)
    s_ca = np.zeros((DEPTH, DB, 3, LW), f32)
    s_lru = np.zeros((DEPTH, DB, LW), f32)
    s_cb = np.zeros((DEPTH, DB, 3, NQ * 128), f32)
    s_dl = np.zeros((DEPTH, DB, H, 128, 128), f32)
    for c, r in enumerate(res):
        ypc = np.asarray(r["yp"]).reshape(D, cfg.NTOK).T
        if c < B:
            y_prompt[c] = ypc[cfg.NMETA:]
        ysc = np.asarray(r["ys"]).reshape(D, 32).T.reshape(2, 16, D)
        y_sample[2 * c:2 * c + 2] = ysc
        ca = np.asarray(r["o_ca"]).transpose(0, 1, 4, 3, 2).reshape(DEPTH, 3, 3, LW)
        lr = np.asarray(r["o_lru"]).transpose(0, 1, 3, 2).reshape(DEPTH, 3, LW)
        cb = np.asarray(r["o_cb"]).transpose(0, 1, 4, 3, 2).reshape(DEPTH, 3, 3, NQ * 128)
        dl = np.asarray(r["o_dl"]).transpose(0, 1, 3, 2, 4)
        if c < B:
            p_ca[:, c], p_lru[:, c], p_cb[:, c], p_dl[:, c] = ca[:, 0], lr[:, 0], cb[:, 0], dl[:, 0]
        for s in range(2):
            b = 2 * c + s
            s_ca[:, b], s_lru[:, b], s_cb[:, b], s_dl[:, b] = ca[:, 1 + s], lr[:, 1 + s], cb[:, 1 + s], dl[:, 1 + s]
    return (y_prompt, y_sample, p_ca, p_lru, p_cb, p_dl, s_ca, s_lru, s_cb, s_dl)


def run(cfg, inputs, trace=False):
    nc = build_program(cfg)
    in_maps = prepare(cfg, inputs)
    res = run_bass_kernel_spmd(nc, in_maps, core_ids=list(range(cfg.NCORES)), trace=trace)
    return assemble(cfg, res.results), res


def kernel(**inputs):
    cfg = Cfg()
    out, _ = run(cfg, inputs)
    return out
```

```python
import contextlib
import numpy as np
import concourse.bass as bass
import concourse.mybir as mybir
from concourse.bass_utils import run_bass_kernel_spmd
from concourse.ap import AP as APc

F32 = mybir.dt.float32
BF16 = mybir.dt.bfloat16
ALU = mybir.AluOpType
AF = mybir.ActivationFunctionType

ENGS = ("pe", "dve", "act", "pool", "sp")
EPS = 1e-6


def _flat(xs):
    out = []
    for x in xs:
        if isinstance(x, list):
            out.extend(_flat(x))
        else:
            out.append(x)
    return out


def tm(k):
    return [("ts", k, i) for i in range(4)]


def psr(b):
    return [("bank", b)]


class Prog:
    def __init__(self, nc):
        self.nc = nc
        self.ops = {e: [] for e in ENGS}
        self.cnt = {e: 0 for e in ("pe", "dve", "act", "pool")}
        self.clock = {e: {} for e in ENGS}
        self.vc = {}
        self.last_w = {}
        self.readers = {}
        self.chans = []

    def _need(self, eng, deps):
        waits = {}
        ck = self.clock[eng]
        for (tl, c) in deps:
            if ck.get(tl, 0) >= c:
                continue
            if waits.get(tl, 0) < c:
                waits[tl] = c
        for tl, c in waits.items():
            snap = self.vc.get((tl, c))
            if snap:
                for k, v in snap.items():
                    if ck.get(k, 0) < v:
                        ck[k] = v
            if ck.get(tl, 0) < c:
                ck[tl] = c
        return sorted(waits.items())

    def _deps(self, reads, writes):
        deps = []
        for r in reads:
            lw = self.last_w.get(r)
            if lw:
                deps.append(lw)
        for w in writes:
            lw = self.last_w.get(w)
            if lw:
                deps.append(lw)
            for tl, c in self.readers.get(w, {}).items():
                deps.append((tl, c))
        return deps

    def _commit(self, tl, c, reads, writes):
        for r in reads:
            self.readers.setdefault(r, {})[tl] = c
        for w in writes:
            self.last_w[w] = (tl, c)
            self.readers[w] = {}

    def op(self, eng, fn, reads=(), writes=()):
        reads, writes = _flat(reads), _flat(writes)
        writes = writes + [r for r in reads if isinstance(r, tuple) and r[0] == "bank"]
        deps = self._deps(reads, writes)
        if eng == "pe":
            deps = [d for d in deps if d[0] != "pe"]
        waits = self._need(eng, deps)
        self.cnt[eng] += 1
        c = self.cnt[eng]
        if eng == "pe":
            self.clock[eng][eng] = c
        snap = dict(self.clock[eng])
        snap[eng] = c
        self.vc[(eng, c)] = snap
        self.ops[eng].append((waits, fn, (eng, 1)))
        self._commit(eng, c, reads, writes)

    def dma(self, queue, chan, fn, reads=(), writes=()):
        if chan not in self.cnt:
            self.cnt[chan] = 0
            self.chans.append(chan)
        reads, writes = _flat(reads), _flat(writes)
        deps = self._deps(reads, writes)
        waits = self._need(queue, deps)
        self.cnt[chan] += 16
        c = self.cnt[chan]
        snap = dict(self.clock[queue])
        snap[chan] = c
        self.vc[(chan, c)] = snap
        self.ops[queue].append((waits, fn, (chan, 16)))
        self._commit(chan, c, reads, writes)

    def emit(self, st):
        nc = self.nc
        names = ["pe", "dve", "act", "pool"] + self.chans
        final = [(tl, self.cnt[tl]) for tl in names if self.cnt.get(tl, 0) > 0]
        sems = {}
        for i, n in enumerate(names):
            sems[n] = st.enter_context(nc.semaphore("s%d" % i))
        block = st.enter_context(nc.Block())
        handles = {"pe": block.tensor, "dve": block.vector, "act": block.scalar,
                   "pool": block.gpsimd, "sp": block.sync}

        def make(engname):
            oplist = self.ops[engname]

            def body(e):
                for waits, fn, inc in oplist:
                    for tl, c in waits:
                        e.wait_ge(sems[tl], c)
                    ins = fn(e)
                    ins.then_inc(sems[inc[0]], inc[1])
                if engname == "sp":
                    for tl, c in final:
                        e.wait_ge(sems[tl], c)
            return body

        for engname in ENGS:
            handles[engname](make(engname))


class Cfg:
    def __init__(self, D=2048, DFF=5632, SEQ=8192, BATCH=2, DEC_BATCH=16, DEPTH=2, NCORES=8):
        self.D, self.DFF, self.SEQ, self.BATCH, self.DEC_BATCH, self.DEPTH = D, DFF, SEQ, BATCH, DEC_BATCH, DEPTH
        self.NCORES = NCORES
        self.NMETA = 16
        self.DEC_SEQ = 16
        self.LW = D // 2
        self.H = (D - self.LW) // 128
        self.KD, self.KF, self.KL = D // 128, DFF // 128, self.LW // 128
        self.NQ = 3 * self.H
        self.N_IN = 2 * self.LW + self.NQ * 128 + self.H * 128 + 2 * self.H
        self.T = 512
        self.NTOK = self.NMETA + SEQ
        assert SEQ % self.T == 0 and DEC_BATCH == 2 * NCORES and BATCH <= NCORES
        self.NBIG = SEQ // self.T
        self.UW = 16 * 128
        self.units = []
        for l in range(DEPTH):
            self.units += self._ffn_units(l, 1)
            for c in range(self.KL):
                self.units.append(("xa", l, c, list(range(self.KD))))
            for j in range(self.NQ):
                self.units.append(("qkv", l, j, list(range(self.KD))))
            self.units.append(("tail", l, 0, list(range(self.KD))))
            for c in range(self.KL):
                self.units.append(("ga", l, c, list(range(self.KD))))
            for h in range(self.H):
                self.units.append(("z", l, h, list(range(self.KD))))
            for m in range(self.KD):
                self.units.append(("wout", l, m, list(range(self.KD))))
            self.units += self._ffn_units(l, 2)
        self.NU = len(self.units)
        self.pcol = {}
        n = 0

        def add(name, w):
            nonlocal n
            self.pcol[name] = (n, w)
            n += w
        for l in range(DEPTH):
            add(("ffn1_norm", l), self.KD)
            add(("mix_norm", l), self.KD)
            add(("ffn2_norm", l), self.KD)
            add(("conv_a_w", l), self.KL * 4)
            add(("conv_a_b", l), self.KL)
            add(("rg_b", l), self.KL)
            add(("ig_b", l), self.KL)
            add(("lam", l), self.KL)
            add(("norm_a", l), self.KL)
            add(("conv_b_w", l), self.NQ * 4)
            add(("norm_b", l), 1)
            add(("a_log", l), 1)
            add(("dt_bias", l), 1)
        add(("final_norm",), self.KD)
        self.NP = n

    def _ffn_units(self, l, which):
        us = []
        for f in range(self.KF):
            us.append(("gate", l, which, f, list(range(self.KD))))
            us.append(("up", l, which, f, list(range(self.KD))))
        for m in range(self.KD):
            ks = list(range(self.KF))
            for i in range(0, self.KF, 16):
                us.append(("down", l, which, m, ks[i:i + 16]))
        return us


def build_program(cfg):
    nc = bass.Bass("TRN2", target_bir_lowering=False)
    D, KD, KF, KL, H, NQ, T, DEPTH = cfg.D, cfg.KD, cfg.KF, cfg.KL, cfg.H, cfg.NQ, cfg.T, cfg.DEPTH
    NTOK, NP = cfg.NTOK, cfg.NP

    def din(name, shape):
        return nc.dram_tensor(name, list(shape), F32, kind="ExternalInput").ap()

    def dout(name, shape):
        return nc.dram_tensor(name, list(shape), F32, kind="ExternalOutput").ap()

    xp = din("xp", [KD, 128, NTOK])
    xs = din("xs", [KD, 128, 32])
    sca = din("sca", [DEPTH, 128, 2, KL, 3])
    slru = din("slru", [DEPTH, 128, 2, KL])
    scb = din("scb", [DEPTH, 128, 2, NQ, 3])
    sdl = din("sdl", [DEPTH, 2, 128, H, 128])
    wstream = din("wstream", [cfg.NU, 128, cfg.UW])
    prm_d = din("prm", [128, NP])
    gw_d = din("gw", [DEPTH, 2, 128, KL, 128])
    cst_d = din("cst", [128, 6, 128])
    xsp = nc.dram_tensor("xsp", [KD, 128, T], F32, kind="Internal").ap()
    yp = dout("yp", [KD, 128, NTOK])
    ys = dout("ys", [KD, 128, 32])
    o_ca = dout("o_ca", [DEPTH, 3, 128, KL, 3])
    o_lru = dout("o_lru", [DEPTH, 3, 128, KL])
    o_cb = dout("o_cb", [DEPTH, 3, 128, NQ, 3])
    o_dl = dout("o_dl", [DEPTH, 3, 128, H, 128])

    P = Prog(nc)
    st = contextlib.ExitStack()
    with st:
        def sb(name, shape, dt=F32):
            return st.enter_context(nc.sbuf_tensor(name, list(shape), dt))

        X = sb("X", [128, KD, T])
        HB = sb("HB", [128, KD, T], BF16)
        NAB = max(KF, NQ + KD)
        AB = sb("AB", [128, NAB, T], BF16)
        NSLOT = 4
        WB = [sb("WB%d" % i, [128, 16, 128], BF16) for i in range(NSLOT)]
        HL = sb("HL", [128, KL, T])
        OO = sb("OO", [128, H, T])
        RAW = sb("RAW", [128, 520])
        NTMP = 16
        TMP = [sb("TMP%d" % i, [128, T]) for i in range(NTMP)]
        BT = sb("BT", [128, 3, T], BF16)
        SQF = sb("SQF", [128, T])
        SP_ = [sb("SP%d" % l, [128, H, 128]) for l in range(DEPTH)]
        HAP = [sb("HAP%d" % l, [128, KL, 3]) for l in range(DEPTH)]
        HBP = [sb("HBP%d" % l, [128, NQ, 3]) for l in range(DEPTH)]
        LHP = [sb("LHP%d" % l, [128, KL]) for l in range(DEPTH)]
        HAS = sb("HAS", [128, 2, KL, 3])
        HBS = sb("HBS", [128, 2, NQ, 3])
        LHS = sb("LHS", [128, 2, KL])
        PRM = sb("PRM", [128, NP])
        DER = sb("DER", [128, DEPTH, KL + 1])
        GW = sb("GW", [128, DEPTH * 2 * KL, 128], BF16)
        CST = sb("CST", [128, 6, 128])
        IDB = sb("IDB", [128, 128], BF16)
        ONB = sb("ONB", [128, 128], BF16)

        PS = [st.enter_context(nc.psum_tensor("PS%d" % i, [128, 512], F32)) for i in range(6)]
        PSBs = [st.enter_context(nc.psum_tensor("PSB%d" % i, [128, 1024], BF16)) for i in range(2)]

        IDENT, MSL, MUI, UTRI, ONES = (CST[:, i, :] for i in range(5))

        def pc(name, j=0, rows=128):
            off, w = cfg.pcol[name]
            return PRM[0:rows, off + j:off + j + 1]

        EPSC = sb("EPSC", [128, 1])
        ONEC = sb("ONEC", [128, 1])
        P.op("dve", lambda e: e.memset(EPSC[:], EPS), writes=["EPSC"])
        P.op("dve", lambda e: e.memset(ONEC[:], 1.0), writes=["EPSC"])
        P.dma("sp", "c_prm", lambda e: e.dma_start(out=PRM[:], in_=prm_d[:, :]), writes=["PRM"])
        P.dma("sp", "c_cst", lambda e: e.dma_start(out=CST[:], in_=cst_d[:, :, :]), writes=["CST"])
        P.dma("pool", "c_gw", lambda e: e.dma_start(
            out=GW[:].rearrange("p (a k) j -> p a k j", k=KL),
            in_=gw_d.rearrange("l g p k j -> p (l g) k j")), writes=["GW"])
        P.op("dve", lambda e: e.tensor_copy(out=IDB[:], in_=IDENT), reads=["CST"], writes=["IDB"])
        P.op("dve", lambda e: e.tensor_copy(out=ONB[:], in_=ONES), reads=["CST"], writes=["ONB"])
        for l in range(DEPTH):
            lo, _ = cfg.pcol[("lam", l)]
            P.op("act", lambda e, l=l, lo=lo: e.activation(out=DER[:, l, 0:KL], in_=PRM[:, lo:lo + KL], func=AF.Exp, scale=-1.0),
                 reads=["PRM"], writes=[("DER", l)])
            P.op("act", lambda e, l=l: e.activation(out=DER[:, l, 0:KL], in_=DER[:, l, 0:KL], func=AF.Ln, bias=ONEC[:, 0:1]),
                 reads=[("DER", l), "EPSC"], writes=[("DER", l)])
            P.op("dve", lambda e, l=l: e.tensor_scalar(out=DER[:, l, 0:KL], in0=DER[:, l, 0:KL], scalar1=-8.0, scalar2=None, op0=ALU.mult),
                 reads=[("DER", l)], writes=[("DER", l)])
            ao, _ = cfg.pcol[("a_log", l)]
            P.op("act", lambda e, l=l, ao=ao: e.activation(out=DER[0:H, l, KL:KL + 1], in_=PRM[0:H, ao:ao + 1], func=AF.Exp),
                 reads=["PRM"], writes=[("DERa", l)])
            P.op("dve", lambda e, l=l: e.tensor_scalar(out=DER[0:H, l, KL:KL + 1], in0=DER[0:H, l, KL:KL + 1], scalar1=-1.0, scalar2=None, op0=ALU.mult),
                 reads=[("DERa", l)], writes=[("DERa", l)])
            P.op("dve", lambda e, l=l: e.memset(SP_[l][:], 0.0), writes=[(("S", l, 0), h) for h in range(H)])
            P.op("dve", lambda e, l=l: e.memset(HAP[l][:], 0.0), writes=[("HA", l, 0)])
            P.op("dve", lambda e, l=l: e.memset(HBP[l][:], 0.0), writes=[("HBh", l, 0)])
            P.op("dve", lambda e, l=l: e.memset(LHP[l][:], 0.0), writes=[("LH", l, 0)])

        wstate = {"gu": 0}

        def next_unit(expect_kind):
            gu = wstate["gu"]
            wstate["gu"] += 1
            u = gu % cfg.NU
            desc = cfg.units[u]
            assert desc[0] == expect_kind, (desc, expect_kind)
            nk = len(desc[-1])
            s = gu % NSLOT
            P.dma("pool", "w%d" % s,
                  lambda e, u=u, s=s, nk=nk: e.dma_start(out=WB[s][:, 0:nk, :],
                                                        in_=wstream[u, :, 0:nk * 128].rearrange("p (k j) -> p k j", j=128)),
                  writes=[("WB", s)])
            return WB[s], ("WB", s), desc

        def rmsnorm(nt, wname, dst_bf, dst_res, dst_list=None):
            ps = PS[3]
            for kc in range(KD):
                b = kc % 2
                P.op("act", lambda e, kc=kc, b=b: e.activation(out=BT[:, b, 0:nt], in_=X[:, kc, 0:nt], func=AF.Square),
                     reads=[("X", kc)], writes=[("BT", b)])
                P.op("pe", lambda e, kc=kc, b=b: e.matmul(ps[:, 0:nt], lhsT=ONB[:], rhs=BT[:, b, 0:nt], start=(kc == 0), stop=(kc == KD - 1)),
                     reads=[("BT", b), "ONB"], writes=[psr(3)])
            rs = TMP[12]
            P.op("act", lambda e: e.activation(out=rs[:, 0:nt], in_=ps[:, 0:nt], func=AF.Ln, scale=1.0 / D, bias=EPSC[:, 0:1]),
                 reads=[psr(3), "EPSC"], writes=[tm(12)])
            P.op("act", lambda e: e.activation(out=rs[:, 0:nt], in_=rs[:, 0:nt], func=AF.Exp, scale=-0.5), reads=[tm(12)], writes=[tm(12)])
            for kc in range(KD):
                if dst_list is not None:
                    P.op("dve", lambda e, kc=kc: e.scalar_tensor_tensor(out=dst_list[kc][0][:, 0:nt], in0=X[:, kc, 0:nt], scalar=pc(wname, kc),
                                                                      in1=rs[:, 0:nt], op0=ALU.mult, op1=ALU.mult),
                         reads=[("X", kc), tm(12), "PRM"], writes=[dst_list[kc][1]])
                else:
                    P.op("dve", lambda e, kc=kc: e.scalar_tensor_tensor(out=dst_bf[:, kc, 0:nt], in0=X[:, kc, 0:nt], scalar=pc(wname, kc),
                                                                      in1=rs[:, 0:nt], op0=ALU.mult, op1=ALU.mult),
                         reads=[("X", kc), tm(12), "PRM"], writes=[(dst_res, kc)])

        def proj_group(ps_ap, ps_res, wt, wres, ks, src, src_res, nt, mcols=slice(0, 128), kmap=None):
            def fn(e):
                ins = None
                n = len(ks)
                for i, k in enumerate(ks):
                    ins = e.matmul(ps_ap, lhsT=wt[:, i, mcols], rhs=src[:, k, 0:nt], start=(i == 0), stop=(i == n - 1))
                return ins
            P.op("pe", fn, reads=[wres] + [(src_res, k) for k in ks], writes=[ps_res])

        def ffn(l, which, nt):
            rmsnorm(nt, ("ffn%d_norm" % which, l), HB, "HB")
            for f in range(KF):
                pg, pu = PS[f % 2], PS[2 + f % 2]
                wt, wres, d = next_unit("gate")
                proj_group(pg[:, 0:nt], psr(f % 2), wt, wres, d[-1], HB, "HB", nt)
                wt, wres, d = next_unit("up")
                proj_group(pu[:, 0:nt], psr(2 + f % 2), wt, wres, d[-1], HB, "HB", nt)
                tb = f % 2
                P.op("act", lambda e, pg=pg, tb=tb: e.activation(out=TMP[tb][:, 0:nt], in_=pg[:, 0:nt], func=AF.Silu),
                     reads=[psr(f % 2)], writes=[tm(tb)])
                P.op("dve", lambda e, pu=pu, tb=tb, f=f: e.tensor_tensor(out=AB[:, f, 0:nt], in0=TMP[tb][:, 0:nt], in1=pu[:, 0:nt], op=ALU.mult),
                     reads=[tm(tb), psr(2 + f % 2)], writes=[("AB", f)])
            for m in range(KD):
                pd = PS[4 + m % 2]
                pres = psr(4 + m % 2)
                nun = (KF + 15) // 16
                parts = [next_unit("down") for _ in range(nun)]

                tot = sum(len(p[2][-1]) for p in parts)
                i0 = 0
                for wt, wres, d in parts:
                    def fn(e, wt=wt, d=d, i0=i0, pd=pd, tot=tot):
                        ins = None
                        for j, k in enumerate(d[-1]):
                            ins = e.matmul(pd[:, 0:nt], lhsT=wt[:, j, :], rhs=AB[:, k, 0:nt], start=(i0 + j == 0), stop=(i0 + j == tot - 1))
                        return ins
                    P.op("pe", fn, reads=[wres] + [("AB", k) for k in d[-1]], writes=[pres])
                    i0 += len(d[-1])
                P.op("dve", lambda e, m=m, pd=pd: e.scalar_tensor_tensor(out=X[:, m, 0:nt], in0=pd[:, 0:nt], scalar=0.5, in1=X[:, m, 0:nt],
                                                                       op0=ALU.mult, op1=ALU.add),
                     reads=[pres, ("X", m)], writes=[("X", m)])

        def hist_a(l, slot):
            return (HAP[l], ("HA", l, 0)) if slot == 0 else (HAS[:, slot - 1], ("HAS", slot))

        def hist_b(l, slot):
            return (HBP[l], ("HBh", l, 0)) if slot == 0 else (HBS[:, slot - 1], ("HBS", slot))

        def lru_h(l, slot):
            return (LHP[l], ("LH", l, 0)) if slot == 0 else (LHS[:, slot - 1], ("LHS", slot))

        def dstate(l, slot):
            return (SP_[l], ("S", l, 0)) if slot == 0 else (HL[:, :, 64 + 128 * (slot - 1):64 + 128 * slot], ("SS", slot))

        def conv_chunk(ps, pres, nt, segs, L, hist_fn, l, ch, wname, bias_name, out_t, out_res):
            nseg = len(segs)
            Le = L + 3
            rawv = RAW[:, 0:nseg * Le].rearrange("p (s l) -> p s l", l=Le)
            P.op("act", lambda e: e.activation(out=rawv[:, :, 3:Le], in_=ps[:, 0:nt].rearrange("p (s l) -> p s l", l=L), func=AF.Copy),
                 reads=[pres], writes=["RAWd"])
            for si, slot in enumerate(segs):
                hb, hres = hist_fn(l, slot)
                P.op("dve", lambda e, si=si, hb=hb: e.tensor_copy(out=rawv[:, si, 0:3], in_=hb[:, ch, :]),
                     reads=[hres], writes=[("RAWh", si)])
                P.op("dve", lambda e, si=si, hb=hb: e.tensor_copy(out=hb[:, ch, :], in_=rawv[:, si, L:Le]),
                     reads=["RAWd", ("RAWh", si)], writes=[hres])
            woff, _ = cfg.pcol[(wname, l)]
            outv = out_t[:, 0:nt].rearrange("p (s l) -> p s l", l=L)
            rd = ["RAWd"] + [("RAWh", si) for si in range(nseg)] + ["PRM"]
            if bias_name is not None:
                P.op("dve", lambda e: e.tensor_scalar(out=outv, in0=rawv[:, :, 0:L], scalar1=PRM[:, woff + ch * 4:woff + ch * 4 + 1],
                                                     scalar2=pc((bias_name, l), ch), op0=ALU.mult, op1=ALU.add),
                     reads=rd, writes=[out_res])
            else:
                P.op("dve", lambda e: e.tensor_scalar(out=outv, in0=rawv[:, :, 0:L], scalar1=PRM[:, woff + ch * 4:woff + ch * 4 + 1],
                                                     scalar2=None, op0=ALU.mult),
                     reads=rd, writes=[out_res])
            for i in range(1, 4):
                P.op("dve", lambda e, i=i: e.scalar_tensor_tensor(out=outv, in0=rawv[:, :, i:i + L],
                                                                scalar=PRM[:, woff + ch * 4 + i:woff + ch * 4 + i + 1],
                                                                in1=outv, op0=ALU.mult, op1=ALU.add),
                     reads=rd + [out_res], writes=[out_res])

        def mixer(l, nt, segs, L, last):
            nseg = len(segs)
            GT = [TMP[0], TMP[1], TMP[12], SQF]
            GTR = [tm(0), tm(1), tm(12), ["SQF"]]
            rmsnorm(nt, ("mix_norm", l), HB, "HB")
            P.dma("sp", "c_xs", lambda e: e.dma_start(out=xsp[:, :, 0:nt].rearrange("k p t -> p k t"), in_=X[:, :, 0:nt]),
                  reads=[("X", k) for k in range(KD)], writes=["xsp"])
            if last:
                P.dma("sp", "c_st0", lambda e: e.dma_start(out=HAS[:], in_=sca[l]), writes=[("HAS", 1), ("HAS", 2)])
                P.dma("sp", "c_st1", lambda e: e.dma_start(out=LHS[:], in_=slru[l]), writes=[("LHS", 1), ("LHS", 2)])
                P.dma("sp", "c_st2", lambda e: e.dma_start(out=HBS[:], in_=scb[l]), writes=[("HBS", 1), ("HBS", 2)])
                for s in range(2):
                    P.dma("sp", "c_st%d" % (3 + s), lambda e, s=s: e.dma_start(out=HL[:, :, 64 + 128 * s:192 + 128 * s], in_=sdl[l, s]),
                          writes=[(("SS", s + 1), h) for h in range(H)] + [("HL", c, 0) for c in range(KL)])
            XCs = [TMP[0], TMP[14], X[:, 10, :]]
            XCr = [tm(0), tm(14), [("X", 10)]]
            SGs = [TMP[1], TMP[15]]
            SGr = [tm(1), tm(15)]
            Rt, IGt, At, A2t, Bt = TMP[1], TMP[2], TMP[3], TMP[4], TMP[5]
            RSA = TMP[6]
            stages = []

            def xa_proj(c, bk, par):
                wt, wres, d = next_unit("xa")
                proj_group(PS[bk][:, 0:nt], psr(bk), wt, wres, d[-1], HB, "HB", nt)

            def xa_a1(c, bk, par):
                XC, rXC = XCs[par % 3], XCr[par % 3]
                conv_chunk(PS[bk], psr(bk), nt, segs, L, hist_a, l, c, "conv_a_w", "conv_a_b", XC, rXC)

            def xa_a2(c, bk, par):
                XC, rXC = XCs[par % 3], XCr[par % 3]
                P.op("act", lambda e: e.activation(out=BT[:, 2, 0:nt], in_=XC[:, 0:nt], func=AF.Copy), reads=[rXC], writes=[("BT", 2)])
                gi = (l * 2 + 0) * KL + c
                P.op("pe", lambda e: e.matmul(PS[0][:, 0:nt], lhsT=GW[:, gi, :], rhs=BT[:, 2, 0:nt], start=True, stop=True),
                     reads=[("BT", 2), "GW"], writes=[psr(0)])
                gi2 = (l * 2 + 1) * KL + c
                P.op("pe", lambda e: e.matmul(PS[2][:, 0:nt], lhsT=GW[:, gi2, :], rhs=BT[:, 2, 0:nt], start=True, stop=True),
                     reads=[("BT", 2), "GW"], writes=[psr(2)])
                P.op("act", lambda e: e.activation(out=Rt[:, 0:nt], in_=PS[0][:, 0:nt], func=AF.Sigmoid, bias=pc(("rg_b", l), c)),
                     reads=[psr(0), "PRM"], writes=[tm(1)])
                P.op("act", lambda e: e.activation(out=IGt[:, 0:nt], in_=PS[2][:, 0:nt], func=AF.Sigmoid, bias=pc(("ig_b", l), c)),
                     reads=[psr(2), "PRM"], writes=[tm(2)])
                P.op("act", lambda e: e.activation(out=At[:, 0:nt], in_=Rt[:, 0:nt], func=AF.Exp, scale=DER[:, l, c:c + 1]),
                     reads=[tm(1), ("DER", l)], writes=[tm(3)])
                P.op("act", lambda e: e.activation(out=A2t[:, 0:nt], in_=At[:, 0:nt], func=AF.Square), reads=[tm(3)], writes=[tm(4)])
                P.op("act", lambda e: e.activation(out=A2t[:, 0:nt], in_=A2t[:, 0:nt], func=AF.Ln, scale=-1.0, bias=ONEC[:, 0:1]),
                     reads=[tm(4), "EPSC"], writes=[tm(4)])
                P.op("act", lambda e: e.activation(out=A2t[:, 0:nt], in_=A2t[:, 0:nt], func=AF.Exp, scale=0.5), reads=[tm(4)], writes=[tm(4)])

            def xa_b(c, bk, par):
                XC, rXC = XCs[par % 3], XCr[par % 3]
                P.op("dve", lambda e: e.tensor_tensor(out=Bt[:, 0:nt], in0=IGt[:, 0:nt], in1=XC[:, 0:nt], op=ALU.mult),
                     reads=[tm(2), rXC], writes=[tm(5)])
                P.op("dve", lambda e: e.tensor_tensor(out=Bt[:, 0:nt], in0=Bt[:, 0:nt], in1=A2t[:, 0:nt], op=ALU.mult),
                     reads=[tm(5), tm(4)], writes=[tm(5)])
                for si, slot in enumerate(segs):
                    hb, hres = lru_h(l, slot)
                    cs = slice(si * L, (si + 1) * L)
                    P.op("dve", lambda e, cs=cs, hb=hb: e.tensor_tensor_scan(out=HL[:, c, cs], data0=At[:, cs], data1=Bt[:, cs],
                                                                         initial=hb[:, c:c + 1], op0=ALU.mult, op1=ALU.add),
                         reads=[tm(3), tm(5), hres], writes=[("HL", c, si)])
                    P.op("dve", lambda e, hb=hb, si=si: e.tensor_copy(out=hb[:, c:c + 1], in_=HL[:, c, (si + 1) * L - 1:(si + 1) * L]),
                         reads=[("HL", c, si)], writes=[hres])
                if c == KL - 1:
                    lru_stats()

            def lru_stats():
                for c in range(KL):
                    b = c % 2
                    P.op("act", lambda e, c=c, b=b: e.activation(out=BT[:, b, 0:nt], in_=HL[:, c, 0:nt], func=AF.Square),
                         reads=[("HL", c, si) for si in range(nseg)], writes=[("BT", b)])
                    P.op("pe", lambda e, c=c, b=b: e.matmul(PS[3][:, 0:nt], lhsT=ONB[:], rhs=BT[:, b, 0:nt], start=(c == 0), stop=(c == KL - 1)),
                         reads=[("BT", b), "ONB"], writes=[psr(3)])
                P.op("act", lambda e: e.activation(out=RSA[:, 0:nt], in_=PS[3][:, 0:nt], func=AF.Ln, scale=1.0 / cfg.LW, bias=EPSC[:, 0:1]),
                     reads=[psr(3), "EPSC"], writes=[tm(6)])
                P.op("act", lambda e: e.activation(out=RSA[:, 0:nt], in_=RSA[:, 0:nt], func=AF.Exp, scale=-0.5), reads=[tm(6)], writes=[tm(6)])
            for c in range(KL):
                stages.append((xa_proj, xa_a1, xa_a2, xa_b, c))

            def qkv_proj(j, bk, par):
                wt, wres, d = next_unit("qkv")
                proj_group(PS[bk][:, 0:nt], psr(bk), wt, wres, d[-1], HB, "HB", nt)

            def qkv_a1(j, bk, par):
                XC, rXC, SG, rSG = XCs[par % 3], XCr[par % 3], SGs[par % 2], SGr[par % 2]
                conv_chunk(PS[bk], psr(bk), nt, segs, L, hist_b, l, j, "conv_b_w", None, XC, rXC)

            def qkv_a2(j, bk, par):
                XC, rXC, SG, rSG = XCs[par % 3], XCr[par % 3], SGs[par % 2], SGr[par % 2]
                P.op("act", lambda e: e.activation(out=SG[:, 0:nt], in_=XC[:, 0:nt], func=AF.Exp, scale=-1.0), reads=[rXC], writes=[rSG])
                P.op("act", lambda e: e.activation(out=SG[:, 0:nt], in_=SG[:, 0:nt], func=AF.Ln, bias=ONEC[:, 0:1]), reads=[rSG, "EPSC"], writes=[rSG])
                P.op("act", lambda e: e.activation(out=SG[:, 0:nt], in_=SG[:, 0:nt], func=AF.Exp, scale=-1.0), reads=[rSG], writes=[rSG])
                if j < 2 * H:
                    nb = [3, 0][par % 2]
                    P.op("dve", lambda e: e.tensor_tensor(out=XC[:, 0:nt], in0=XC[:, 0:nt], in1=SG[:, 0:nt], op=ALU.mult), reads=[rXC, rSG], writes=[rXC])
                    P.op("act", lambda e: e.activation(out=SQF[:, 0:nt], in_=XC[:, 0:nt], func=AF.Square), reads=[rXC], writes=["SQF"])
                    P.op("pe", lambda e: e.matmul(PS[nb][:, 0:nt], lhsT=ONES, rhs=SQF[:, 0:nt], start=True, stop=True),
                         reads=["SQF", "CST"], writes=[psr(nb)])
                else:
                    P.op("dve", lambda e: e.tensor_tensor(out=AB[:, j, 0:nt], in0=XC[:, 0:nt], in1=SG[:, 0:nt], op=ALU.mult),
                         reads=[rXC, rSG], writes=[("AB", j)])

            def qkv_b(j, bk, par):
                if j >= 2 * H:
                    return
                XC, rXC, SG, rSG = XCs[par % 3], XCr[par % 3], SGs[par % 2], SGr[par % 2]
                nb = [3, 0][par % 2]
                P.op("act", lambda e: e.activation(out=SG[:, 0:nt], in_=PS[nb][:, 0:nt], func=AF.Ln, bias=EPSC[:, 0:1]),
                     reads=[psr(nb), "EPSC"], writes=[rSG])
                P.op("act", lambda e: e.activation(out=SG[:, 0:nt], in_=SG[:, 0:nt], func=AF.Exp, scale=-0.5), reads=[rSG], writes=[rSG])
                sc = (128.0 ** -0.5) if j < H else 1.0
                P.op("dve", lambda e: e.scalar_tensor_tensor(out=AB[:, j, 0:nt], in0=XC[:, 0:nt], scalar=sc, in1=SG[:, 0:nt],
                                                           op0=ALU.mult, op1=ALU.mult),
                     reads=[rXC, rSG], writes=[("AB", j)])
            for j in range(NQ):
                stages.append((qkv_proj, qkv_a1, qkv_a2, qkv_b, j))

            def tail_proj(_, bk, par):
                wt, wres, d = next_unit("tail")
                proj_group(PS[1][0:H, 0:nt], psr(1), wt, wres, d[-1], HB, "HB", nt, mcols=slice(0, H))
                proj_group(PS[2][0:H, 0:nt], psr(2), wt, wres, d[-1], HB, "HB", nt, mcols=slice(H, 2 * H))

            def tail_b(_, bk, par):
                P.op("act", lambda e: e.activation(out=GT[0][0:H, 0:nt], in_=PS[1][0:H, 0:nt], func=AF.Sigmoid), reads=[psr(1)], writes=[GTR[0]])
                P.op("act", lambda e: e.activation(out=GT[1][0:H, 0:nt], in_=PS[2][0:H, 0:nt], func=AF.Exp, bias=pc(("dt_bias", l), 0, H)),
                     reads=[psr(2), "PRM"], writes=[GTR[1]])
                P.op("act", lambda e: e.activation(out=GT[1][0:H, 0:nt], in_=GT[1][0:H, 0:nt], func=AF.Ln, bias=ONEC[0:H, 0:1]),
                     reads=[GTR[1], "EPSC"], writes=[GTR[1]])
                P.op("dve", lambda e: e.tensor_scalar(out=GT[1][0:H, 0:nt], in0=GT[1][0:H, 0:nt], scalar1=DER[0:H, l, KL:KL + 1], scalar2=None, op0=ALU.mult),
                     reads=[GTR[1], ("DERa", l)], writes=[GTR[1]])
            stages.append((tail_proj, None, None, tail_b, 0))

            def run_pipeline(stages):
                n = len(stages)

                def call(i, k):
                    if 0 <= i < n and stages[i][k] is not None:
                        stages[i][k](stages[i][4], 4 + i % 2, i)
                call(0, 0)
                call(1, 0)
                call(0, 1)
                call(2, 0)
                call(1, 1)
                call(0, 2)
                for i in range(n):
                    call(i + 3, 0)
                    call(i + 2, 1)
                    call(i, 3)
                    call(i + 1, 2)
            run_pipeline(stages)

            C = min(L, 128)
            nch = L // C
            nsq = max(1, int(np.ceil(np.log2(C))) - 1)
            HG = max(1, min(4, H // 2))
            assert H // HG == 2 and H % HG == 0 and KD >= 10
            Gsets = [[TMP[2], TMP[3], TMP[4], TMP[5], TMP[8], TMP[9], TMP[10], TMP[11], TMP[13]], [X[:, k, :] for k in range(9)]]
            GRsets = [[tm(2), tm(3), tm(4), tm(5), tm(8), tm(9), tm(10), tm(11), tm(13)], [[("X", k)] for k in range(9)]]
            SMs = [TMP[7], X[:, 9, :]]
            SMr = [tm(7), [("X", 9)]]

            def v3(t):
                return t[:, 0:HG * 128].rearrange("p (h c) -> p h c", c=128)

            def bcl(ap2, n):
                return ap2.unsqueeze(2).to_broadcast([ap2.shape[0], ap2.shape[1], n])

            def bcm(ap2, n):
                a = [list(x) for x in ap2.ap]
                return APc(ap2.tensor, ap2.offset, [a[0], [0, n], a[1]])
            def do_chunk(si, slot, ci, cidx):
                Sb, Sres = dstate(l, slot)
                SM, rSM = SMs[cidx % 2], SMr[cidx % 2]
                pre = []

                def OP(eng, fn, reads=(), writes=()):
                    pre.append((eng, fn, _flat(list(reads)), _flat(list(writes))))
                c0 = si * L + ci * C
                cs = slice(c0, c0 + C)
                OP("dve", lambda e, cs=cs: e.tensor_tensor_scan(out=GT[2][0:H, cs], data0=ONES[0:H, 0:C], data1=GT[1][0:H, cs],
                                                                 initial=0.0, op0=ALU.mult, op1=ALU.add),
                     reads=[GTR[1], "CST"], writes=[GTR[2]])
                OP("act", lambda e, cs=cs: e.activation(out=GT[3][0:H, cs], in_=GT[2][0:H, cs], func=AF.Exp), reads=[GTR[2]], writes=[GTR[3]])
                pss = PS[2]
                b3 = psr(2)

                def trfn(e, cs=cs):
                    e.transpose(out=pss[0:C, 0:H], in_=GT[0][0:H, cs], identity=IDENT[0:H, 0:H])
                    return e.transpose(out=pss[0:C, 8:8 + H], in_=GT[1][0:H, cs], identity=IDENT[0:H, 0:H])
                OP("pe", trfn, reads=[GTR[0], GTR[1], "CST"], writes=[b3])
                OP("dve", lambda e: e.tensor_copy(out=SM[0:C, 0:H], in_=pss[0:C, 0:H]), reads=[b3], writes=[rSM])
                OP("dve", lambda e: e.tensor_copy(out=SM[0:C, H:2 * H], in_=pss[0:C, 8:8 + H]), reads=[b3], writes=[rSM])

                def cumfn(e):
                    e.matmul(pss[0:C, 16:16 + H], lhsT=UTRI[0:C, 0:C], rhs=SM[0:C, H:2 * H], start=True, stop=True)
                    return e.matmul(pss[:, 24:24 + H], lhsT=ONES[0:C, :], rhs=SM[0:C, H:2 * H], start=True, stop=True)
                OP("pe", cumfn, reads=[rSM, "CST"], writes=[b3])
                OP("dve", lambda e: e.tensor_copy(out=SM[0:C, 2 * H:3 * H], in_=pss[0:C, 16:16 + H]), reads=[b3], writes=[rSM])
                OP("act", lambda e: e.activation(out=SM[0:C, 3 * H:4 * H], in_=SM[0:C, 2 * H:3 * H], func=AF.Exp), reads=[rSM], writes=[rSM])
                OP("dve", lambda e: e.tensor_tensor(out=SM[0:C, 4 * H:5 * H], in0=pss[0:C, 24:24 + H], in1=SM[0:C, 2 * H:3 * H], op=ALU.subtract),
                     reads=[b3, rSM], writes=[rSM])
                OP("act", lambda e: e.activation(out=SM[0:C, 4 * H:5 * H], in_=SM[0:C, 4 * H:5 * H], func=AF.Exp), reads=[rSM], writes=[rSM])
                OP("dve", lambda e: e.tensor_tensor(out=SM[0:C, 5 * H:6 * H], in0=SM[0:C, 0:H], in1=SM[0:C, 3 * H:4 * H], op=ALU.mult),
                     reads=[rSM], writes=[rSM])
                OP("act", lambda e: e.activation(out=SM[:, 6 * H:7 * H], in_=pss[:, 24:24 + H], func=AF.Exp), reads=[b3], writes=[rSM])
                def do_group(h0, gset):
                    ops = []

                    def OP(eng, fn, reads=(), writes=()):
                        ops.append((eng, fn, _flat(list(reads)), _flat(list(writes))))
                    G, GR = Gsets[gset], GRsets[gset]
                    hs = list(range(h0, h0 + HG))
                    ia, ib, ic = (0, 1, 2) if gset == 0 else (3, 4, 5)
                    Ba, Bb, Bc_ = v3(PS[ia]), v3(PS[ib]), v3(PS[ic])
                    ra, rb_, rc = psr(ia), psr(ib), psr(ic)
                    PBt = PSBs[gset]
                    PTK = PBt[:, 0:HG * 128].rearrange("p (h c) -> p h c", c=128)
                    PTV = PBt[:, 512:512 + HG * 128].rearrange("p (h c) -> p h c", c=128)
                    r7 = psr(6 + gset)
                    Gv = [v3(g) for g in G]
                    rK = [("AB", H + h) for h in hs]
                    rQ = [("AB", h) for h in hs]
                    rV = [("AB", 2 * H + h) for h in hs]

                    def smc(k):
                        return SM[0:C, k * H + h0:k * H + h0 + HG]

                    def mm_each(fnh):
                        def fn(e):
                            ins = None
                            for gi, h in enumerate(hs):
                                ins = fnh(e, gi, h)
                            return ins
                        return fn
                    W3 = (slice(0, C), slice(0, HG), slice(0, C))
                    WF = (slice(0, C), slice(0, HG), slice(0, 128))
                    WT_ = (slice(0, 128), slice(0, HG), slice(0, C))
                    WS_ = (slice(0, 128), slice(0, HG), slice(0, 128))
                    Dm, DT = Gv[0], Gv[1]
                    OP("pe", mm_each(lambda e, gi, h: e.matmul(Bc_[0:C, gi, 0:C], lhsT=IDENT[0:H, h:h + 1].to_broadcast([H, C]), rhs=GT[2][0:H, cs], start=True, stop=True)),
                       reads=[GTR[2], "CST"], writes=[rc])
                    OP("pe", mm_each(lambda e, gi, h: e.matmul(Ba[0:C, gi, 0:C], lhsT=AB[:, H + h, cs], rhs=AB[:, H + h, cs], start=True, stop=True)),
                       reads=[rK], writes=[ra])
                    OP("pe", mm_each(lambda e, gi, h: e.matmul(Bb[0:C, gi, 0:C], lhsT=AB[:, H + h, cs], rhs=AB[:, h, cs], start=True, stop=True)),
                       reads=[rK, rQ], writes=[rb_])
                    OP("dve", lambda e: e.tensor_tensor(out=Dm[W3], in0=Bc_[W3], in1=bcl(smc(2), C), op=ALU.subtract), reads=[rc, rSM], writes=[GR[0]])
                    OP("dve", lambda e: e.tensor_scalar(out=DT[W3], in0=Dm[W3], scalar1=0.0, scalar2=None, op0=ALU.min), reads=[GR[0]], writes=[GR[1]])
                    OP("dve", lambda e: e.tensor_scalar(out=Dm[W3], in0=Dm[W3], scalar1=0.0, scalar2=None, op0=ALU.max), reads=[GR[0]], writes=[GR[0]])
                    OP("act", lambda e: e.activation(out=Dm[W3], in_=Dm[W3], func=AF.Exp, scale=-1.0), reads=[GR[0]], writes=[GR[0]])
                    OP("act", lambda e: e.activation(out=DT[W3], in_=DT[W3], func=AF.Exp), reads=[GR[1]], writes=[GR[1]])
                    OP("dve", lambda e: e.tensor_tensor(out=Dm[W3], in0=Dm[W3], in1=bcm(MSL[0:C, 0:C], HG), op=ALU.mult), reads=[GR[0], "CST"], writes=[GR[0]])
                    OP("dve", lambda e: e.tensor_tensor(out=DT[W3], in0=DT[W3], in1=bcm(MUI[0:C, 0:C], HG), op=ALU.mult), reads=[GR[1], "CST"], writes=[GR[1]])
                    OP("dve", lambda e: e.tensor_tensor(out=Dm[W3], in0=Dm[W3], in1=bcl(smc(0), C), op=ALU.mult), reads=[GR[0], rSM], writes=[GR[0]])
                    OP("dve", lambda e: e.tensor_tensor(out=DT[W3], in0=Bb[W3], in1=DT[W3], op=ALU.mult), reads=[rb_, GR[1]], writes=[GR[1]])
                    Ac, rAc, An, rAn = Gv[2], GR[2], Gv[3], GR[3]
                    Bc, rBc, Bn, rBn = Gv[4], GR[4], Gv[5], GR[5]
                    Qc, rQc, Qn, rQn = Gv[6], GR[6], Gv[7], GR[7]
                    OP("dve", lambda e, Ac=Ac: e.tensor_tensor(out=Ac[W3], in0=Ba[W3], in1=Dm[W3], op=ALU.mult), reads=[ra, GR[0]], writes=[rAc])
                    OP("pe", mm_each(lambda e, gi, h, Ac=Ac: e.transpose(out=Ba[0:C, gi, 0:C], in_=Ac[0:C, gi, 0:C], identity=IDENT[0:C, 0:C])),
                       reads=[rAc, "CST"], writes=[ra])
                    OP("act", lambda e, Bc=Bc: e.activation(out=Bc[W3], in_=Ba[W3], func=AF.Copy), reads=[ra], writes=[rBc])
                    OP("dve", lambda e, Qc=Qc: e.tensor_tensor(out=Qc[W3], in0=bcm(IDENT[0:C, 0:C], HG), in1=Ba[W3], op=ALU.subtract), reads=[ra, "CST"], writes=[rQc])
                    for jq in range(1, nsq + 1):
                        OP("pe", mm_each(lambda e, gi, h, Bc=Bc, Ac=Ac: e.matmul(Ba[0:C, gi, 0:C], lhsT=Bc[0:C, gi, 0:C], rhs=Ac[0:C, gi, 0:C], start=True, stop=True)),
                           reads=[rBc, rAc], writes=[ra])
                        if jq < nsq:
                            OP("pe", mm_each(lambda e, gi, h, Bc=Bc, Ac=Ac: e.matmul(Bc_[0:C, gi, 0:C], lhsT=Ac[0:C, gi, 0:C], rhs=Bc[0:C, gi, 0:C], start=True, stop=True)),
                               reads=[rBc, rAc], writes=[rc])
                        OP("act", lambda e, An=An: e.activation(out=An[W3], in_=Ba[W3], func=AF.Copy), reads=[ra], writes=[rAn])
                        if jq < nsq:
                            OP("dve", lambda e, Bn=Bn: e.tensor_copy(out=Bn[W3], in_=Bc_[W3]), reads=[rc], writes=[rBn])
                        OP("pe", mm_each(lambda e, gi, h, An=An, Qc=Qc: e.matmul(Bb[0:C, gi, 0:C], lhsT=An[0:C, gi, 0:C], rhs=Qc[0:C, gi, 0:C], start=True, stop=True)),
                           reads=[rAn, rQc], writes=[rb_])
                        OP("dve", lambda e, Qn=Qn, Qc=Qc: e.tensor_tensor(out=Qn[W3], in0=Qc[W3], in1=Bb[W3], op=ALU.add), reads=[rQc, rb_], writes=[rQn])
                        Ac, rAc, An, rAn = An, rAn, Ac, rAc
                        Bc, rBc, Bn, rBn = Bn, rBn, Bc, rBc
                        Qc, rQc, Qn, rQn = Qn, rQn, Qc, rQc
                    RK, rRK, KE, rKE = Gv[2], GR[2], Gv[3], GR[3]
                    Ut, rUt, WK, rWK = Gv[4], GR[4], Gv[5], GR[5]
                    QD, rQD = Qn, rQn
                    VB, rVB = Gv[0], GR[0]
                    Wt, rWt = Gv[8], GR[8]
                    QK, rQK = DT, GR[1]

                    def tkv(e):
                        ins = None
                        for gi, h in enumerate(hs):
                            e.transpose(out=PTK[0:C, gi, :], in_=AB[:, H + h, cs], identity=IDB[:])
                            ins = e.transpose(out=PTV[0:C, gi, :], in_=AB[:, 2 * H + h, cs], identity=IDB[:])
                        return ins
                    OP("pe", tkv, reads=[rK, rV, "IDB"], writes=[r7])
                    OP("dve", lambda e: e.tensor_tensor(out=RK[WF], in0=PTK[WF], in1=bcl(smc(5), 128), op=ALU.mult), reads=[r7, rSM], writes=[rRK])
                    OP("dve", lambda e: e.tensor_tensor(out=KE[WF], in0=PTK[WF], in1=bcl(smc(4), 128), op=ALU.mult), reads=[r7, rSM], writes=[rKE])
                    OP("dve", lambda e: e.tensor_tensor(out=VB[WF], in0=PTV[WF], in1=bcl(smc(0), 128), op=ALU.mult), reads=[r7, rSM], writes=[rVB])
                    OP("pe", mm_each(lambda e, gi, h: e.matmul(Ba[0:C, gi, :], lhsT=Qc[0:C, gi, 0:C], rhs=VB[0:C, gi, :], start=True, stop=True)),
                       reads=[rQc, rVB], writes=[ra])
                    OP("act", lambda e: e.activation(out=Ut[WF], in_=Ba[WF], func=AF.Copy), reads=[ra], writes=[rUt])
                    OP("pe", mm_each(lambda e, gi, h: e.matmul(Bc_[:, gi, 0:C], lhsT=RK[0:C, gi, :], rhs=Qc[0:C, gi, 0:C], start=True, stop=True)),
                       reads=[rRK, rQc], writes=[rc])
                    OP("act", lambda e: e.activation(out=WK[WT_], in_=Bc_[WT_], func=AF.Copy), reads=[rc], writes=[rWK])
                    OP("pe", mm_each(lambda e, gi, h: e.matmul(Bb[:, gi, 0:C], lhsT=IDENT[0:H, h:h + 1].to_broadcast([H, 128]), rhs=GT[3][0:H, cs], start=True, stop=True)),
                       reads=[GTR[3], "CST"], writes=[rb_])
                    OP("dve", lambda e: e.tensor_tensor(out=QD[WT_], in0=AB[:, h0:h0 + HG, cs], in1=Bb[WT_], op=ALU.mult), reads=[rQ, rb_], writes=[rQD])
                    rS = [(Sres, h) for h in hs]
                    Sg = Sb[:, h0:h0 + HG, :]
                    OP("pe", mm_each(lambda e, gi, h: e.matmul(Ba[0:C, gi, :], lhsT=WK[:, gi, 0:C], rhs=Sb[:, h, :], start=True, stop=True)),
                       reads=[rWK, rS], writes=[ra])
                    OP("dve", lambda e: e.tensor_tensor(out=Wt[WF], in0=Ut[WF], in1=Ba[WF], op=ALU.subtract), reads=[rUt, ra], writes=[rWt])

                    def ofn(e):
                        ins = None
                        for gi, h in enumerate(hs):
                            e.matmul(Bc_[:, gi, 0:C], lhsT=Sb[:, h, :], rhs=QD[:, gi, 0:C], start=True, stop=False)
                            ins = e.matmul(Bc_[:, gi, 0:C], lhsT=Wt[0:C, gi, :], rhs=QK[0:C, gi, 0:C], start=False, stop=True)
                        return ins
                    OP("pe", ofn, reads=[rS, rQD, rWt, rQK], writes=[rc])
                    OP("act", lambda e: e.activation(out=OO[:, h0:h0 + HG, cs], in_=Bc_[WT_], func=AF.Copy), reads=[rc],
                       writes=[("OO", h, si, ci) for h in hs])
                    OP("pe", mm_each(lambda e, gi, h: e.matmul(Bb[:, gi, :], lhsT=KE[0:C, gi, :], rhs=Wt[0:C, gi, :], start=True, stop=True)),
                       reads=[rKE, rWt], writes=[rb_])
                    OP("dve", lambda e: e.tensor_tensor(out=Sg, in0=Sg, in1=bcl(SM[:, 6 * H + h0:6 * H + h0 + HG], 128), op=ALU.mult), reads=[rS, rSM], writes=[rS])
                    OP("dve", lambda e: e.tensor_tensor(out=Sg, in0=Sg, in1=Bb[WS_], op=ALU.add), reads=[rS, rb_], writes=[rS])
                    return ops
                return [pre + do_group(0, 0)] + [do_group(h0, gi) for gi, h0 in list(enumerate(range(0, H, HG)))[1:]], len(pre)

            allops = []
            cidx = 0
            for si_, slot_ in enumerate(segs):
                for ci_ in range(nch):
                    lists, npre = do_chunk(si_, slot_, ci_, cidx)
                    period = len(lists[0])
                    for gi_, lst in enumerate(lists):
                        off = 0 if gi_ == 0 else npre + 16
                        for k_, op_ in enumerate(lst):
                            allops.append((cidx * period + off + k_, gi_, op_))
                    cidx += 1
            allops.sort(key=lambda t: (t[0], t[1]))
            for _, _, (eng_, fn_, rd_, wr_) in allops:
                P.op(eng_, fn_, reads=rd_, writes=wr_)
            oo_res = lambda h: [("OO", h, si, ci) for si in range(nseg) for ci in range(nch)]

            for c in range(KL):
                wt, wres, d = next_unit("ga")
                po = PS[4 + c % 2]
                pres = psr(4 + c % 2)
                proj_group(po[:, 0:nt], pres, wt, wres, d[-1], HB, "HB", nt)
                G1, G2 = TMP[0], TMP[1]
                P.op("act", lambda e, po=po: e.activation(out=G1[:, 0:nt], in_=po[:, 0:nt], func=AF.Square), reads=[pres], writes=[tm(0)])
                P.op("dve", lambda e: e.tensor_scalar(out=G1[:, 0:nt], in0=G1[:, 0:nt], scalar1=0.044715, scalar2=1.0, op0=ALU.mult, op1=ALU.add),
                     reads=[tm(0)], writes=[tm(0)])
                P.op("dve", lambda e, po=po: e.tensor_tensor(out=G1[:, 0:nt], in0=G1[:, 0:nt], in1=po[:, 0:nt], op=ALU.mult), reads=[tm(0), pres], writes=[tm(0)])
                P.op("act", lambda e: e.activation(out=G1[:, 0:nt], in_=G1[:, 0:nt], func=AF.Sigmoid, scale=1.5957691216057308), reads=[tm(0)], writes=[tm(0)])
                P.op("dve", lambda e, po=po: e.tensor_tensor(out=G1[:, 0:nt], in0=G1[:, 0:nt], in1=po[:, 0:nt], op=ALU.mult), reads=[tm(0), pres], writes=[tm(0)])
                P.op("dve", lambda e, c=c: e.scalar_tensor_tensor(out=G2[:, 0:nt], in0=HL[:, c, 0:nt], scalar=pc(("norm_a", l), c), in1=RSA[:, 0:nt],
                                                               op0=ALU.mult, op1=ALU.mult),
                     reads=[("HL", c, si) for si in range(nseg)] + [tm(6), "PRM"], writes=[tm(1)])
                P.op("dve", lambda e, c=c: e.tensor_tensor(out=AB[:, NQ + c, 0:nt], in0=G1[:, 0:nt], in1=G2[:, 0:nt], op=ALU.mult),
                     reads=[tm(0), tm(1)], writes=[("AB", NQ + c)])
            for h in range(H):
                Z1, Z2 = TMP[0], TMP[1]
                P.op("act", lambda e, h=h: e.activation(out=SQF[:, 0:nt], in_=OO[:, h, 0:nt], func=AF.Square), reads=oo_res(h), writes=["SQF"])
                P.op("pe", lambda e: e.matmul(PS[3][:, 0:nt], lhsT=ONES, rhs=SQF[:, 0:nt], start=True, stop=True), reads=["SQF", "CST"], writes=[psr(3)])
                wt, wres, d = next_unit("z")
                po = PS[4 + h % 2]
                pres = psr(4 + h % 2)
                proj_group(po[:, 0:nt], pres, wt, wres, d[-1], HB, "HB", nt)
                P.op("act", lambda e: e.activation(out=Z2[:, 0:nt], in_=PS[3][:, 0:nt], func=AF.Ln, scale=1.0 / 128.0, bias=EPSC[:, 0:1]),
                     reads=[psr(3), "EPSC"], writes=[tm(1)])
                P.op("act", lambda e: e.activation(out=Z2[:, 0:nt], in_=Z2[:, 0:nt], func=AF.Exp, scale=-0.5), reads=[tm(1)], writes=[tm(1)])
                P.op("dve", lambda e, h=h: e.scalar_tensor_tensor(out=Z2[:, 0:nt], in0=OO[:, h, 0:nt], scalar=pc(("norm_b", l), 0), in1=Z2[:, 0:nt],
                                                               op0=ALU.mult, op1=ALU.mult),
                     reads=oo_res(h) + [tm(1), "PRM"], writes=[tm(1)])
                P.op("act", lambda e, po=po: e.activation(out=Z1[:, 0:nt], in_=po[:, 0:nt], func=AF.Exp, scale=-1.0), reads=[pres], writes=[tm(0)])
                P.op("act", lambda e: e.activation(out=Z1[:, 0:nt], in_=Z1[:, 0:nt], func=AF.Ln, bias=ONEC[:, 0:1]), reads=[tm(0), "EPSC"], writes=[tm(0)])
                P.op("act", lambda e: e.activation(out=Z1[:, 0:nt], in_=Z1[:, 0:nt], func=AF.Exp, scale=-1.0), reads=[tm(0)], writes=[tm(0)])
                P.op("dve", lambda e, po=po: e.tensor_tensor(out=Z1[:, 0:nt], in0=Z1[:, 0:nt], in1=po[:, 0:nt], op=ALU.mult), reads=[tm(0), pres], writes=[tm(0)])
                P.op("dve", lambda e, h=h: e.tensor_tensor(out=AB[:, NQ + KL + h, 0:nt], in0=Z1[:, 0:nt], in1=Z2[:, 0:nt], op=ALU.mult),
                     reads=[tm(0), tm(1)], writes=[("AB", NQ + KL + h)])
            P.dma("sp", "c_xl", lambda e: e.dma_start(out=X[:, :, 0:nt], in_=xsp[:, :, 0:nt].rearrange("k p t -> p k t")),
                  reads=["xsp"], writes=[("X", k) for k in range(KD)])
            for m in range(KD):
                wt, wres, d = next_unit("wout")
                pd = PS[4 + m % 2]
                pres = psr(4 + m % 2)

                def fn(e, wt=wt, pd=pd):
                    ins = None
                    for k in range(KD):
                        ins = e.matmul(pd[:, 0:nt], lhsT=wt[:, k, :], rhs=AB[:, NQ + k, 0:nt], start=(k == 0), stop=(k == KD - 1))
                    return ins
                P.op("pe", fn, reads=[wres] + [("AB", NQ + k) for k in range(KD)], writes=[pres])
                P.op("dve", lambda e, m=m, pd=pd: e.tensor_tensor(out=X[:, m, 0:nt], in0=X[:, m, 0:nt], in1=pd[:, 0:nt], op=ALU.add),
                     reads=[pres, ("X", m)], writes=[("X", m)])
            if last:
                for slot in range(3):
                    hb, hres = hist_a(l, slot)
                    P.dma("sp", "c_o0_%d_%d" % (l, slot), lambda e, hb=hb, slot=slot: e.dma_start(out=o_ca[l, slot], in_=hb[:, :, :]), reads=[hres], writes=[("o_ca", l, slot)])
                    hb, hres = lru_h(l, slot)
                    P.dma("sp", "c_o1_%d_%d" % (l, slot), lambda e, hb=hb, slot=slot: e.dma_start(out=o_lru[l, slot], in_=hb[:, :]), reads=[hres], writes=[("o_lru", l, slot)])
                    hb, hres = hist_b(l, slot)
                    P.dma("sp", "c_o2_%d_%d" % (l, slot), lambda e, hb=hb, slot=slot: e.dma_start(out=o_cb[l, slot], in_=hb[:, :, :]), reads=[hres], writes=[("o_cb", l, slot)])
                    sbuf_, sres = dstate(l, slot)
                    P.dma("sp", "c_o3_%d_%d" % (l, slot), lambda e, sbuf_=sbuf_, slot=slot: e.dma_start(out=o_dl[l, slot], in_=sbuf_[:, :, :]),
                          reads=[(sres, h) for h in range(H)], writes=[("o_dl", l, slot)])


        tiles = []
        for i in range(cfg.NBIG):
            tiles.append(("big", i * T, T, [0], T, False))
        tiles.append(("small", cfg.NBIG * T, 48, [0, 1, 2], 16, True))
        for kind, t0, nt, segs, L, last in tiles:
            if kind == "big":
                P.dma("sp", "c_x", lambda e, t0=t0, nt=nt: e.dma_start(out=X[:, :, 0:nt], in_=xp[:, :, t0:t0 + nt].rearrange("k p t -> p k t")),
                      writes=[("X", k) for k in range(KD)])
            else:
                P.dma("sp", "c_x", lambda e, t0=t0: e.dma_start(out=X[:, :, 0:16], in_=xp[:, :, t0:t0 + 16].rearrange("k p t -> p k t")),
                      writes=[("X", k) for k in range(KD)])
                P.dma("sp", "c_x2", lambda e: e.dma_start(out=X[:, :, 16:48], in_=xs.rearrange("k p t -> p k t")),
                      writes=[("X", k) for k in range(KD)])
            for l in range(DEPTH):
                ffn(l, 1, nt)
                mixer(l, nt, segs, L, last)
                ffn(l, 2, nt)
            ydst = [(HL[:, k, :], [("HL", k, si) for si in range(3)]) for k in range(KL)] + \
                   [(OO[:, h, :], [("OO", h, si, ci) for si in range(3) for ci in range(4)]) for h in range(H)]
            rmsnorm(nt, ("final_norm",), None, None, dst_list=ydst)
            rdA = [r for k in range(KL) for r in ydst[k][1]]
            rdB = [r for k in range(KL, KD) for r in ydst[k][1]]
            if kind == "big":
                P.dma("sp", "c_y", lambda e, t0=t0, nt=nt: e.dma_start(out=yp[0:KL, :, t0:t0 + nt].rearrange("k p t -> p k t"), in_=HL[:, :, 0:nt]),
                      reads=rdA, writes=[("yp", t0, 0)])
                P.dma("sp", "c_yb", lambda e, t0=t0, nt=nt: e.dma_start(out=yp[KL:KD, :, t0:t0 + nt].rearrange("k p t -> p k t"), in_=OO[:, :, 0:nt]),
                      reads=rdB, writes=[("yp", t0, 1)])
            else:
                P.dma("sp", "c_y", lambda e, t0=t0: e.dma_start(out=yp[0:KL, :, t0:t0 + 16].rearrange("k p t -> p k t"), in_=HL[:, :, 0:16]),
                      reads=rdA, writes=[("yp", t0, 0)])
                P.dma("sp", "c_yb", lambda e, t0=t0: e.dma_start(out=yp[KL:KD, :, t0:t0 + 16].rearrange("k p t -> p k t"), in_=OO[:, :, 0:16]),
                      reads=rdB, writes=[("yp", t0, 1)])
                P.dma("sp", "c_y2", lambda e: e.dma_start(out=ys[0:KL].rearrange("k p t -> p k t"), in_=HL[:, :, 16:48]),
                      reads=rdA, writes=[("ys", 0)])
                P.dma("sp", "c_y2b", lambda e: e.dma_start(out=ys[KL:KD].rearrange("k p t -> p k t"), in_=OO[:, :, 16:48]),
                      reads=rdB, writes=[("ys", 1)])
        assert wstate["gu"] == cfg.NU * len(tiles)
        P.emit(st)
    return nc


def _blk(w, ks, cols, UW):
    out = np.zeros((128, UW), np.float32)
    for i, k in enumerate(ks):
        blk = w[k * 128:(k + 1) * 128, cols]
        out[:, i * 128:i * 128 + blk.shape[1]] = blk
    return out


def prepare(cfg, inp):
    f32 = np.float32
    D, KD, KF, KL, H, NQ, DEPTH, LW = cfg.D, cfg.KD, cfg.KF, cfg.KL, cfg.H, cfg.NQ, cfg.DEPTH, cfg.LW
    g = {k: np.asarray(v, f32) for k, v in inp.items()}
    ws = np.zeros((cfg.NU, 128, cfg.UW), f32)
    o2 = 2 * LW
    o3 = o2 + NQ * 128
    o4 = o3 + H * 128
    for u, d in enumerate(cfg.units):
        kind = d[0]
        if kind in ("gate", "up", "down"):
            _, l, which, idx, ks = d
            wsel = {("gate", 1): g["ffn1_w_gate"], ("up", 1): g["ffn1_w_up"], ("down", 1): g["ffn1_w_down"],
                    ("gate", 2): g["ffn2_w_gate"], ("up", 2): g["ffn2_w_up"], ("down", 2): g["ffn2_w_down"]}[(kind, which)]
            ws[u] = _blk(wsel[l], ks, slice(idx * 128, (idx + 1) * 128), cfg.UW)
        else:
            _, l, idx, ks = d
            if kind == "xa":
                ws[u] = _blk(g["w_in"][l], ks, slice(idx * 128, (idx + 1) * 128), cfg.UW)
            elif kind == "ga":
                ws[u] = _blk(g["w_in"][l], ks, slice(LW + idx * 128, LW + (idx + 1) * 128), cfg.UW)
            elif kind == "qkv":
                ws[u] = _blk(g["w_in"][l], ks, slice(o2 + idx * 128, o2 + (idx + 1) * 128), cfg.UW)
            elif kind == "z":
                ws[u] = _blk(g["w_in"][l], ks, slice(o3 + idx * 128, o3 + (idx + 1) * 128), cfg.UW)
            elif kind == "tail":
                ws[u] = _blk(g["w_in"][l], ks, slice(o4, o4 + 2 * H), cfg.UW)
            elif kind == "wout":
                ws[u] = _blk(g["w_out"][l], ks, slice(idx * 128, (idx + 1) * 128), cfg.UW)
    prm = np.zeros((128, cfg.NP), f32)

    def put(name, arr):
        off, w = cfg.pcol[name]
        prm[:arr.shape[0], off:off + w] = arr

    def pk(v):
        return v.reshape(-1, 128).T
    for l in range(DEPTH):
        put(("ffn1_norm", l), pk(g["ffn1_norm"][l]))
        put(("mix_norm", l), pk(g["mix_norm"][l]))
        put(("ffn2_norm", l), pk(g["ffn2_norm"][l]))
        put(("conv_a_w", l), g["conv_a_w"][l].reshape(4, KL, 128).transpose(2, 1, 0).reshape(128, KL * 4))
        put(("conv_a_b", l), pk(g["conv_a_b"][l]))
        put(("rg_b", l), pk(g["rg_b"][l]))
        put(("ig_b", l), pk(g["ig_b"][l]))
        put(("lam", l), pk(g["lru_lambda"][l]))
        put(("norm_a", l), pk(g["norm_a"][l]))
        put(("conv_b_w", l), g["conv_b_w"][l].reshape(4, NQ, 128).transpose(2, 1, 0).reshape(128, NQ * 4))
        put(("norm_b", l), g["norm_b"][l].reshape(128, 1))
        put(("a_log", l), g["a_log"][l].reshape(H, 1))
        put(("dt_bias", l), g["dt_bias"][l].reshape(H, 1))
    put(("final_norm",), pk(g["final_norm"]))
    gw = np.zeros((DEPTH, 2, 128, KL, 128), f32)
    for l in range(DEPTH):
        for gi, name in enumerate(("rg_w", "ig_w")):
            w = g[name][l]
            for c in range(KL):
                gw[l, gi, 0:64, c, 0:64] = w[2 * c]
                gw[l, gi, 64:128, c, 64:128] = w[2 * c + 1]
    cst = np.zeros((128, 6, 128), f32)
    ii = np.arange(128)
    cst[:, 0, :] = np.eye(128)
    cst[:, 1, :] = (ii[:, None] > ii[None, :])
    cst[:, 2, :] = (ii[None, :] >= ii[:, None])
    cst[:, 3, :] = (ii[:, None] <= ii[None, :])
    cst[:, 4, :] = 1.0
    shared = {"wstream": ws, "prm": prm, "gw": gw, "cst": cst}
    in_maps = []
    for c in range(cfg.NCORES):
        m = dict(shared)
        if c < cfg.BATCH:
            stream = np.concatenate([g["meta_tokens"], g["x_prompt"][c]], axis=0)
            m["xp"] = np.ascontiguousarray(stream.T.reshape(KD, 128, cfg.NTOK))
        else:
            m["xp"] = np.zeros((KD, 128, cfg.NTOK), f32)
        xsm = g["x_sample"][2 * c:2 * c + 2].reshape(32, D)
        m["xs"] = np.ascontiguousarray(xsm.T.reshape(KD, 128, 32))
        sl = slice(2 * c, 2 * c + 2)
        m["sca"] = np.ascontiguousarray(g["state_conv_a"][:, sl].reshape(DEPTH, 2, 3, KL, 128).transpose(0, 4, 1, 3, 2))
        m["slru"] = np.ascontiguousarray(g["state_lru"][:, sl].reshape(DEPTH, 2, KL, 128).transpose(0, 3, 1, 2))
        m["scb"] = np.ascontiguousarray(g["state_conv_b"][:, sl].reshape(DEPTH, 2, 3, NQ, 128).transpose(0, 4, 1, 3, 2))
        m["sdl"] = np.ascontiguousarray(g["state_delta"][:, sl].transpose(0, 1, 3, 2, 4))
        in_maps.append(m)
    return in_maps


def assemble(cfg, res):
    f32 = np.float32
    D, KD, KL, H, NQ, DEPTH, LW = cfg.D, cfg.KD, cfg.KL, cfg.H, cfg.NQ, cfg.DEPTH, cfg.LW
    B, DB = cfg.BATCH, cfg.DEC_BATCH
    y_prompt = np.zeros((B, cfg.SEQ, D), f32)
    y_sample = np.zeros((DB, 16, D), f32)
    p_ca = np.zeros((DEPTH, B, 3, LW), f32)
    p_lru = np.zeros((DEPTH, B, LW), f32)
    p_cb = np.zeros((DEPTH, B, 3, NQ * 128), f32)
    p_dl = np.zeros((DEPTH, B, H, 128, 128), f32)
    s_ca = np.zeros((DEPTH, DB, 3, LW), f32)
    s_lru = np.zeros((DEPTH, DB, LW), f32)
    s_cb = np.zeros((DEPTH, DB, 3, NQ * 128), f32)
    s_dl = np.zeros((DEPTH, DB, H, 128, 128), f32)
    for c, r in enumerate(res):
        ypc = np.asarray(r["yp"]).reshape(D, cfg.NTOK).T
        if c < B:
            y_prompt[c] = ypc[cfg.NMETA:]
        ysc = np.asarray(r["ys"]).reshape(D, 32).T.reshape(2, 16, D)
        y_sample[2 * c:2 * c + 2] = ysc
        ca = np.asarray(r["o_ca"]).transpose(0, 1, 4, 3, 2).reshape(DEPTH, 3, 3, LW)
        lr = np.asarray(r["o_lru"]).transpose(0, 1, 3, 2).reshape(DEPTH, 3, LW)
        cb = np.asarray(r["o_cb"]).transpose(0, 1, 4, 3, 2).reshape(DEPTH, 3, 3, NQ * 128)
        dl = np.asarray(r["o_dl"]).transpose(0, 1, 3, 2, 4)
        if c < B:
            p_ca[:, c], p_lru[:, c], p_cb[:, c], p_dl[:, c] = ca[:, 0], lr[:, 0], cb[:, 0], dl[:, 0]
        for s in range(2):
            b = 2 * c + s
            s_ca[:, b], s_lru[:, b], s_cb[:, b], s_dl[:, b] = ca[:, 1 + s], lr[:, 1 + s], cb[:, 1 + s], dl[:, 1 + s]
    return (y_prompt, y_sample, p_ca, p_lru, p_cb, p_dl, s_ca, s_lru, s_cb, s_dl)


def run(cfg, inputs, trace=False):
    nc = build_program(cfg)
    in_maps = prepare(cfg, inputs)
    res = run_bass_kernel_spmd(nc, in_maps, core_ids=list(range(cfg.NCORES)), trace=trace)
    return assemble(cfg, res.results), res


def kernel(**inputs):
    cfg = Cfg()
    out, _ = run(cfg, inputs)
    return out
```

```python
import contextlib
import numpy as np
import concourse.bass as bass
import concourse.mybir as mybir
from concourse.bass_utils import run_bass_kernel_spmd
from concourse.ap import AP as APc

F32 = mybir.dt.float32
BF16 = mybir.dt.bfloat16
ALU = mybir.AluOpType
AF = mybir.ActivationFunctionType

ENGS = ("pe", "dve", "act", "pool", "sp")
EPS = 1e-6


def _flat(xs):
    out = []
    for x in xs:
        if isinstance(x, list):
            out.extend(_flat(x))
        else:
            out.append(x)
    return out


def tm(k):
    return [("ts", k, i) for i in range(4)]


def psr(b):
    return [("bank", b)]


class Prog:
    def __init__(self, nc):
        self.nc = nc
        self.ops = {e: [] for e in ENGS}
        self.cnt = {e: 0 for e in ("pe", "dve", "act", "pool")}
        self.clock = {e: {} for e in ENGS}
        self.vc = {}
        self.last_w = {}
        self.readers = {}
        self.chans = []

    def _need(self, eng, deps):
        waits = {}
        ck = self.clock[eng]
        for (tl, c) in deps:
            if ck.get(tl, 0) >= c:
                continue
            if waits.get(tl, 0) < c:
                waits[tl] = c
        for tl, c in waits.items():
            snap = self.vc.get((tl, c))
            if snap:
                for k, v in snap.items():
                    if ck.get(k, 0) < v:
                        ck[k] = v
            if ck.get(tl, 0) < c:
                ck[tl] = c
        return sorted(waits.items())

    def _deps(self, reads, writes):
        deps = []
        for r in reads:
            lw = self.last_w.get(r)
            if lw:
                deps.append(lw)
        for w in writes:
            lw = self.last_w.get(w)
            if lw:
                deps.append(lw)
            for tl, c in self.readers.get(w, {}).items():
                deps.append((tl, c))
        return deps

    def _commit(self, tl, c, reads, writes):
        for r in reads:
            self.readers.setdefault(r, {})[tl] = c
        for w in writes:
            self.last_w[w] = (tl, c)
            self.readers[w] = {}

    def op(self, eng, fn, reads=(), writes=()):
        reads, writes = _flat(reads), _flat(writes)
        writes = writes + [r for r in reads if isinstance(r, tuple) and r[0] == "bank"]
        deps = self._deps(reads, writes)
        if eng == "pe":
            deps = [d for d in deps if d[0] != "pe"]
        waits = self._need(eng, deps)
        self.cnt[eng] += 1
        c = self.cnt[eng]
        if eng == "pe":
            self.clock[eng][eng] = c
        snap = dict(self.clock[eng])
        snap[eng] = c
        self.vc[(eng, c)] = snap
        self.ops[eng].append((waits, fn, (eng, 1)))
        self._commit(eng, c, reads, writes)

    def dma(self, queue, chan, fn, reads=(), writes=()):
        if chan not in self.cnt:
            self.cnt[chan] = 0
            self.chans.append(chan)
        reads, writes = _flat(reads), _flat(writes)
        deps = self._deps(reads, writes)
        waits = self._need(queue, deps)
        self.cnt[chan] += 16
        c = self.cnt[chan]
        snap = dict(self.clock[queue])
        snap[chan] = c
        self.vc[(chan, c)] = snap
        self.ops[queue].append((waits, fn, (chan, 16)))
        self._commit(chan, c, reads, writes)

    def emit(self, st):
        nc = self.nc
        names = ["pe", "dve", "act", "pool"] + self.chans
        final = [(tl, self.cnt[tl]) for tl in names if self.cnt.get(tl, 0) > 0]
        sems = {}
        for i, n in enumerate(names):
            sems[n] = st.enter_context(nc.semaphore("s%d" % i))
        block = st.enter_context(nc.Block())
        handles = {"pe": block.tensor, "dve": block.vector, "act": block.scalar,
                   "pool": block.gpsimd, "sp": block.sync}

        def make(engname):
            oplist = self.ops[engname]

            def body(e):
                for waits, fn, inc in oplist:
                    for tl, c in waits:
                        e.wait_ge(sems[tl], c)
                    ins = fn(e)
                    ins.then_inc(sems[inc[0]], inc[1])
                if engname == "sp":
                    for tl, c in final:
                        e.wait_ge(sems[tl], c)
            return body

        for engname in ENGS:
            handles[engname](make(engname))


class Cfg:
    def __init__(self, D=2048, DFF=5632, SEQ=8192, BATCH=2, DEC_BATCH=16, DEPTH=2, NCORES=8):
        self.D, self.DFF, self.SEQ, self.BATCH, self.DEC_BATCH, self.DEPTH = D, DFF, SEQ, BATCH, DEC_BATCH, DEPTH
        self.NCORES = NCORES
        self.NMETA = 16
        self.DEC_SEQ = 16
        self.LW = D // 2
        self.H = (D - self.LW) // 128
        self.KD, self.KF, self.KL = D // 128, DFF // 128, self.LW // 128
        self.NQ = 3 * self.H
        self.N_IN = 2 * self.LW + self.NQ * 128 + self.H * 128 + 2 * self.H
        self.T = 512
        self.NTOK = self.NMETA + SEQ
        assert SEQ % self.T == 0 and DEC_BATCH == 2 * NCORES and BATCH <= NCORES
        self.NBIG = SEQ // self.T
        self.UW = 16 * 128
        self.units = []
        for l in range(DEPTH):
            self.units += self._ffn_units(l, 1)
            for c in range(self.KL):
                self.units.append(("xa", l, c, list(range(self.KD))))
            for j in range(self.NQ):
                self.units.append(("qkv", l, j, list(range(self.KD))))
            self.units.append(("tail", l, 0, list(range(self.KD))))
            for c in range(self.KL):
                self.units.append(("ga", l, c, list(range(self.KD))))
            for h in range(self.H):
                self.units.append(("z", l, h, list(range(self.KD))))
            for m in range(self.KD):
                self.units.append(("wout", l, m, list(range(self.KD))))
            self.units += self._ffn_units(l, 2)
        self.NU = len(self.units)
        self.pcol = {}
        n = 0

        def add(name, w):
            nonlocal n
            self.pcol[name] = (n, w)
            n += w
        for l in range(DEPTH):
            add(("ffn1_norm", l), self.KD)
            add(("mix_norm", l), self.KD)
            add(("ffn2_norm", l), self.KD)
            add(("conv_a_w", l), self.KL * 4)
            add(("conv_a_b", l), self.KL)
            add(("rg_b", l), self.KL)
            add(("ig_b", l), self.KL)
            add(("lam", l), self.KL)
            add(("norm_a", l), self.KL)
            add(("conv_b_w", l), self.NQ * 4)
            add(("norm_b", l), 1)
            add(("a_log", l), 1)
            add(("dt_bias", l), 1)
        add(("final_norm",), self.KD)
        self.NP = n

    def _ffn_units(self, l, which):
        us = []
        for f in range(self.KF):
            us.append(("gate", l, which, f, list(range(self.KD))))
            us.append(("up", l, which, f, list(range(self.KD))))
        for m in range(self.KD):
            ks = list(range(self.KF))
            for i in range(0, self.KF, 16):
                us.append(("down", l, which, m, ks[i:i + 16]))
        return us


def build_program(cfg):
    nc = bass.Bass("TRN2", target_bir_lowering=False)
    D, KD, KF, KL, H, NQ, T, DEPTH = cfg.D, cfg.KD, cfg.KF, cfg.KL, cfg.H, cfg.NQ, cfg.T, cfg.DEPTH
    NTOK, NP = cfg.NTOK, cfg.NP

    def din(name, shape):
        return nc.dram_tensor(name, list(shape), F32, kind="ExternalInput").ap()

    def dout(name, shape):
        return nc.dram_tensor(name, list(shape), F32, kind="ExternalOutput").ap()

    xp = din("xp", [KD, 128, NTOK])
    xs = din("xs", [KD, 128, 32])
    sca = din("sca", [DEPTH, 128, 2, KL, 3])
    slru = din("slru", [DEPTH, 128, 2, KL])
    scb = din("scb", [DEPTH, 128, 2, NQ, 3])
    sdl = din("sdl", [DEPTH, 2, 128, H, 128])
    wstream = din("wstream", [cfg.NU, 128, cfg.UW])
    prm_d = din("prm", [128, NP])
    gw_d = din("gw", [DEPTH, 2, 128, KL, 128])
    cst_d = din("cst", [128, 6, 128])
    xsp = nc.dram_tensor("xsp", [KD, 128, T], F32, kind="Internal").ap()
    yp = dout("yp", [KD, 128, NTOK])
    ys = dout("ys", [KD, 128, 32])
    o_ca = dout("o_ca", [DEPTH, 3, 128, KL, 3])
    o_lru = dout("o_lru", [DEPTH, 3, 128, KL])
    o_cb = dout("o_cb", [DEPTH, 3, 128, NQ, 3])
    o_dl = dout("o_dl", [DEPTH, 3, 128, H, 128])

    P = Prog(nc)
    st = contextlib.ExitStack()
    with st:
        def sb(name, shape, dt=F32):
            return st.enter_context(nc.sbuf_tensor(name, list(shape), dt))

        X = sb("X", [128, KD, T])
        HB = sb("HB", [128, KD, T], BF16)
        NAB = max(KF, NQ + KD)
        AB = sb("AB", [128, NAB, T], BF16)
        NSLOT = 4
        WB = [sb("WB%d" % i, [128, 16, 128], BF16) for i in range(NSLOT)]
        HL = sb("HL", [128, KL, T])
        OO = sb("OO", [128, H, T])
        RAW = sb("RAW", [128, 520])
        NTMP = 16
        TMP = [sb("TMP%d" % i, [128, T]) for i in range(NTMP)]
        BT = sb("BT", [128, 3, T], BF16)
        SQF = sb("SQF", [128, T])
        SP_ = [sb("SP%d" % l, [128, H, 128]) for l in range(DEPTH)]
        HAP = [sb("HAP%d" % l, [128, KL, 3]) for l in range(DEPTH)]
        HBP = [sb("HBP%d" % l, [128, NQ, 3]) for l in range(DEPTH)]
        LHP = [sb("LHP%d" % l, [128, KL]) for l in range(DEPTH)]
        HAS = sb("HAS", [128, 2, KL, 3])
        HBS = sb("HBS", [128, 2, NQ, 3])
        LHS = sb("LHS", [128, 2, KL])
        PRM = sb("PRM", [128, NP])
        DER = sb("DER", [128, DEPTH, KL + 1])
        GW = sb("GW", [128, DEPTH * 2 * KL, 128], BF16)
        CST = sb("CST", [128, 6, 128])
        IDB = sb("IDB", [128, 128], BF16)
        ONB = sb("ONB", [128, 128], BF16)

        PS = [st.enter_context(nc.psum_tensor("PS%d" % i, [128, 512], F32)) for i in range(6)]
        PSBs = [st.enter_context(nc.psum_tensor("PSB%d" % i, [128, 1024], BF16)) for i in range(2)]

        IDENT, MSL, MUI, UTRI, ONES = (CST[:, i, :] for i in range(5))

        def pc(name, j=0, rows=128):
            off, w = cfg.pcol[name]
            return PRM[0:rows, off + j:off + j + 1]

        EPSC = sb("EPSC", [128, 1])
        ONEC = sb("ONEC", [128, 1])
        P.op("dve", lambda e: e.memset(EPSC[:], EPS), writes=["EPSC"])
        P.op("dve", lambda e: e.memset(ONEC[:], 1.0), writes=["EPSC"])
        P.dma("sp", "c_prm", lambda e: e.dma_start(out=PRM[:], in_=prm_d[:, :]), writes=["PRM"])
        P.dma("sp", "c_cst", lambda e: e.dma_start(out=CST[:], in_=cst_d[:, :, :]), writes=["CST"])
        P.dma("pool", "c_gw", lambda e: e.dma_start(
            out=GW[:].rearrange("p (a k) j -> p a k j", k=KL),
            in_=gw_d.rearrange("l g p k j -> p (l g) k j")), writes=["GW"])
        P.op("dve", lambda e: e.tensor_copy(out=IDB[:], in_=IDENT), reads=["CST"], writes=["IDB"])
        P.op("dve", lambda e: e.tensor_copy(out=ONB[:], in_=ONES), reads=["CST"], writes=["ONB"])
        for l in range(DEPTH):
            lo, _ = cfg.pcol[("lam", l)]
            P.op("act", lambda e, l=l, lo=lo: e.activation(out=DER[:, l, 0:KL], in_=PRM[:, lo:lo + KL], func=AF.Exp, scale=-1.0),
                 reads=["PRM"], writes=[("DER", l)])
            P.op("act", lambda e, l=l: e.activation(out=DER[:, l, 0:KL], in_=DER[:, l, 0:KL], func=AF.Ln, bias=ONEC[:, 0:1]),
                 reads=[("DER", l), "EPSC"], writes=[("DER", l)])
            P.op("dve", lambda e, l=l: e.tensor_scalar(out=DER[:, l, 0:KL], in0=DER[:, l, 0:KL], scalar1=-8.0, scalar2=None, op0=ALU.mult),
                 reads=[("DER", l)], writes=[("DER", l)])
            ao, _ = cfg.pcol[("a_log", l)]
            P.op("act", lambda e, l=l, ao=ao: e.activation(out=DER[0:H, l, KL:KL + 1], in_=PRM[0:H, ao:ao + 1], func=AF.Exp),
                 reads=["PRM"], writes=[("DERa", l)])
            P.op("dve", lambda e, l=l: e.tensor_scalar(out=DER[0:H, l, KL:KL + 1], in0=DER[0:H, l, KL:KL + 1], scalar1=-1.0, scalar2=None, op0=ALU.mult),
                 reads=[("DERa", l)], writes=[("DERa", l)])
            P.op("dve", lambda e, l=l: e.memset(SP_[l][:], 0.0), writes=[(("S", l, 0), h) for h in range(H)])
            P.op("dve", lambda e, l=l: e.memset(HAP[l][:], 0.0), writes=[("HA", l, 0)])
            P.op("dve", lambda e, l=l: e.memset(HBP[l][:], 0.0), writes=[("HBh", l, 0)])
            P.op("dve", lambda e, l=l: e.memset(LHP[l][:], 0.0), writes=[("LH", l, 0)])

        wstate = {"gu": 0}

        def next_unit(expect_kind):
            gu = wstate["gu"]
            wstate["gu"] += 1
            u = gu % cfg.NU
            desc = cfg.units[u]
            assert desc[0] == expect_kind, (desc, expect_kind)
            nk = len(desc[-1])
            s = gu % NSLOT
            P.dma("pool", "w%d" % s,
                  lambda e, u=u, s=s, nk=nk: e.dma_start(out=WB[s][:, 0:nk, :],
                                                        in_=wstream[u, :, 0:nk * 128].rearrange("p (k j) -> p k j", j=128)),
                  writes=[("WB", s)])
            return WB[s], ("WB", s), desc

        def x_stats_sq(nt, m):
            b = m % 2
            P.op("act", lambda e: e.activation(out=BT[:, b, 0:nt], in_=X[:, m, 0:nt], func=AF.Square), reads=[("X", m)], writes=[("BT", b)])

        def x_stats_mm(nt, m):
            b = m % 2
            P.op("pe", lambda e: e.matmul(PS[3][:, 0:nt], lhsT=ONB[:], rhs=BT[:, b, 0:nt], start=(m == 0), stop=(m == KD - 1)),
                 reads=[("BT", b), "ONB"], writes=[psr(3)])

        def rmsnorm(nt, wname, dst_bf, dst_res, dst_list=None, have_stats=False):
            ps = PS[3]
            for kc in (range(0) if have_stats else range(KD)):
                b = kc % 2
                P.op("act", lambda e, kc=kc, b=b: e.activation(out=BT[:, b, 0:nt], in_=X[:, kc, 0:nt], func=AF.Square),
                     reads=[("X", kc)], writes=[("BT", b)])
                P.op("pe", lambda e, kc=kc, b=b: e.matmul(ps[:, 0:nt], lhsT=ONB[:], rhs=BT[:, b, 0:nt], start=(kc == 0), stop=(kc == KD - 1)),
                     reads=[("BT", b), "ONB"], writes=[psr(3)])
            rs = TMP[12]
            P.op("act", lambda e: e.activation(out=rs[:, 0:nt], in_=ps[:, 0:nt], func=AF.Ln, scale=1.0 / D, bias=EPSC[:, 0:1]),
                 reads=[psr(3), "EPSC"], writes=[tm(12)])
            P.op("act", lambda e: e.activation(out=rs[:, 0:nt], in_=rs[:, 0:nt], func=AF.Exp, scale=-0.5), reads=[tm(12)], writes=[tm(12)])
            for kc in range(KD):
                if dst_list is not None:
                    P.op("dve", lambda e, kc=kc: e.scalar_tensor_tensor(out=dst_list[kc][0][:, 0:nt], in0=X[:, kc, 0:nt], scalar=pc(wname, kc),
                                                                      in1=rs[:, 0:nt], op0=ALU.mult, op1=ALU.mult),
                         reads=[("X", kc), tm(12), "PRM"], writes=[dst_list[kc][1]])
                else:
                    P.op("dve", lambda e, kc=kc: e.scalar_tensor_tensor(out=dst_bf[:, kc, 0:nt], in0=X[:, kc, 0:nt], scalar=pc(wname, kc),
                                                                      in1=rs[:, 0:nt], op0=ALU.mult, op1=ALU.mult),
                         reads=[("X", kc), tm(12), "PRM"], writes=[(dst_res, kc)])

        def proj_group(ps_ap, ps_res, wt, wres, ks, src, src_res, nt, mcols=slice(0, 128), kmap=None):
            def fn(e):
                ins = None
                n = len(ks)
                for i, k in enumerate(ks):
                    ins = e.matmul(ps_ap, lhsT=wt[:, i, mcols], rhs=src[:, k, 0:nt], start=(i == 0), stop=(i == n - 1))
                return ins
            P.op("pe", fn, reads=[wres] + [(src_res, k) for k in ks], writes=[ps_res])

        def ffn(l, which, nt, have_stats):
            rmsnorm(nt, ("ffn%d_norm" % which, l), HB, "HB", have_stats=have_stats)
            for f in range(KF):
                pg, pu = PS[f % 2], PS[2 + f % 2]
                wt, wres, d = next_unit("gate")
                proj_group(pg[:, 0:nt], psr(f % 2), wt, wres, d[-1], HB, "HB", nt)
                wt, wres, d = next_unit("up")
                proj_group(pu[:, 0:nt], psr(2 + f % 2), wt, wres, d[-1], HB, "HB", nt)
                tb = f % 2
                P.op("act", lambda e, pg=pg, tb=tb: e.activation(out=TMP[tb][:, 0:nt], in_=pg[:, 0:nt], func=AF.Silu),
                     reads=[psr(f % 2)], writes=[tm(tb)])
                P.op("dve", lambda e, pu=pu, tb=tb, f=f: e.tensor_tensor(out=AB[:, f, 0:nt], in0=TMP[tb][:, 0:nt], in1=pu[:, 0:nt], op=ALU.mult),
                     reads=[tm(tb), psr(2 + f % 2)], writes=[("AB", f)])
            for m in range(KD):
                pd = PS[4 + m % 2]
                pres = psr(4 + m % 2)
                nun = (KF + 15) // 16
                parts = [next_unit("down") for _ in range(nun)]

                tot = sum(len(p[2][-1]) for p in parts)
                i0 = 0
                for wt, wres, d in parts:
                    def fn(e, wt=wt, d=d, i0=i0, pd=pd, tot=tot):
                        ins = None
                        for j, k in enumerate(d[-1]):
                            ins = e.matmul(pd[:, 0:nt], lhsT=wt[:, j, :], rhs=AB[:, k, 0:nt], start=(i0 + j == 0), stop=(i0 + j == tot - 1))
                        return ins
                    P.op("pe", fn, reads=[wres] + [("AB", k) for k in d[-1]], writes=[pres])
                    i0 += len(d[-1])
                P.op("dve", lambda e, m=m, pd=pd: e.scalar_tensor_tensor(out=X[:, m, 0:nt], in0=pd[:, 0:nt], scalar=0.5, in1=X[:, m, 0:nt],
                                                                       op0=ALU.mult, op1=ALU.add),
                     reads=[pres, ("X", m)], writes=[("X", m)])
                x_stats_sq(nt, m)
                if m > 0:
                    x_stats_mm(nt, m - 1)
            x_stats_mm(nt, KD - 1)

        def hist_a(l, slot):
            return (HAP[l], ("HA", l, 0)) if slot == 0 else (HAS[:, slot - 1], ("HAS", slot))

        def hist_b(l, slot):
            return (HBP[l], ("HBh", l, 0)) if slot == 0 else (HBS[:, slot - 1], ("HBS", slot))

        def lru_h(l, slot):
            return (LHP[l], ("LH", l, 0)) if slot == 0 else (LHS[:, slot - 1], ("LHS", slot))

        def dstate(l, slot):
            return (SP_[l], ("S", l, 0)) if slot == 0 else (HL[:, :, 64 + 128 * (slot - 1):64 + 128 * slot], ("SS", slot))

        def conv_chunk(ps, pres, nt, segs, L, hist_fn, l, ch, wname, bias_name, out_t, out_res):
            nseg = len(segs)
            Le = L + 3
            rawv = RAW[:, 0:nseg * Le].rearrange("p (s l) -> p s l", l=Le)
            P.op("act", lambda e: e.activation(out=rawv[:, :, 3:Le], in_=ps[:, 0:nt].rearrange("p (s l) -> p s l", l=L), func=AF.Copy),
                 reads=[pres], writes=["RAWd"])
            for si, slot in enumerate(segs):
                hb, hres = hist_fn(l, slot)
                P.op("dve", lambda e, si=si, hb=hb: e.tensor_copy(out=rawv[:, si, 0:3], in_=hb[:, ch, :]),
                     reads=[hres], writes=[("RAWh", si)])
                P.op("dve", lambda e, si=si, hb=hb: e.tensor_copy(out=hb[:, ch, :], in_=rawv[:, si, L:Le]),
                     reads=["RAWd", ("RAWh", si)], writes=[hres])
            woff, _ = cfg.pcol[(wname, l)]
            outv = out_t[:, 0:nt].rearrange("p (s l) -> p s l", l=L)
            rd = ["RAWd"] + [("RAWh", si) for si in range(nseg)] + ["PRM"]
            if bias_name is not None:
                P.op("dve", lambda e: e.tensor_scalar(out=outv, in0=rawv[:, :, 0:L], scalar1=PRM[:, woff + ch * 4:woff + ch * 4 + 1],
                                                     scalar2=pc((bias_name, l), ch), op0=ALU.mult, op1=ALU.add),
                     reads=rd, writes=[out_res])
            else:
                P.op("dve", lambda e: e.tensor_scalar(out=outv, in0=rawv[:, :, 0:L], scalar1=PRM[:, woff + ch * 4:woff + ch * 4 + 1],
                                                     scalar2=None, op0=ALU.mult),
                     reads=rd, writes=[out_res])
            for i in range(1, 4):
                P.op("dve", lambda e, i=i: e.scalar_tensor_tensor(out=outv, in0=rawv[:, :, i:i + L],
                                                                scalar=PRM[:, woff + ch * 4 + i:woff + ch * 4 + i + 1],
                                                                in1=outv, op0=ALU.mult, op1=ALU.add),
                     reads=rd + [out_res], writes=[out_res])

        def mixer(l, nt, segs, L, last):
            nseg = len(segs)
            GT = [TMP[0], TMP[1], TMP[12], SQF]
            GTR = [tm(0), tm(1), tm(12), ["SQF"]]
            rmsnorm(nt, ("mix_norm", l), HB, "HB", have_stats=True)
            P.dma("sp", "c_xs", lambda e: e.dma_start(out=xsp[:, :, 0:nt].rearrange("k p t -> p k t"), in_=X[:, :, 0:nt]),
                  reads=[("X", k) for k in range(KD)], writes=["xsp"])
            if last:
                P.dma("sp", "c_st0", lambda e: e.dma_start(out=HAS[:], in_=sca[l]), writes=[("HAS", 1), ("HAS", 2)])
                P.dma("sp", "c_st1", lambda e: e.dma_start(out=LHS[:], in_=slru[l]), writes=[("LHS", 1), ("LHS", 2)])
                P.dma("sp", "c_st2", lambda e: e.dma_start(out=HBS[:], in_=scb[l]), writes=[("HBS", 1), ("HBS", 2)])
                for s in range(2):
                    P.dma("sp", "c_st%d" % (3 + s), lambda e, s=s: e.dma_start(out=HL[:, :, 64 + 128 * s:192 + 128 * s], in_=sdl[l, s]),
                          writes=[(("SS", s + 1), h) for h in range(H)] + [("HL", c, 0) for c in range(KL)])
            XCs = [TMP[0], TMP[14], X[:, 10, :]]
            XCr = [tm(0), tm(14), [("X", 10)]]
            SGs = [TMP[1], TMP[15]]
            SGr = [tm(1), tm(15)]
            Rt, IGt, At, A2t, Bt = TMP[1], TMP[2], TMP[3], TMP[4], TMP[5]
            RSA = TMP[6]
            stages = []

            def xa_proj(c, bk, par):
                wt, wres, d = next_unit("xa")
                proj_group(PS[bk][:, 0:nt], psr(bk), wt, wres, d[-1], HB, "HB", nt)

            def xa_a1(c, bk, par):
                XC, rXC = XCs[par % 3], XCr[par % 3]
                conv_chunk(PS[bk], psr(bk), nt, segs, L, hist_a, l, c, "conv_a_w", "conv_a_b", XC, rXC)

            def xa_a2(c, bk, par):
                XC, rXC = XCs[par % 3], XCr[par % 3]
                P.op("act", lambda e: e.activation(out=BT[:, 2, 0:nt], in_=XC[:, 0:nt], func=AF.Copy), reads=[rXC], writes=[("BT", 2)])
                gi = (l * 2 + 0) * KL + c
                P.op("pe", lambda e: e.matmul(PS[0][:, 0:nt], lhsT=GW[:, gi, :], rhs=BT[:, 2, 0:nt], start=True, stop=True),
                     reads=[("BT", 2), "GW"], writes=[psr(0)])
                gi2 = (l * 2 + 1) * KL + c
                P.op("pe", lambda e: e.matmul(PS[2][:, 0:nt], lhsT=GW[:, gi2, :], rhs=BT[:, 2, 0:nt], start=True, stop=True),
                     reads=[("BT", 2), "GW"], writes=[psr(2)])
                P.op("act", lambda e: e.activation(out=Rt[:, 0:nt], in_=PS[0][:, 0:nt], func=AF.Sigmoid, bias=pc(("rg_b", l), c)),
                     reads=[psr(0), "PRM"], writes=[tm(1)])
                P.op("act", lambda e: e.activation(out=IGt[:, 0:nt], in_=PS[2][:, 0:nt], func=AF.Sigmoid, bias=pc(("ig_b", l), c)),
                     reads=[psr(2), "PRM"], writes=[tm(2)])
                P.op("act", lambda e: e.activation(out=At[:, 0:nt], in_=Rt[:, 0:nt], func=AF.Exp, scale=DER[:, l, c:c + 1]),
                     reads=[tm(1), ("DER", l)], writes=[tm(3)])
                P.op("act", lambda e: e.activation(out=A2t[:, 0:nt], in_=At[:, 0:nt], func=AF.Square), reads=[tm(3)], writes=[tm(4)])
                P.op("act", lambda e: e.activation(out=A2t[:, 0:nt], in_=A2t[:, 0:nt], func=AF.Ln, scale=-1.0, bias=ONEC[:, 0:1]),
                     reads=[tm(4), "EPSC"], writes=[tm(4)])
                P.op("act", lambda e: e.activation(out=A2t[:, 0:nt], in_=A2t[:, 0:nt], func=AF.Exp, scale=0.5), reads=[tm(4)], writes=[tm(4)])

            def xa_b(c, bk, par):
                XC, rXC = XCs[par % 3], XCr[par % 3]
                P.op("dve", lambda e: e.tensor_tensor(out=Bt[:, 0:nt], in0=IGt[:, 0:nt], in1=XC[:, 0:nt], op=ALU.mult),
                     reads=[tm(2), rXC], writes=[tm(5)])
                P.op("dve", lambda e: e.tensor_tensor(out=Bt[:, 0:nt], in0=Bt[:, 0:nt], in1=A2t[:, 0:nt], op=ALU.mult),
                     reads=[tm(5), tm(4)], writes=[tm(5)])
                for si, slot in enumerate(segs):
                    hb, hres = lru_h(l, slot)
                    cs = slice(si * L, (si + 1) * L)
                    P.op("dve", lambda e, cs=cs, hb=hb: e.tensor_tensor_scan(out=HL[:, c, cs], data0=At[:, cs], data1=Bt[:, cs],
                                                                         initial=hb[:, c:c + 1], op0=ALU.mult, op1=ALU.add),
                         reads=[tm(3), tm(5), hres], writes=[("HL", c, si)])
                    P.op("dve", lambda e, hb=hb, si=si: e.tensor_copy(out=hb[:, c:c + 1], in_=HL[:, c, (si + 1) * L - 1:(si + 1) * L]),
                         reads=[("HL", c, si)], writes=[hres])
                if c == KL - 1:
                    lru_stats()

            def lru_stats():
                for c in range(KL):
                    b = c % 2
                    P.op("act", lambda e, c=c, b=b: e.activation(out=BT[:, b, 0:nt], in_=HL[:, c, 0:nt], func=AF.Square),
                         reads=[("HL", c, si) for si in range(nseg)], writes=[("BT", b)])
                    P.op("pe", lambda e, c=c, b=b: e.matmul(PS[3][:, 0:nt], lhsT=ONB[:], rhs=BT[:, b, 0:nt], start=(c == 0), stop=(c == KL - 1)),
                         reads=[("BT", b), "ONB"], writes=[psr(3)])
                P.op("act", lambda e: e.activation(out=RSA[:, 0:nt], in_=PS[3][:, 0:nt], func=AF.Ln, scale=1.0 / cfg.LW, bias=EPSC[:, 0:1]),
                     reads=[psr(3), "EPSC"], writes=[tm(6)])
                P.op("act", lambda e: e.activation(out=RSA[:, 0:nt], in_=RSA[:, 0:nt], func=AF.Exp, scale=-0.5), reads=[tm(6)], writes=[tm(6)])
            for c in range(KL):
                stages.append((xa_proj, xa_a1, xa_a2, xa_b, c))

            def qkv_proj(j, bk, par):
                wt, wres, d = next_unit("qkv")
                proj_group(PS[bk][:, 0:nt], psr(bk), wt, wres, d[-1], HB, "HB", nt)

            def qkv_a1(j, bk, par):
                XC, rXC, SG, rSG = XCs[par % 3], XCr[par % 3], SGs[par % 2], SGr[par % 2]
                conv_chunk(PS[bk], psr(bk), nt, segs, L, hist_b, l, j, "conv_b_w", None, XC, rXC)

            def qkv_a2(j, bk, par):
                XC, rXC, SG, rSG = XCs[par % 3], XCr[par % 3], SGs[par % 2], SGr[par % 2]
                P.op("act", lambda e: e.activation(out=SG[:, 0:nt], in_=XC[:, 0:nt], func=AF.Exp, scale=-1.0), reads=[rXC], writes=[rSG])
                P.op("act", lambda e: e.activation(out=SG[:, 0:nt], in_=SG[:, 0:nt], func=AF.Ln, bias=ONEC[:, 0:1]), reads=[rSG, "EPSC"], writes=[rSG])
                P.op("act", lambda e: e.activation(out=SG[:, 0:nt], in_=SG[:, 0:nt], func=AF.Exp, scale=-1.0), reads=[rSG], writes=[rSG])
                if j < 2 * H:
                    nb = [3, 0][par % 2]
                    P.op("dve", lambda e: e.tensor_tensor(out=XC[:, 0:nt], in0=XC[:, 0:nt], in1=SG[:, 0:nt], op=ALU.mult), reads=[rXC, rSG], writes=[rXC])
                    P.op("act", lambda e: e.activation(out=SQF[:, 0:nt], in_=XC[:, 0:nt], func=AF.Square), reads=[rXC], writes=["SQF"])
                    P.op("pe", lambda e: e.matmul(PS[nb][:, 0:nt], lhsT=ONES, rhs=SQF[:, 0:nt], start=True, stop=True),
                         reads=["SQF", "CST"], writes=[psr(nb)])
                else:
                    P.op("dve", lambda e: e.tensor_tensor(out=AB[:, j, 0:nt], in0=XC[:, 0:nt], in1=SG[:, 0:nt], op=ALU.mult),
                         reads=[rXC, rSG], writes=[("AB", j)])

            def qkv_b(j, bk, par):
                if j >= 2 * H:
                    return
                XC, rXC, SG, rSG = XCs[par % 3], XCr[par % 3], SGs[par % 2], SGr[par % 2]
                nb = [3, 0][par % 2]
                P.op("act", lambda e: e.activation(out=SG[:, 0:nt], in_=PS[nb][:, 0:nt], func=AF.Ln, bias=EPSC[:, 0:1]),
                     reads=[psr(nb), "EPSC"], writes=[rSG])
                P.op("act", lambda e: e.activation(out=SG[:, 0:nt], in_=SG[:, 0:nt], func=AF.Exp, scale=-0.5), reads=[rSG], writes=[rSG])
                sc = (128.0 ** -0.5) if j < H else 1.0
                P.op("dve", lambda e: e.scalar_tensor_tensor(out=AB[:, j, 0:nt], in0=XC[:, 0:nt], scalar=sc, in1=SG[:, 0:nt],
                                                           op0=ALU.mult, op1=ALU.mult),
                     reads=[rXC, rSG], writes=[("AB", j)])
            for j in range(NQ):
                stages.append((qkv_proj, qkv_a1, qkv_a2, qkv_b, j))

            def tail_proj(_, bk, par):
                wt, wres, d = next_unit("tail")
                proj_group(PS[1][0:H, 0:nt], psr(1), wt, wres, d[-1], HB, "HB", nt, mcols=slice(0, H))
                proj_group(PS[2][0:H, 0:nt], psr(2), wt, wres, d[-1], HB, "HB", nt, mcols=slice(H, 2 * H))

            def tail_b(_, bk, par):
                P.op("act", lambda e: e.activation(out=GT[0][0:H, 0:nt], in_=PS[1][0:H, 0:nt], func=AF.Sigmoid), reads=[psr(1)], writes=[GTR[0]])
                P.op("act", lambda e: e.activation(out=GT[1][0:H, 0:nt], in_=PS[2][0:H, 0:nt], func=AF.Exp, bias=pc(("dt_bias", l), 0, H)),
                     reads=[psr(2), "PRM"], writes=[GTR[1]])
                P.op("act", lambda e: e.activation(out=GT[1][0:H, 0:nt], in_=GT[1][0:H, 0:nt], func=AF.Ln, bias=ONEC[0:H, 0:1]),
                     reads=[GTR[1], "EPSC"], writes=[GTR[1]])
                P.op("dve", lambda e: e.tensor_scalar(out=GT[1][0:H, 0:nt], in0=GT[1][0:H, 0:nt], scalar1=DER[0:H, l, KL:KL + 1], scalar2=None, op0=ALU.mult),
                     reads=[GTR[1], ("DERa", l)], writes=[GTR[1]])
            stages.append((tail_proj, None, None, tail_b, 0))

            def run_pipeline(stages):
                n = len(stages)

                def call(i, k):
                    if 0 <= i < n and stages[i][k] is not None:
                        stages[i][k](stages[i][4], 4 + i % 2, i)
                call(0, 0)
                call(1, 0)
                call(0, 1)
                call(2, 0)
                call(1, 1)
                call(0, 2)
                for i in range(n):
                    call(i + 3, 0)
                    call(i + 2, 1)
                    call(i, 3)
                    call(i + 1, 2)
            run_pipeline(stages)

            C = min(L, 128)
            nch = L // C
            nsq = max(1, int(np.ceil(np.log2(C))) - 1)
            HG = max(1, min(4, H // 2))
            assert H // HG == 2 and H % HG == 0 and KD >= 10
            Gsets = [[TMP[2], TMP[3], TMP[4], TMP[5], TMP[8], TMP[9], TMP[10], TMP[11], TMP[13]], [X[:, k, :] for k in range(9)]]
            GRsets = [[tm(2), tm(3), tm(4), tm(5), tm(8), tm(9), tm(10), tm(11), tm(13)], [[("X", k)] for k in range(9)]]
            SMs = [TMP[7], X[:, 9, :]]
            SMr = [tm(7), [("X", 9)]]

            def v3(t):
                return t[:, 0:HG * 128].rearrange("p (h c) -> p h c", c=128)

            def bcl(ap2, n):
                return ap2.unsqueeze(2).to_broadcast([ap2.shape[0], ap2.shape[1], n])

            def bcm(ap2, n):
                a = [list(x) for x in ap2.ap]
                return APc(ap2.tensor, ap2.offset, [a[0], [0, n], a[1]])
            def do_chunk(si, slot, ci, cidx):
                Sb, Sres = dstate(l, slot)
                SM, rSM = SMs[cidx % 2], SMr[cidx % 2]
                pre = []

                def OP(eng, fn, reads=(), writes=()):
                    pre.append((eng, fn, _flat(list(reads)), _flat(list(writes))))
                c0 = si * L + ci * C
                cs = slice(c0, c0 + C)
                OP("dve", lambda e, cs=cs: e.tensor_tensor_scan(out=GT[2][0:H, cs], data0=ONES[0:H, 0:C], data1=GT[1][0:H, cs],
                                                                 initial=0.0, op0=ALU.mult, op1=ALU.add),
                     reads=[GTR[1], "CST"], writes=[GTR[2]])
                OP("act", lambda e, cs=cs: e.activation(out=GT[3][0:H, cs], in_=GT[2][0:H, cs], func=AF.Exp), reads=[GTR[2]], writes=[GTR[3]])
                pss = PS[2]
                b3 = psr(2)

                def trfn(e, cs=cs):
                    e.transpose(out=pss[0:C, 0:H], in_=GT[0][0:H, cs], identity=IDENT[0:H, 0:H])
                    return e.transpose(out=pss[0:C, 8:8 + H], in_=GT[1][0:H, cs], identity=IDENT[0:H, 0:H])
                OP("pe", trfn, reads=[GTR[0], GTR[1], "CST"], writes=[b3])
                OP("dve", lambda e: e.tensor_copy(out=SM[0:C, 0:H], in_=pss[0:C, 0:H]), reads=[b3], writes=[rSM])
                OP("dve", lambda e: e.tensor_copy(out=SM[0:C, H:2 * H], in_=pss[0:C, 8:8 + H]), reads=[b3], writes=[rSM])

                def cumfn(e):
                    e.matmul(pss[0:C, 16:16 + H], lhsT=UTRI[0:C, 0:C], rhs=SM[0:C, H:2 * H], start=True, stop=True)
                    return e.matmul(pss[:, 24:24 + H], lhsT=ONES[0:C, :], rhs=SM[0:C, H:2 * H], start=True, stop=True)
                OP("pe", cumfn, reads=[rSM, "CST"], writes=[b3])
                OP("dve", lambda e: e.tensor_copy(out=SM[0:C, 2 * H:3 * H], in_=pss[0:C, 16:16 + H]), reads=[b3], writes=[rSM])
                OP("act", lambda e: e.activation(out=SM[0:C, 3 * H:4 * H], in_=SM[0:C, 2 * H:3 * H], func=AF.Exp), reads=[rSM], writes=[rSM])
                OP("dve", lambda e: e.tensor_tensor(out=SM[0:C, 4 * H:5 * H], in0=pss[0:C, 24:24 + H], in1=SM[0:C, 2 * H:3 * H], op=ALU.subtract),
                     reads=[b3, rSM], writes=[rSM])
                OP("act", lambda e: e.activation(out=SM[0:C, 4 * H:5 * H], in_=SM[0:C, 4 * H:5 * H], func=AF.Exp), reads=[rSM], writes=[rSM])
                OP("dve", lambda e: e.tensor_tensor(out=SM[0:C, 5 * H:6 * H], in0=SM[0:C, 0:H], in1=SM[0:C, 3 * H:4 * H], op=ALU.mult),
                     reads=[rSM], writes=[rSM])
                OP("act", lambda e: e.activation(out=SM[:, 6 * H:7 * H], in_=pss[:, 24:24 + H], func=AF.Exp), reads=[b3], writes=[rSM])
                def do_group(h0, gset):
                    ops = []

                    def OP(eng, fn, reads=(), writes=()):
                        ops.append((eng, fn, _flat(list(reads)), _flat(list(writes))))
                    G, GR = Gsets[gset], GRsets[gset]
                    hs = list(range(h0, h0 + HG))
                    ia, ib, ic = (0, 1, 2) if gset == 0 else (3, 4, 5)
                    Ba, Bb, Bc_ = v3(PS[ia]), v3(PS[ib]), v3(PS[ic])
                    ra, rb_, rc = psr(ia), psr(ib), psr(ic)
                    PBt = PSBs[gset]
                    PTK = PBt[:, 0:HG * 128].rearrange("p (h c) -> p h c", c=128)
                    PTV = PBt[:, 512:512 + HG * 128].rearrange("p (h c) -> p h c", c=128)
                    r7 = psr(6 + gset)
                    Gv = [v3(g) for g in G]
                    rK = [("AB", H + h) for h in hs]
                    rQ = [("AB", h) for h in hs]
                    rV = [("AB", 2 * H + h) for h in hs]

                    def smc(k):
                        return SM[0:C, k * H + h0:k * H + h0 + HG]

                    def mm_each(fnh):
                        def fn(e):
                            ins = None
                            for gi, h in enumerate(hs):
                                ins = fnh(e, gi, h)
                            return ins
                        return fn
                    W3 = (slice(0, C), slice(0, HG), slice(0, C))
                    WF = (slice(0, C), slice(0, HG), slice(0, 128))
                    WT_ = (slice(0, 128), slice(0, HG), slice(0, C))
                    WS_ = (slice(0, 128), slice(0, HG), slice(0, 128))
                    Dm, DT = Gv[0], Gv[1]
                    OP("pe", mm_each(lambda e, gi, h: e.matmul(Bc_[0:C, gi, 0:C], lhsT=IDENT[0:H, h:h + 1].to_broadcast([H, C]), rhs=GT[2][0:H, cs], start=True, stop=True)),
                       reads=[GTR[2], "CST"], writes=[rc])
                    OP("pe", mm_each(lambda e, gi, h: e.matmul(Ba[0:C, gi, 0:C], lhsT=AB[:, H + h, cs], rhs=AB[:, H + h, cs], start=True, stop=True)),
                       reads=[rK], writes=[ra])
                    OP("pe", mm_each(lambda e, gi, h: e.matmul(Bb[0:C, gi, 0:C], lhsT=AB[:, H + h, cs], rhs=AB[:, h, cs], start=True, stop=True)),
                       reads=[rK, rQ], writes=[rb_])
                    OP("dve", lambda e: e.tensor_tensor(out=Dm[W3], in0=Bc_[W3], in1=bcl(smc(2), C), op=ALU.subtract), reads=[rc, rSM], writes=[GR[0]])
                    OP("dve", lambda e: e.tensor_scalar(out=DT[W3], in0=Dm[W3], scalar1=0.0, scalar2=None, op0=ALU.min), reads=[GR[0]], writes=[GR[1]])
                    OP("dve", lambda e: e.tensor_scalar(out=Dm[W3], in0=Dm[W3], scalar1=0.0, scalar2=None, op0=ALU.max), reads=[GR[0]], writes=[GR[0]])
                    OP("act", lambda e: e.activation(out=Dm[W3], in_=Dm[W3], func=AF.Exp, scale=-1.0), reads=[GR[0]], writes=[GR[0]])
                    OP("act", lambda e: e.activation(out=DT[W3], in_=DT[W3], func=AF.Exp), reads=[GR[1]], writes=[GR[1]])
                    OP("dve", lambda e: e.tensor_tensor(out=Dm[W3], in0=Dm[W3], in1=bcm(MSL[0:C, 0:C], HG), op=ALU.mult), reads=[GR[0], "CST"], writes=[GR[0]])
                    OP("dve", lambda e: e.tensor_tensor(out=DT[W3], in0=DT[W3], in1=bcm(MUI[0:C, 0:C], HG), op=ALU.mult), reads=[GR[1], "CST"], writes=[GR[1]])
                    OP("dve", lambda e: e.tensor_tensor(out=Dm[W3], in0=Dm[W3], in1=bcl(smc(0), C), op=ALU.mult), reads=[GR[0], rSM], writes=[GR[0]])
                    OP("dve", lambda e: e.tensor_tensor(out=DT[W3], in0=Bb[W3], in1=DT[W3], op=ALU.mult), reads=[rb_, GR[1]], writes=[GR[1]])
                    Ac, rAc, An, rAn = Gv[2], GR[2], Gv[3], GR[3]
                    Bc, rBc, Bn, rBn = Gv[4], GR[4], Gv[5], GR[5]
                    Qc, rQc, Qn, rQn = Gv[6], GR[6], Gv[7], GR[7]
                    OP("dve", lambda e, Ac=Ac: e.tensor_tensor(out=Ac[W3], in0=Ba[W3], in1=Dm[W3], op=ALU.mult), reads=[ra, GR[0]], writes=[rAc])
                    OP("pe", mm_each(lambda e, gi, h, Ac=Ac: e.transpose(out=Ba[0:C, gi, 0:C], in_=Ac[0:C, gi, 0:C], identity=IDENT[0:C, 0:C])),
                       reads=[rAc, "CST"], writes=[ra])
                    OP("act", lambda e, Bc=Bc: e.activation(out=Bc[W3], in_=Ba[W3], func=AF.Copy), reads=[ra], writes=[rBc])
                    OP("dve", lambda e, Qc=Qc: e.tensor_tensor(out=Qc[W3], in0=bcm(IDENT[0:C, 0:C], HG), in1=Ba[W3], op=ALU.subtract), reads=[ra, "CST"], writes=[rQc])
                    for jq in range(1, nsq + 1):
                        OP("pe", mm_each(lambda e, gi, h, Bc=Bc, Ac=Ac: e.matmul(Ba[0:C, gi, 0:C], lhsT=Bc[0:C, gi, 0:C], rhs=Ac[0:C, gi, 0:C], start=True, stop=True)),
                           reads=[rBc, rAc], writes=[ra])
                        if jq < nsq:
                            OP("pe", mm_each(lambda e, gi, h, Bc=Bc, Ac=Ac: e.matmul(Bc_[0:C, gi, 0:C], lhsT=Ac[0:C, gi, 0:C], rhs=Bc[0:C, gi, 0:C], start=True, stop=True)),
                               reads=[rBc, rAc], writes=[rc])
                        OP("act", lambda e, An=An: e.activation(out=An[W3], in_=Ba[W3], func=AF.Copy), reads=[ra], writes=[rAn])
                        if jq < nsq:
                            OP("dve", lambda e, Bn=Bn: e.tensor_copy(out=Bn[W3], in_=Bc_[W3]), reads=[rc], writes=[rBn])
                        OP("pe", mm_each(lambda e, gi, h, An=An, Qc=Qc: e.matmul(Bb[0:C, gi, 0:C], lhsT=An[0:C, gi, 0:C], rhs=Qc[0:C, gi, 0:C], start=True, stop=True)),
                           reads=[rAn, rQc], writes=[rb_])
                        OP("dve", lambda e, Qn=Qn, Qc=Qc: e.tensor_tensor(out=Qn[W3], in0=Qc[W3], in1=Bb[W3], op=ALU.add), reads=[rQc, rb_], writes=[rQn])
                        Ac, rAc, An, rAn = An, rAn, Ac, rAc
                        Bc, rBc, Bn, rBn = Bn, rBn, Bc, rBc
                        Qc, rQc, Qn, rQn = Qn, rQn, Qc, rQc
                    RK, rRK, KE, rKE = Gv[2], GR[2], Gv[3], GR[3]
                    Ut, rUt, WK, rWK = Gv[4], GR[4], Gv[5], GR[5]
                    QD, rQD = Qn, rQn
                    VB, rVB = Gv[0], GR[0]
                    Wt, rWt = Gv[8], GR[8]
                    QK, rQK = DT, GR[1]

                    def tkv(e):
                        ins = None
                        for gi, h in enumerate(hs):
                            e.transpose(out=PTK[0:C, gi, :], in_=AB[:, H + h, cs], identity=IDB[:])
                            ins = e.transpose(out=PTV[0:C, gi, :], in_=AB[:, 2 * H + h, cs], identity=IDB[:])
                        return ins
                    OP("pe", tkv, reads=[rK, rV, "IDB"], writes=[r7])
                    OP("dve", lambda e: e.tensor_tensor(out=RK[WF], in0=PTK[WF], in1=bcl(smc(5), 128), op=ALU.mult), reads=[r7, rSM], writes=[rRK])
                    OP("dve", lambda e: e.tensor_tensor(out=KE[WF], in0=PTK[WF], in1=bcl(smc(4), 128), op=ALU.mult), reads=[r7, rSM], writes=[rKE])
                    OP("dve", lambda e: e.tensor_tensor(out=VB[WF], in0=PTV[WF], in1=bcl(smc(0), 128), op=ALU.mult), reads=[r7, rSM], writes=[rVB])
                    OP("pe", mm_each(lambda e, gi, h: e.matmul(Ba[0:C, gi, :], lhsT=Qc[0:C, gi, 0:C], rhs=VB[0:C, gi, :], start=True, stop=True)),
                       reads=[rQc, rVB], writes=[ra])
                    OP("act", lambda e: e.activation(out=Ut[WF], in_=Ba[WF], func=AF.Copy), reads=[ra], writes=[rUt])
                    OP("pe", mm_each(lambda e, gi, h: e.matmul(Bc_[:, gi, 0:C], lhsT=RK[0:C, gi, :], rhs=Qc[0:C, gi, 0:C], start=True, stop=True)),
                       reads=[rRK, rQc], writes=[rc])
                    OP("act", lambda e: e.activation(out=WK[WT_], in_=Bc_[WT_], func=AF.Copy), reads=[rc], writes=[rWK])
                    OP("pe", mm_each(lambda e, gi, h: e.matmul(Bb[:, gi, 0:C], lhsT=IDENT[0:H, h:h + 1].to_broadcast([H, 128]), rhs=GT[3][0:H, cs], start=True, stop=True)),
                       reads=[GTR[3], "CST"], writes=[rb_])
                    OP("dve", lambda e: e.tensor_tensor(out=QD[WT_], in0=AB[:, h0:h0 + HG, cs], in1=Bb[WT_], op=ALU.mult), reads=[rQ, rb_], writes=[rQD])
                    rS = [(Sres, h) for h in hs]
                    Sg = Sb[:, h0:h0 + HG, :]
                    OP("pe", mm_each(lambda e, gi, h: e.matmul(Ba[0:C, gi, :], lhsT=WK[:, gi, 0:C], rhs=Sb[:, h, :], start=True, stop=True)),
                       reads=[rWK, rS], writes=[ra])
                    OP("dve", lambda e: e.tensor_tensor(out=Wt[WF], in0=Ut[WF], in1=Ba[WF], op=ALU.subtract), reads=[rUt, ra], writes=[rWt])

                    def ofn(e):
                        ins = None
                        for gi, h in enumerate(hs):
                            e.matmul(Bc_[:, gi, 0:C], lhsT=Sb[:, h, :], rhs=QD[:, gi, 0:C], start=True, stop=False)
                            ins = e.matmul(Bc_[:, gi, 0:C], lhsT=Wt[0:C, gi, :], rhs=QK[0:C, gi, 0:C], start=False, stop=True)
                        return ins
                    OP("pe", ofn, reads=[rS, rQD, rWt, rQK], writes=[rc])
                    OP("act", lambda e: e.activation(out=OO[:, h0:h0 + HG, cs], in_=Bc_[WT_], func=AF.Copy), reads=[rc],
                       writes=[("OO", h, si, ci) for h in hs])
                    OP("pe", mm_each(lambda e, gi, h: e.matmul(Bb[:, gi, :], lhsT=KE[0:C, gi, :], rhs=Wt[0:C, gi, :], start=True, stop=True)),
                       reads=[rKE, rWt], writes=[rb_])
                    OP("dve", lambda e: e.tensor_tensor(out=Sg, in0=Sg, in1=bcl(SM[:, 6 * H + h0:6 * H + h0 + HG], 128), op=ALU.mult), reads=[rS, rSM], writes=[rS])
                    OP("dve", lambda e: e.tensor_tensor(out=Sg, in0=Sg, in1=Bb[WS_], op=ALU.add), reads=[rS, rb_], writes=[rS])
                    return ops
                return [pre + do_group(0, 0)] + [do_group(h0, gi) for gi, h0 in list(enumerate(range(0, H, HG)))[1:]], len(pre)

            allops = []
            cidx = 0
            for si_, slot_ in enumerate(segs):
                for ci_ in range(nch):
                    lists, npre = do_chunk(si_, slot_, ci_, cidx)
                    period = len(lists[0])
                    for gi_, lst in enumerate(lists):
                        off = 0 if gi_ == 0 else npre + 16
                        for k_, op_ in enumerate(lst):
                            allops.append((cidx * period + off + k_, gi_, op_))
                    cidx += 1
            allops.sort(key=lambda t: (t[0], t[1]))
            for _, _, (eng_, fn_, rd_, wr_) in allops:
                P.op(eng_, fn_, reads=rd_, writes=wr_)
            oo_res = lambda h: [("OO", h, si, ci) for si in range(nseg) for ci in range(nch)]

            for c in range(KL):
                wt, wres, d = next_unit("ga")
                po = PS[4 + c % 2]
                pres = psr(4 + c % 2)
                proj_group(po[:, 0:nt], pres, wt, wres, d[-1], HB, "HB", nt)
                G1, G2 = TMP[0], TMP[1]
                P.op("act", lambda e, po=po: e.activation(out=G1[:, 0:nt], in_=po[:, 0:nt], func=AF.Square), reads=[pres], writes=[tm(0)])
                P.op("dve", lambda e: e.tensor_scalar(out=G1[:, 0:nt], in0=G1[:, 0:nt], scalar1=0.044715, scalar2=1.0, op0=ALU.mult, op1=ALU.add),
                     reads=[tm(0)], writes=[tm(0)])
                P.op("dve", lambda e, po=po: e.tensor_tensor(out=G1[:, 0:nt], in0=G1[:, 0:nt], in1=po[:, 0:nt], op=ALU.mult), reads=[tm(0), pres], writes=[tm(0)])
                P.op("act", lambda e: e.activation(out=G1[:, 0:nt], in_=G1[:, 0:nt], func=AF.Sigmoid, scale=1.5957691216057308), reads=[tm(0)], writes=[tm(0)])
                P.op("dve", lambda e, po=po: e.tensor_tensor(out=G1[:, 0:nt], in0=G1[:, 0:nt], in1=po[:, 0:nt], op=ALU.mult), reads=[tm(0), pres], writes=[tm(0)])
                P.op("dve", lambda e, c=c: e.scalar_tensor_tensor(out=G2[:, 0:nt], in0=HL[:, c, 0:nt], scalar=pc(("norm_a", l), c), in1=RSA[:, 0:nt],
                                                               op0=ALU.mult, op1=ALU.mult),
                     reads=[("HL", c, si) for si in range(nseg)] + [tm(6), "PRM"], writes=[tm(1)])
                P.op("dve", lambda e, c=c: e.tensor_tensor(out=AB[:, NQ + c, 0:nt], in0=G1[:, 0:nt], in1=G2[:, 0:nt], op=ALU.mult),
                     reads=[tm(0), tm(1)], writes=[("AB", NQ + c)])
            for h in range(H):
                Z1, Z2 = TMP[0], TMP[1]
                P.op("act", lambda e, h=h: e.activation(out=SQF[:, 0:nt], in_=OO[:, h, 0:nt], func=AF.Square), reads=oo_res(h), writes=["SQF"])
                P.op("pe", lambda e: e.matmul(PS[3][:, 0:nt], lhsT=ONES, rhs=SQF[:, 0:nt], start=True, stop=True), reads=["SQF", "CST"], writes=[psr(3)])
                wt, wres, d = next_unit("z")
                po = PS[4 + h % 2]
                pres = psr(4 + h % 2)
                proj_group(po[:, 0:nt], pres, wt, wres, d[-1], HB, "HB", nt)
                P.op("act", lambda e: e.activation(out=Z2[:, 0:nt], in_=PS[3][:, 0:nt], func=AF.Ln, scale=1.0 / 128.0, bias=EPSC[:, 0:1]),
                     reads=[psr(3), "EPSC"], writes=[tm(1)])
                P.op("act", lambda e: e.activation(out=Z2[:, 0:nt], in_=Z2[:, 0:nt], func=AF.Exp, scale=-0.5), reads=[tm(1)], writes=[tm(1)])
                P.op("dve", lambda e, h=h: e.scalar_tensor_tensor(out=Z2[:, 0:nt], in0=OO[:, h, 0:nt], scalar=pc(("norm_b", l), 0), in1=Z2[:, 0:nt],
                                                               op0=ALU.mult, op1=ALU.mult),
                     reads=oo_res(h) + [tm(1), "PRM"], writes=[tm(1)])
                P.op("act", lambda e, po=po: e.activation(out=Z1[:, 0:nt], in_=po[:, 0:nt], func=AF.Exp, scale=-1.0), reads=[pres], writes=[tm(0)])
                P.op("act", lambda e: e.activation(out=Z1[:, 0:nt], in_=Z1[:, 0:nt], func=AF.Ln, bias=ONEC[:, 0:1]), reads=[tm(0), "EPSC"], writes=[tm(0)])
                P.op("act", lambda e: e.activation(out=Z1[:, 0:nt], in_=Z1[:, 0:nt], func=AF.Exp, scale=-1.0), reads=[tm(0)], writes=[tm(0)])
                P.op("dve", lambda e, po=po: e.tensor_tensor(out=Z1[:, 0:nt], in0=Z1[:, 0:nt], in1=po[:, 0:nt], op=ALU.mult), reads=[tm(0), pres], writes=[tm(0)])
                P.op("dve", lambda e, h=h: e.tensor_tensor(out=AB[:, NQ + KL + h, 0:nt], in0=Z1[:, 0:nt], in1=Z2[:, 0:nt], op=ALU.mult),
                     reads=[tm(0), tm(1)], writes=[("AB", NQ + KL + h)])
            P.dma("sp", "c_xl", lambda e: e.dma_start(out=X[:, :, 0:nt], in_=xsp[:, :, 0:nt].rearrange("k p t -> p k t")),
                  reads=["xsp"], writes=[("X", k) for k in range(KD)])
            for m in range(KD):
                wt, wres, d = next_unit("wout")
                pd = PS[4 + m % 2]
                pres = psr(4 + m % 2)

                def fn(e, wt=wt, pd=pd):
                    ins = None
                    for k in range(KD):
                        ins = e.matmul(pd[:, 0:nt], lhsT=wt[:, k, :], rhs=AB[:, NQ + k, 0:nt], start=(k == 0), stop=(k == KD - 1))
                    return ins
                P.op("pe", fn, reads=[wres] + [("AB", NQ + k) for k in range(KD)], writes=[pres])
                P.op("dve", lambda e, m=m, pd=pd: e.tensor_tensor(out=X[:, m, 0:nt], in0=X[:, m, 0:nt], in1=pd[:, 0:nt], op=ALU.add),
                     reads=[pres, ("X", m)], writes=[("X", m)])
                x_stats_sq(nt, m)
                if m > 0:
                    x_stats_mm(nt, m - 1)
            x_stats_mm(nt, KD - 1)
            if last:
                for slot in range(3):
                    hb, hres = hist_a(l, slot)
                    P.dma("sp", "c_o0_%d_%d" % (l, slot), lambda e, hb=hb, slot=slot: e.dma_start(out=o_ca[l, slot], in_=hb[:, :, :]), reads=[hres], writes=[("o_ca", l, slot)])
                    hb, hres = lru_h(l, slot)
                    P.dma("sp", "c_o1_%d_%d" % (l, slot), lambda e, hb=hb, slot=slot: e.dma_start(out=o_lru[l, slot], in_=hb[:, :]), reads=[hres], writes=[("o_lru", l, slot)])
                    hb, hres = hist_b(l, slot)
                    P.dma("sp", "c_o2_%d_%d" % (l, slot), lambda e, hb=hb, slot=slot: e.dma_start(out=o_cb[l, slot], in_=hb[:, :, :]), reads=[hres], writes=[("o_cb", l, slot)])
                    sbuf_, sres = dstate(l, slot)
                    P.dma("sp", "c_o3_%d_%d" % (l, slot), lambda e, sbuf_=sbuf_, slot=slot: e.dma_start(out=o_dl[l, slot], in_=sbuf_[:, :, :]),
                          reads=[(sres, h) for h in range(H)], writes=[("o_dl", l, slot)])


        tiles = []
        for i in range(cfg.NBIG):
            tiles.append(("big", i * T, T, [0], T, False))
        tiles.append(("small", cfg.NBIG * T, 48, [0, 1, 2], 16, True))
        for kind, t0, nt, segs, L, last in tiles:
            if kind == "big":
                P.dma("sp", "c_x", lambda e, t0=t0, nt=nt: e.dma_start(out=X[:, :, 0:nt], in_=xp[:, :, t0:t0 + nt].rearrange("k p t -> p k t")),
                      writes=[("X", k) for k in range(KD)])
            else:
                P.dma("sp", "c_x", lambda e, t0=t0: e.dma_start(out=X[:, :, 0:16], in_=xp[:, :, t0:t0 + 16].rearrange("k p t -> p k t")),
                      writes=[("X", k) for k in range(KD)])
                P.dma("sp", "c_x2", lambda e: e.dma_start(out=X[:, :, 16:48], in_=xs.rearrange("k p t -> p k t")),
                      writes=[("X", k) for k in range(KD)])
            for l in range(DEPTH):
                ffn(l, 1, nt, have_stats=(l > 0))
                mixer(l, nt, segs, L, last)
                ffn(l, 2, nt, have_stats=True)
            ydst = [(HL[:, k, :], [("HL", k, si) for si in range(3)]) for k in range(KL)] + \
                   [(OO[:, h, :], [("OO", h, si, ci) for si in range(3) for ci in range(4)]) for h in range(H)]
            rmsnorm(nt, ("final_norm",), None, None, dst_list=ydst, have_stats=True)
            rdA = [r for k in range(KL) for r in ydst[k][1]]
            rdB = [r for k in range(KL, KD) for r in ydst[k][1]]
            if kind == "big":
                P.dma("sp", "c_y", lambda e, t0=t0, nt=nt: e.dma_start(out=yp[0:KL, :, t0:t0 + nt].rearrange("k p t -> p k t"), in_=HL[:, :, 0:nt]),
                      reads=rdA, writes=[("yp", t0, 0)])
                P.dma("sp", "c_yb", lambda e, t0=t0, nt=nt: e.dma_start(out=yp[KL:KD, :, t0:t0 + nt].rearrange("k p t -> p k t"), in_=OO[:, :, 0:nt]),
                      reads=rdB, writes=[("yp", t0, 1)])
            else:
                P.dma("sp", "c_y", lambda e, t0=t0: e.dma_start(out=yp[0:KL, :, t0:t0 + 16].rearrange("k p t -> p k t"), in_=HL[:, :, 0:16]),
                      reads=rdA, writes=[("yp", t0, 0)])
                P.dma("sp", "c_yb", lambda e, t0=t0: e.dma_start(out=yp[KL:KD, :, t0:t0 + 16].rearrange("k p t -> p k t"), in_=OO[:, :, 0:16]),
                      reads=rdB, writes=[("yp", t0, 1)])
                P.dma("sp", "c_y2", lambda e: e.dma_start(out=ys[0:KL].rearrange("k p t -> p k t"), in_=HL[:, :, 16:48]),
                      reads=rdA, writes=[("ys", 0)])
                P.dma("sp", "c_y2b", lambda e: e.dma_start(out=ys[KL:KD].rearrange("k p t -> p k t"), in_=OO[:, :, 16:48]),
                      reads=rdB, writes=[("ys", 1)])
        assert wstate["gu"] == cfg.NU * len(tiles)
        P.emit(st)
    return nc


def _blk(w, ks, cols, UW):
    out = np.zeros((128, UW), np.float32)
    for i, k in enumerate(ks):
        blk = w[k * 128:(k + 1) * 128, cols]
        out[:, i * 128:i * 128 + blk.shape[1]] = blk
    return out


def prepare(cfg, inp):
    f32 = np.float32
    D, KD, KF, KL, H, NQ, DEPTH, LW = cfg.D, cfg.KD, cfg.KF, cfg.KL, cfg.H, cfg.NQ, cfg.DEPTH, cfg.LW
    g = {k: np.asarray(v, f32) for k, v in inp.items()}
    ws = np.zeros((cfg.NU, 128, cfg.UW), f32)
    o2 = 2 * LW
    o3 = o2 + NQ * 128
    o4 = o3 + H * 128
    for u, d in enumerate(cfg.units):
        kind = d[0]
        if kind in ("gate", "up", "down"):
            _, l, which, idx, ks = d
            wsel = {("gate", 1): g["ffn1_w_gate"], ("up", 1): g["ffn1_w_up"], ("down", 1): g["ffn1_w_down"],
                    ("gate", 2): g["ffn2_w_gate"], ("up", 2): g["ffn2_w_up"], ("down", 2): g["ffn2_w_down"]}[(kind, which)]
            ws[u] = _blk(wsel[l], ks, slice(idx * 128, (idx + 1) * 128), cfg.UW)
        else:
            _, l, idx, ks = d
            if kind == "xa":
                ws[u] = _blk(g["w_in"][l], ks, slice(idx * 128, (idx + 1) * 128), cfg.UW)
            elif kind == "ga":
                ws[u] = _blk(g["w_in"][l], ks, slice(LW + idx * 128, LW + (idx + 1) * 128), cfg.UW)
            elif kind == "qkv":
                ws[u] = _blk(g["w_in"][l], ks, slice(o2 + idx * 128, o2 + (idx + 1) * 128), cfg.UW)
            elif kind == "z":
                ws[u] = _blk(g["w_in"][l], ks, slice(o3 + idx * 128, o3 + (idx + 1) * 128), cfg.UW)
            elif kind == "tail":
                ws[u] = _blk(g["w_in"][l], ks, slice(o4, o4 + 2 * H), cfg.UW)
            elif kind == "wout":
                ws[u] = _blk(g["w_out"][l], ks, slice(idx * 128, (idx + 1) * 128), cfg.UW)
    prm = np.zeros((128, cfg.NP), f32)

    def put(name, arr):
        off, w = cfg.pcol[name]
        prm[:arr.shape[0], off:off + w] = arr

    def pk(v):
        return v.reshape(-1, 128).T
    for l in range(DEPTH):
        put(("ffn1_norm", l), pk(g["ffn1_norm"][l]))
        put(("mix_norm", l), pk(g["mix_norm"][l]))
        put(("ffn2_norm", l), pk(g["ffn2_norm"][l]))
        put(("conv_a_w", l), g["conv_a_w"][l].reshape(4, KL, 128).transpose(2, 1, 0).reshape(128, KL * 4))
        put(("conv_a_b", l), pk(g["conv_a_b"][l]))
        put(("rg_b", l), pk(g["rg_b"][l]))
        put(("ig_b", l), pk(g["ig_b"][l]))
        put(("lam", l), pk(g["lru_lambda"][l]))
        put(("norm_a", l), pk(g["norm_a"][l]))
        put(("conv_b_w", l), g["conv_b_w"][l].reshape(4, NQ, 128).transpose(2, 1, 0).reshape(128, NQ * 4))
        put(("norm_b", l), g["norm_b"][l].reshape(128, 1))
        put(("a_log", l), g["a_log"][l].reshape(H, 1))
        put(("dt_bias", l), g["dt_bias"][l].reshape(H, 1))
    put(("final_norm",), pk(g["final_norm"]))
    gw = np.zeros((DEPTH, 2, 128, KL, 128), f32)
    for l in range(DEPTH):
        for gi, name in enumerate(("rg_w", "ig_w")):
            w = g[name][l]
            for c in range(KL):
                gw[l, gi, 0:64, c, 0:64] = w[2 * c]
                gw[l, gi, 64:128, c, 64:128] = w[2 * c + 1]
    cst = np.zeros((128, 6, 128), f32)
    ii = np.arange(128)
    cst[:, 0, :] = np.eye(128)
    cst[:, 1, :] = (ii[:, None] > ii[None, :])
    cst[:, 2, :] = (ii[None, :] >= ii[:, None])
    cst[:, 3, :] = (ii[:, None] <= ii[None, :])
    cst[:, 4, :] = 1.0
    shared = {"wstream": ws, "prm": prm, "gw": gw, "cst": cst}
    in_maps = []
    for c in range(cfg.NCORES):
        m = dict(shared)
        if c < cfg.BATCH:
            stream = np.concatenate([g["meta_tokens"], g["x_prompt"][c]], axis=0)
            m["xp"] = np.ascontiguousarray(stream.T.reshape(KD, 128, cfg.NTOK))
        else:
            m["xp"] = np.zeros((KD, 128, cfg.NTOK), f32)
        xsm = g["x_sample"][2 * c:2 * c + 2].reshape(32, D)
        m["xs"] = np.ascontiguousarray(xsm.T.reshape(KD, 128, 32))
        sl = slice(2 * c, 2 * c + 2)
        m["sca"] = np.ascontiguousarray(g["state_conv_a"][:, sl].reshape(DEPTH, 2, 3, KL, 128).transpose(0, 4, 1, 3, 2))
        m["slru"] = np.ascontiguousarray(g["state_lru"][:, sl].reshape(DEPTH, 2, KL, 128).transpose(0, 3, 1, 2))
        m["scb"] = np.ascontiguousarray(g["state_conv_b"][:, sl].reshape(DEPTH, 2, 3, NQ, 128).transpose(0, 4, 1, 3, 2))
        m["sdl"] = np.ascontiguousarray(g["state_delta"][:, sl].transpose(0, 1, 3, 2, 4))
        in_maps.append(m)
    return in_maps


def assemble(cfg, res):
    f32 = np.float32
    D, KD, KL, H, NQ, DEPTH, LW = cfg.D, cfg.KD, cfg.KL, cfg.H, cfg.NQ, cfg.DEPTH, cfg.LW
    B, DB = cfg.BATCH, cfg.DEC_BATCH
    y_prompt = np.zeros((B, cfg.SEQ, D), f32)
    y_sample = np.zeros((DB, 16, D), f32)
    p_ca = np.zeros((DEPTH, B, 3, LW), f32)
    p_lru = np.zeros((DEPTH, B, LW), f32)
    p_cb = np.zeros((DEPTH, B, 3, NQ * 128), f32)
    p_dl = np.zeros((DEPTH, B, H, 128, 128), f32)
    s_ca = np.zeros((DEPTH, DB, 3, LW), f32)
    s_lru = np.zeros((DEPTH, DB, LW), f32)
    s_cb = np.zeros((DEPTH, DB, 3, NQ * 128), f32)
    s_dl = np.zeros((DEPTH, DB, H, 128, 128), f32)
    for c, r in enumerate(res):
        ypc = np.asarray(r["yp"]).reshape(D, cfg.NTOK).T
        if c < B:
            y_prompt[c] = ypc[cfg.NMETA:]
        ysc = np.asarray(r["ys"]).reshape(D, 32).T.reshape(2, 16, D)
        y_sample[2 * c:2 * c + 2] = ysc
        ca = np.asarray(r["o_ca"]).transpose(0, 1, 4, 3, 2).reshape(DEPTH, 3, 3, LW)
        lr = np.asarray(r["o_lru"]).transpose(0, 1, 3, 2).reshape(DEPTH, 3, LW)
        cb = np.asarray(r["o_cb"]).transpose(0, 1, 4, 3, 2).reshape(DEPTH, 3, 3, NQ * 128)
        dl = np.asarray(r["o_dl"]).transpose(0, 1, 3, 2, 4)
        if c < B:
            p_ca[:, c], p_lru[:, c], p_cb[:, c], p_dl[:, c] = ca[:, 0], lr[:, 0], cb[:, 0], dl[:, 0]
        for s in range(2):
            b = 2 * c + s
            s_ca[:, b], s_lru[:, b], s_cb[:, b], s_dl[:, b] = ca[:, 1 + s], lr[:, 1 + s], cb[:, 1 + s], dl[:, 1 + s]
    return (y_prompt, y_sample, p_ca, p_lru, p_cb, p_dl, s_ca, s_lru, s_cb, s_dl)


def run(cfg, inputs, trace=False):
    nc = build_program(cfg)
    in_maps = prepare(cfg, inputs)
    res = run_bass_kernel_spmd(nc, in_maps, core_ids=list(range(cfg.NCORES)), trace=trace)
    return assemble(cfg, res.results), res


def kernel(**inputs):
    cfg = Cfg()
    out, _ = run(cfg, inputs)
    return out
```
